# Optimizing a Trainium2 kernel written in Bass

```python
import jax, jax.numpy as jnp
from jax import lax
import numpy as np

D_MODEL = 1024
BATCH = 16
SEQ = 4096
DEPTH = 2

MIX_W = 1024
CONV_K = 3
CONV_GROUPS = 8
N_HEADS = 16
HEAD_DIM = 64
N_KV = 4
HPG = N_HEADS // N_KV
KV_W = N_KV * HEAD_DIM
L_CMP = 32
STRIDE_CMP = 16
CMP_HIDDEN = 128
L_SEL = 64
N_SEL_BLOCKS = 16
WINDOW = 512
Q_BLOCK = 16
CHUNK = 128
GM_GROUPS = 8
GM_GW = MIX_W // GM_GROUPS
N_BRANCH = 3
EPS = 1e-6
NEG = -1e30

A_OFF = 0
A_COLS = 4 * MIX_W
B_OFF = A_OFF + A_COLS
B_COLS = 2 * MIX_W + 6 * KV_W + 3 * N_HEADS
C_OFF = B_OFF + B_COLS
C_COLS = 3 * MIX_W
G_OFF = C_OFF + C_COLS
G_COLS = N_BRANCH * D_MODEL
IN_COLS = G_OFF + G_COLS

kernel_name = "hybrid_conv_nsa_gmlp_gated_block"


def rmsnorm(x, g):
    x32 = x.astype(jnp.float32)
    y = x32 * lax.rsqrt(jnp.mean(x32 * x32, axis=-1, keepdims=True) + EPS)
    return y.astype(x.dtype) * g


def layernorm(x, g, b):
    x32 = x.astype(jnp.float32)
    mu = jnp.mean(x32, axis=-1, keepdims=True)
    xc = x32 - mu
    y = xc * lax.rsqrt(jnp.mean(xc * xc, axis=-1, keepdims=True) + EPS)
    return y.astype(x.dtype) * g + b


def masked_softmax(s, mask):
    p = jax.nn.softmax(jnp.where(mask, s, NEG), axis=-1)
    return jnp.where(mask, p, 0.0)


def alibi_slopes():
    i = jnp.arange(1, N_HEADS + 1, dtype=jnp.float32)
    return (2.0 ** (-8.0 * i / N_HEADS)).reshape(N_KV, HPG)


def short_conv_mixer(h, w_in_a, conv_w, conv_b):
    S = h.shape[1]
    b, cg, xin, z = jnp.split(h @ w_in_a, 4, axis=-1)
    y = cg * xin
    yp = jnp.pad(y, ((0, 0), (CONV_K - 1, 0), (0, 0)))
    conv = conv_b + sum(conv_w[k] * yp[:, k:k + S] for k in range(CONV_K))
    return b * conv * jax.nn.silu(z)


def nsa_mixer(h, w_in_b, pos_ck, w_ck1, w_ck2, pos_cv, w_cv1, w_cv2):
    Bsz, S, _ = h.shape
    sizes = [MIX_W] + [KV_W] * 6 + [MIX_W]
    q, kc, vc, ks, vs, kw, vw, z, gl = jnp.split(h @ w_in_b, np.cumsum(sizes).tolist(), axis=-1)

    def heads_kv(t):
        return t.reshape(Bsz, S, N_KV, HEAD_DIM).transpose(0, 2, 1, 3)

    n_cmp = (S - L_CMP) // STRIDE_CMP + 1
    cmp_start = jnp.arange(n_cmp) * STRIDE_CMP
    cmp_end = cmp_start + (L_CMP - 1)
    cmp_idx = cmp_start[:, None] + jnp.arange(L_CMP)[None, :]

    def compress(t, pos, w1, w2):
        blocks = heads_kv(t)[:, :, cmp_idx] + pos
        flat = blocks.reshape(Bsz, N_KV, n_cmp, L_CMP * HEAD_DIM)
        return jax.nn.silu(flat @ w1) @ w2

    k_cmp = compress(kc, pos_ck, w_ck1, w_ck2)
    v_cmp = compress(vc, pos_cv, w_cv1, w_cv2)

    n_sel = S // L_SEL
    k_top = min(N_SEL_BLOCKS, n_sel)
    k_blk = heads_kv(ks).reshape(Bsz, N_KV, n_sel, L_SEL, HEAD_DIM)
    v_blk = heads_kv(vs).reshape(Bsz, N_KV, n_sel, L_SEL, HEAD_DIM)
    sel_ids = jnp.arange(n_sel)
    overlap = ((cmp_start[:, None] <= (sel_ids[None, :] + 1) * L_SEL - 1)
               & (cmp_end[:, None] >= sel_ids[None, :] * L_SEL)).astype(jnp.float32)

    pad = ((0, 0), (0, 0), (WINDOW - 1, 0), (0, 0))
    k_win = jnp.pad(heads_kv(kw), pad)
    v_win = jnp.pad(heads_kv(vw), pad)
    span = Q_BLOCK + WINDOW - 1

    nq = S // Q_BLOCK
    qh = q.reshape(Bsz, nq, Q_BLOCK, N_KV, HPG, HEAD_DIM).transpose(1, 0, 3, 4, 2, 5)
    gh = gl.reshape(Bsz, nq, Q_BLOCK, N_KV, HPG, 3).transpose(1, 0, 3, 4, 2, 5)
    slopes = alibi_slopes()
    scale = HEAD_DIM ** -0.5
    bi = jnp.arange(Bsz)[:, None, None, None]
    gi = jnp.arange(N_KV)[None, :, None, None]

    def block_fn(args):
        ci, qb, gb = args
        t0 = ci * Q_BLOCK
        tq = t0 + jnp.arange(Q_BLOCK)
        d_c = tq[:, None] - cmp_end[None, :]
        s = jnp.einsum('bgnqd,bgkd->bgnqk', qb, k_cmp).astype(jnp.float32) * scale \
            - slopes[:, :, None, None] * d_c.astype(jnp.float32)
        p_cmp = masked_softmax(s, d_c >= 0)
        o_cmp = jnp.einsum('bgnqk,bgkd->bgnqd', p_cmp.astype(v_cmp.dtype), v_cmp)
        imp = jnp.einsum('bgnqk,ks->bgqs', p_cmp, overlap)
        cur = tq // L_SEL
        valid = sel_ids[None, :] * L_SEL <= tq[:, None]
        forced = (sel_ids[None, :] == 0) | (sel_ids[None, :] == cur[:, None]) | (sel_ids[None, :] == cur[:, None] - 1)
        score = jnp.where(forced, jnp.inf, jnp.where(valid, imp, -jnp.inf))
        _, sel = lax.top_k(score, k_top)
        kg = k_blk[bi, gi, sel]
        vg = v_blk[bi, gi, sel]
        spos = sel[..., None] * L_SEL + jnp.arange(L_SEL)
        d_s = (tq[None, None, :, None, None] - spos)[:, :, None]
        s = jnp.einsum('bgnqd,bgqkld->bgnqkl', qb, kg).astype(jnp.float32) * scale \
            - slopes[:, :, None, None, None] * d_s.astype(jnp.float32)
        m = jnp.broadcast_to(d_s >= 0, s.shape)
        flat_shape = (Bsz, N_KV, HPG, Q_BLOCK, k_top * L_SEL)
        p = masked_softmax(s.reshape(flat_shape), m.reshape(flat_shape))
        o_slc = jnp.einsum('bgnqm,bgqmd->bgnqd', p.astype(vg.dtype),
                           vg.reshape(Bsz, N_KV, Q_BLOCK, k_top * L_SEL, HEAD_DIM))
        kwb = lax.dynamic_slice_in_dim(k_win, t0, span, axis=2)
        vwb = lax.dynamic_slice_in_dim(v_win, t0, span, axis=2)
        kpos = t0 - (WINDOW - 1) + jnp.arange(span)
        d_w = tq[:, None] - kpos[None, :]
        m_w = (d_w >= 0) & (d_w < WINDOW) & (kpos[None, :] >= 0)
        s = jnp.einsum('bgnqd,bgkd->bgnqk', qb, kwb).astype(jnp.float32) * scale \
            - slopes[:, :, None, None] * d_w.astype(jnp.float32)
        p = masked_softmax(s, m_w)
        o_win = jnp.einsum('bgnqk,bgkd->bgnqd', p.astype(vwb.dtype), vwb)
        g = jax.nn.sigmoid(gb)
        return g[..., 0:1] * o_cmp + g[..., 1:2] * o_slc + g[..., 2:3] * o_win

    o = lax.map(block_fn, (jnp.arange(nq), qh, gh))
    o = o.transpose(1, 0, 4, 2, 3, 5).reshape(Bsz, S, MIX_W)
    return o * jax.nn.silu(z)


def gmlp_mixer(h, w_in_c, ln_g, ln_b, w_s, b_s):
    Bsz, S, _ = h.shape
    u, v, z = jnp.split(h @ w_in_c, 3, axis=-1)
    u = jax.nn.gelu(u)
    v = layernorm(jax.nn.gelu(v), ln_g, ln_b)
    vr = v.reshape(Bsz, S // CHUNK, CHUNK, GM_GROUPS, GM_GW)
    tril = jnp.tril(jnp.ones((CHUNK, CHUNK), dtype=bool))
    wm = jnp.where(tril, w_s, 0.0)
    sp = jnp.einsum('gij,bnjgc->bnigc', wm, vr) + b_s.T[:, :, None]
    return u * sp.reshape(Bsz, S, MIX_W) * jax.nn.silu(z)


def hybrid_layer(x, c, g_pre, g_post, w_ada, b_ada, w_in, conv_w, conv_b, pos_ck, w_ck1, w_ck2,
                 pos_cv, w_cv1, w_cv2, ln_g, ln_b, w_s, b_s, w_br, w_out):
    shift, scl, gate = jnp.split(jax.nn.silu(c) @ w_ada + b_ada, 3, axis=-1)
    h = rmsnorm(x, g_pre) * (1.0 + scl[:, None]) + shift[:, None]
    ys = (
        short_conv_mixer(h, w_in[:, A_OFF:A_OFF + A_COLS], conv_w, conv_b),
        nsa_mixer(h, w_in[:, B_OFF:B_OFF + B_COLS], pos_ck, w_ck1, w_ck2, pos_cv, w_cv1, w_cv2),
        gmlp_mixer(h, w_in[:, C_OFF:C_OFF + C_COLS], ln_g, ln_b, w_s, b_s),
    )
    merged = None
    for i in range(N_BRANCH):
        g_i = jax.nn.sigmoid(h @ w_in[:, G_OFF + i * D_MODEL:G_OFF + (i + 1) * D_MODEL])
        term = g_i * (ys[i] @ w_br[i])
        merged = term if merged is None else merged + term
    out = rmsnorm(merged @ w_out, g_post)
    return x + gate[:, None] * out


def setup_inputs(seed: int = 0) -> dict:
    key = jax.random.key(seed)
    k = jax.random.split(key, 24)
    nrm = jax.random.normal
    f = jnp.float32
    return {
        "x": nrm(k[0], (BATCH, SEQ, D_MODEL), f),
        "c": nrm(k[1], (BATCH, D_MODEL), f),
        "g_pre": 1.0 + 0.02 * nrm(k[2], (DEPTH, D_MODEL), f),
        "g_post": 1.0 + 0.02 * nrm(k[3], (DEPTH, D_MODEL), f),
        "w_ada": 0.5 * D_MODEL ** -0.5 * nrm(k[4], (DEPTH, D_MODEL, 3 * D_MODEL), f),
        "b_ada": 0.02 * nrm(k[5], (DEPTH, 3 * D_MODEL), f),
        "w_in": D_MODEL ** -0.5 * nrm(k[6], (DEPTH, D_MODEL, IN_COLS), f),
        "conv_w": CONV_K ** -0.5 * nrm(k[7], (DEPTH, CONV_K, MIX_W), f),
        "conv_b": 0.02 * nrm(k[8], (DEPTH, MIX_W), f),
        "pos_ck": 0.02 * nrm(k[9], (DEPTH, L_CMP, HEAD_DIM), f),
        "w_ck1": (L_CMP * HEAD_DIM) ** -0.5 * nrm(k[10], (DEPTH, L_CMP * HEAD_DIM, CMP_HIDDEN), f),
        "w_ck2": CMP_HIDDEN ** -0.5 * nrm(k[11], (DEPTH, CMP_HIDDEN, HEAD_DIM), f),
        "pos_cv": 0.02 * nrm(k[12], (DEPTH, L_CMP, HEAD_DIM), f),
        "w_cv1": (L_CMP * HEAD_DIM) ** -0.5 * nrm(k[13], (DEPTH, L_CMP * HEAD_DIM, CMP_HIDDEN), f),
        "w_cv2": CMP_HIDDEN ** -0.5 * nrm(k[14], (DEPTH, CMP_HIDDEN, HEAD_DIM), f),
        "ln_g": 1.0 + 0.02 * nrm(k[15], (DEPTH, MIX_W), f),
        "ln_b": 0.02 * nrm(k[16], (DEPTH, MIX_W), f),
        "w_s": CHUNK ** -0.5 * nrm(k[17], (DEPTH, GM_GROUPS, CHUNK, CHUNK), f),
        "b_s": 1.0 + 0.02 * nrm(k[18], (DEPTH, GM_GROUPS, CHUNK), f),
        "w_br": MIX_W ** -0.5 * nrm(k[19], (DEPTH, N_BRANCH, MIX_W, D_MODEL), f),
        "w_out": D_MODEL ** -0.5 * nrm(k[20], (DEPTH, D_MODEL, D_MODEL), f),
    }


def reference(x, c, g_pre, g_post, w_ada, b_ada, w_in, conv_w, conv_b, pos_ck, w_ck1, w_ck2,
              pos_cv, w_cv1, w_cv2, ln_g, ln_b, w_s, b_s, w_br, w_out):
    for l in range(DEPTH):
        x = hybrid_layer(x, c, g_pre[l], g_post[l], w_ada[l], b_ada[l], w_in[l], conv_w[l], conv_b[l],
                         pos_ck[l], w_ck1[l], w_ck2[l], pos_cv[l], w_cv1[l], w_cv2[l],
                         ln_g[l], ln_b[l], w_s[l], b_s[l], w_br[l], w_out[l])
    return x
```

```python
from contextlib import ExitStack
import numpy as np
import concourse.bass as bass
import concourse.mybir as mybir
from concourse.bass_utils import run_bass_kernel_spmd

F32 = mybir.dt.float32
BF16 = mybir.dt.bfloat16
AF = mybir.ActivationFunctionType
ALU = mybir.AluOpType

D = 1024
KC = 8
NCH = 109
EPS = 1e-6
MNEG = -32768.0
CH_B, CH_CG, CH_XIN, CH_ZA = 0, 8, 16, 24
CH_Q, CH_KC, CH_VC, CH_KS, CH_VS, CH_KW, CH_VW, CH_ZB, CH_GL = 32, 40, 42, 44, 46, 48, 50, 52, 60
CH_U, CH_V, CH_ZC, CH_GA, CH_GB, CH_GC = 61, 69, 77, 85, 93, 101


class Buf:
    __slots__ = ("w", "r")

    def __init__(self):
        self.w = {}
        self.r = {}


class Sched:
    def __init__(self, nc, ndma=8):
        self.nc = nc
        self.engs = {"pe": nc.tensor, "act": nc.scalar, "dve": nc.vector, "pool": nc.gpsimd, "sp": nc.sync}
        self.sems = {}
        self.cnt = {}
        for e in ("pe", "act", "dve", "pool"):
            self.sems[e] = nc.alloc_semaphore("c_" + e)
            self.cnt[e] = 0
        self.dq = {}
        for q in ("sp", "pool", "act"):
            keys = []
            for i in range(ndma):
                k = "d_%s%d" % (q, i)
                self.sems[k] = nc.alloc_semaphore(k)
                keys.append(k)
            self.dq[q] = [keys, 0]
        self.known = {e: {} for e in self.engs}
        self.last = {}
        self.ninst = 0

    def _wait(self, eng, deps):
        kn = self.known[eng]
        for k, v in deps.items():
            if kn.get(k, 0) >= v:
                continue
            self.engs[eng].wait_ge(self.sems[k], v)
            kn[k] = v
            self.ninst += 1

    def _deps(self, reads, writes, skip):
        deps = {}
        for b in reads:
            for k, v in b.w.items():
                if k != skip and deps.get(k, 0) < v:
                    deps[k] = v
        for b in writes:
            for dd in (b.w, b.r):
                for k, v in dd.items():
                    if k != skip and deps.get(k, 0) < v:
                        deps[k] = v
        return deps

    def _commit(self, k, v, reads, writes):
        self.last[k] = v
        for b in reads:
            if b.r.get(k, 0) < v:
                b.r[k] = v
        for b in writes:
            if b.w.get(k, 0) < v:
                b.w[k] = v

    def op(self, eng, fn, reads=(), writes=()):
        deps = self._deps(reads, writes, "pe" if eng == "pe" else None)
        self._wait(eng, deps)
        inst = fn(self.engs[eng])
        self.cnt[eng] += 1
        inst.then_inc(self.sems[eng], 1)
        self.ninst += 1
        self._commit(eng, self.cnt[eng], reads, writes)

    def dma(self, q, out, in_, reads=(), writes=()):
        keys, i = self.dq[q]
        self.dq[q][1] = i + 1
        k = keys[i % len(keys)]
        v = 16 * (i // len(keys) + 1)
        deps = self._deps(reads, writes, None)
        if v > 16 and deps.get(k, 0) < v - 16:
            deps[k] = v - 16
        self._wait(q, deps)
        self.engs[q].dma_start(out=out, in_=in_).then_inc(self.sems[k], 16)
        self.ninst += 1
        self._commit(k, v, reads, writes)

    def barrier(self):
        for e in self.engs:
            self._wait(e, dict(self.last))

    def finish(self):
        self._wait("sp", dict(self.last))


class TT:
    def __init__(self, t, nparts=1):
        self.t = t
        self.b = [Buf() for _ in range(nparts)]

    def __getitem__(self, idx):
        return self.t[idx]


def host_consts(S):
    NT = S // 128
    NSEL = S // 64
    NCMP = (S - 32) // 16 + 1
    NCT = (NCMP + 127) // 128
    NQB = S // 512
    c = {}
    c["ident"] = np.eye(128, dtype=np.float32)
    p = np.arange(128)[:, None]
    tt = np.arange(512)[None, :]
    cm = np.zeros((13, 128, 512), np.float32)
    for o in range(-4, 4):
        kk = 128 * o + p
        valid = (tt < kk + 512) if o < 0 else (tt >= kk)
        cm[o + 4] = np.where(valid, 0.0, MNEG)
    for m in range(5):
        cm[8 + m] = np.where(tt + 512 * m >= 16 * p + 31, 0.0, MNEG)
    c["cm"] = np.ascontiguousarray(cm.transpose(1, 0, 2))
    i = np.arange(NCT * 128)[:, None]
    j = np.arange(NSEL)[None, :]
    ov = ((16 * i <= 64 * j + 63) & (16 * i + 31 >= 64 * j) & (i < NCMP)).astype(np.float32)
    c["ov"] = np.ascontiguousarray(ov.reshape(NCT, 128, NSEL).transpose(1, 0, 2))
    kp = np.arange(NT * 128)[None, :]
    c["esel"] = (np.arange(NSEL)[:, None] == kp // 64).astype(np.float32)
    t = np.arange(S)[:, None]
    cur = t // 64
    forced = (j == 0) | (j == cur) | (j == cur - 1)
    fadd = np.where(forced, 1e4, np.where(j * 64 <= t, 0.0, -1e4)).astype(np.float32)
    c["fadd"] = np.ascontiguousarray(fadd.reshape(NT, 128, NSEL).transpose(1, 0, 2))
    slopes = 2.0 ** (-8.0 * np.arange(1, 17) / 16.0)
    import ml_dtypes
    v = (-8.0 * slopes[:, None] * np.arange(512)[None, :]).astype(np.float64)
    rows = []
    rem = v.copy()
    for _ in range(3):
        hi = rem.astype(np.float32).astype(ml_dtypes.bfloat16).astype(np.float64)
        rows.append(hi)
        rem = rem - hi
    c["aq"] = np.stack(rows, 0).astype(np.float32)
    o = np.arange(NT) - (NT - 4)
    c["alb"] = (slopes[None, :, None] * (128.0 * o[None, None, :] + np.arange(128)[:, None, None])).astype(np.float32)
    m = np.arange(8)
    c["albc"] = (slopes[None, :, None] * (16.0 * np.arange(128)[:, None, None] + 31.0 - 512.0 * m[None, None, :])).astype(np.float32)
    c["tri"] = (np.arange(128)[None, :] >= np.arange(128)[:, None]).astype(np.float32)
    return c


def build(nc, S, NB, DEPTH, dbg=None):
    NT = S // 128
    NSEL = S // 64
    NCMP = (S - 32) // 16 + 1
    NCT = (NCMP + 127) // 128
    NQB = S // 512
    dbg = dbg or set()

    def din(name, shape, dt=F32):
        return nc.dram_tensor(name, list(shape), dt, kind="ExternalInput").ap()

    def dscr(name, shape, dt):
        return nc.dram_tensor(name, list(shape), dt, kind="ExternalOutput" if name in dbg else "Internal").ap()

    x_in = din("x", [NB, S, D])
    cT = din("cT", [128, KC, NB])
    gpre = din("gpre", [DEPTH, 128, KC])
    gpost = din("gpost", [DEPTH, 1, D])
    badag = din("badag", [DEPTH, 1, D])
    badaf = din("badaf", [DEPTH, 128, 16])
    wadaT = din("wadaT", [DEPTH, 24, 128, KC * 128])
    winT = din("winT", [DEPTH, NCH, 128, KC * 128])
    wbrT = din("wbrT", [DEPTH, 24, 128, KC * 128])
    woutT = din("woutT", [DEPTH, 128, KC * D])
    cw = din("cw", [DEPTH, 128, 8, 3])
    cb = din("cb", [DEPTH, 128, 8])
    posT = din("posT", [DEPTH, 2, 64, 32])
    w1 = din("w1", [DEPTH, 2, 64, 32 * 128])
    w2 = din("w2", [DEPTH, 2, 128, 64])
    lng = din("lng", [DEPTH, 128, 8])
    lnb = din("lnb", [DEPTH, 1, D])
    wsT = din("wsT", [DEPTH, 128, 8, 128])
    bs = din("bs", [DEPTH, 1, D])
    k_ident = din("k_ident", [128, 128])
    k_cm = din("k_cm", [128, 13, 512])
    k_ov = din("k_ov", [128, NCT, NSEL])
    k_esel = din("k_esel", [NSEL, NT * 128])
    k_fadd = din("k_fadd", [128, NT, NSEL])
    k_aq = din("k_aq", [3, 16, 512])
    k_alb = din("k_alb", [128, 16, NT])
    k_albc = din("k_albc", [128, 16, 8])
    k_tri = din("k_tri", [128, 128])
    y_out = nc.dram_tensor("y", [NB, S, D], F32, kind="ExternalOutput").ap()

    winB = dscr("winB", [DEPTH, NCH, 128, KC * 128], BF16)
    wbrB = dscr("wbrB", [DEPTH, 24, 128, KC * 128], BF16)
    woutB = dscr("woutB", [DEPTH, 128, KC * D], BF16)
    s_qT = dscr("s_qT", [64, 16, S], BF16)
    s_kT = {nm: dscr("s_" + nm, [64, 4, S], BF16) for nm in ("kc", "vc", "ks", "kw")}
    s_vX = {nm: dscr("s_" + nm, [NT, 128, 260], BF16) for nm in ("vs", "vw")}
    s_szB = dscr("s_szB", [8, 128, S], BF16)
    s_gB = dscr("s_gB", [8, 128, S], BF16)
    s_glg = dscr("s_glg", [NT, 128, 48], F32)
    s_mp = dscr("s_mp", [8, 128, S], BF16)
    s_o = dscr("s_o", [NT, 128, D], F32)
    xmid = dscr("xmid", [NB, S, D], F32)
    SB = {k: Buf() for k in ("winB", "wbrB", "woutB", "qT", "kc", "vc", "ks", "kw", "vs", "vw", "szB", "gB", "glg", "mp", "o", "xmid", "y")}

    sc = Sched(nc)
    op, dma = sc.op, sc.dma

    with ExitStack() as top:
        uniq = [0]

        def sb(st, name, shape, dt, nparts=1):
            uniq[0] += 1
            return TT(st.enter_context(nc.sbuf_tensor("%s_%d" % (name, uniq[0]), list(shape), dt)), nparts)

        PS = [TT(top.enter_context(nc.psum_tensor("ps%d" % i, [128, 512], F32))) for i in range(8)]
        ident_f = sb(top, "ident_f", [128, 128], F32)
        ident_b = sb(top, "ident_b", [128, 128], BF16)
        dma("sp", ident_f[:], k_ident[:, :], writes=ident_f.b)
        dma("pool", ident_b[:], k_ident[:, :], writes=ident_b.b)
        A_scale = sb(top, "A_scale", [128, KC, NB], F32)
        A_shift = sb(top, "A_shift", [128, KC, NB], F32)
        GP = sb(top, "GP", [128, NB, D], F32)

        with ExitStack() as st:
            G = 4
            stg = [sb(st, "cv_f%d" % i, [128, G, 1024], F32) for i in range(2)]
            stb = [sb(st, "cv_b%d" % i, [128, G, 1024], BF16) for i in range(2)]
            jobs = []
            for l in range(DEPTH):
                for c0 in range(0, NCH, G):
                    n = min(G, NCH - c0)
                    jobs.append((winT[l, c0:c0 + n], winB[l, c0:c0 + n], n, SB["winB"]))
                for c0 in range(0, 24, G):
                    jobs.append((wbrT[l, c0:c0 + G], wbrB[l, c0:c0 + G], G, SB["wbrB"]))
                for c0 in range(0, 8, G):
                    jobs.append((woutT[l, :, c0 * 1024:(c0 + G) * 1024], woutB[l, :, c0 * 1024:(c0 + G) * 1024], -G, SB["woutB"]))
            for ji, (src, dst, n, bf) in enumerate(jobs):
                f, b_ = stg[ji % 2], stb[ji % 2]
                if n > 0:
                    dma("sp", f[:, 0:n, :], src.rearrange("c p f -> p c f"), writes=f.b)
                else:
                    n = -n
                    dma("sp", f[:, 0:n, :], src.rearrange("p (c f) -> p c f", c=n), writes=f.b)
                    dst = dst.rearrange("p (c f) -> c p f", c=n)
                eng = ("dve", "act", "pool")[ji % 3]
                if eng == "act":
                    op("act", lambda e, f=f, b_=b_, n=n: e.copy(out=b_[:, 0:n, :], in_=f[:, 0:n, :]), reads=f.b, writes=b_.b)
                else:
                    op(eng, lambda e, f=f, b_=b_, n=n: e.tensor_copy(out=b_[:, 0:n, :], in_=f[:, 0:n, :]), reads=f.b, writes=b_.b)
                dma("pool", dst.rearrange("c p f -> p c f"), b_[:, 0:n, :], reads=b_.b, writes=[bf])
            sc.barrier()

        rr = {"ps": 0, "wt": 0, "cs": 0}

        def nb():
            rr["ps"] = (rr["ps"] + 1) % 8
            return PS[rr["ps"]]

        def evac_copy(i, out, in_, reads, writes):
            if i % 2 == 0:
                op("act", lambda e: e.copy(out=out, in_=in_), reads=reads, writes=writes)
            else:
                op("dve", lambda e: e.tensor_copy(out=out, in_=in_), reads=reads, writes=writes)

        def phaseA(l, b, xcur, xcur_b):
            with ExitStack() as st:
                hT = sb(st, "hT", [128, KC, 512], BF16)
                xt = [sb(st, "xt%d" % i, [128, D], F32) for i in range(4)]
                junk = sb(st, "junk", [128, D], BF16)
                stat = sb(st, "stat", [128, 8], F32)
                wt = [sb(st, "wt%d" % i, [128, 4, KC * 128], BF16) for i in range(3)]
                wbr0 = sb(st, "wbr0", [128, 8, KC * 128], BF16)
                wbr2 = sb(st, "wbr2", [128, 8, KC * 128], BF16)
                yAT = sb(st, "yAT", [128, 8, 512], BF16)
                yCT = sb(st, "yCT", [128, 8, 512], BF16)
                sgA = sb(st, "sgA", [128, 8, 512], BF16)
                sgC = sb(st, "sgC", [128, 8, 512], BF16)
                carry = sb(st, "carry", [128, 8, 2], F32)
                cw_t = sb(st, "cw_t", [128, 8, 3], F32)
                cb_t = sb(st, "cb_t", [128, 8], F32)
                lng_t = sb(st, "lng_t", [128, 8], F32)
                wmT = sb(st, "wmT", [128, 8, 128], BF16)
                Bc = sb(st, "Bc", [128, 8, 128], F32)
                vn = [sb(st, "vn%d" % i, [128, D], BF16) for i in range(4)]
                gv = sb(st, "gv", [128, D], F32)
                bst = sb(st, "bst", [128, 16], F32)
                tmp = [sb(st, "tmpA%d" % i, [128, 516], F32) for i in range(6)]
                qst = [sb(st, "qst%d" % i, [128, 4, 512], BF16) for i in range(3)]
                vst = sb(st, "vst", [128, 4, 2, 260], BF16)
                gst = sb(st, "gst", [128, 4, 48], F32)
                cst = [sb(st, "cst%d" % i, [128, 512], BF16) for i in range(8)]
                ugs = [sb(st, "ugs%d" % i, [128, 512], BF16) for i in range(4)]
                dma("sp", wbr0[:], wbrB[l, 0:8].rearrange("c p f -> p c f"), reads=[SB["wbrB"]], writes=wbr0.b)
                dma("sp", wbr2[:], wbrB[l, 16:24].rearrange("c p f -> p c f"), reads=[SB["wbrB"]], writes=wbr2.b)
                dma("sp", cw_t[:], cw[l], writes=cw_t.b)
                dma("sp", cb_t[:], cb[l], writes=cb_t.b)
                dma("sp", lng_t[:], lng[l], writes=lng_t.b)
                op("dve", lambda e: e.memset(vst[:], 1.0), writes=vst.b)
                op("dve", lambda e: e.memset(carry[:], 0.0), writes=carry.b)
                wsf, trif, lnbf, bsf = xt[0], xt[1], xt[2], xt[3]
                lnbb = junk
                dma("sp", wsf[:, 0:1024].rearrange("p (g i) -> p g i", g=8), wsT[l], writes=wsf.b)
                dma("sp", trif[:, 0:128], k_tri[:, :], writes=trif.b)
                dma("sp", lnbf[:], lnb[l].broadcast_to([128, D]), writes=lnbf.b)
                dma("sp", bsf[:], bs[l].broadcast_to([128, D]), writes=bsf.b)
                op("dve", lambda e: e.tensor_copy(out=lnbb[:], in_=lnbf[:]), reads=lnbf.b, writes=lnbb.b)
                for g in range(8):
                    op("dve", lambda e, g=g: e.tensor_tensor(out=wmT[:, g, :], in0=wsf[:, g * 128:(g + 1) * 128], in1=trif[:, 0:128], op=ALU.mult), reads=wsf.b + trif.b, writes=wmT.b)
                for g in range(8):
                    ps = nb()
                    op("pe", lambda e, g=g, ps=ps: e.matmul(ps[:, 0:128], lhsT=lnbb[:, g * 128:(g + 1) * 128], rhs=wmT[:, g, :], start=True, stop=True), reads=lnbb.b + wmT.b, writes=ps.b)
                    op("dve", lambda e, g=g, ps=ps: e.tensor_tensor(out=Bc[:, g, :], in0=ps[:, 0:128], in1=bsf[:, g * 128:(g + 1) * 128], op=ALU.add), reads=ps.b + bsf.b, writes=Bc.b)

                def loadw(ch0, n):
                    w = wt[rr["wt"] % 3]
                    rr["wt"] += 1
                    dma("sp", w[:, 0:n, :], winB[l, ch0:ch0 + n].rearrange("c p f -> p c f"), reads=[SB["winB"]], writes=w.b)
                    return w

                def mm_fm(ps, w, ci, m0, M):
                    hT_ = cur["hT"]
                    for kc in range(KC):
                        op("pe", lambda e, kc=kc: e.matmul(ps[0:M, :], lhsT=w[:, ci, kc * 128 + m0:kc * 128 + m0 + M], rhs=hT_[:, kc, :], start=(kc == 0), stop=(kc == KC - 1)),
                           reads=w.b + hT_.b, writes=ps.b)

                def mm_tm(ps, w, c0, n, i, ncol=None):
                    hT_ = cur["hT"]
                    for kc in range(KC):
                        if ncol is None:
                            o_ap = ps[:, 0:n * 128].rearrange("p (c m) -> p c m", c=n)
                            r_ap = w[:, c0:c0 + n, kc * 128:(kc + 1) * 128]
                        else:
                            o_ap = ps[:, 0:ncol]
                            r_ap = w[:, c0, kc * 128:kc * 128 + ncol]
                        op("pe", lambda e, kc=kc, o_ap=o_ap, r_ap=r_ap: e.matmul(o_ap, lhsT=hT_[:, kc, i * 128:(i + 1) * 128], rhs=r_ap, start=(kc == 0), stop=(kc == KC - 1)),
                           reads=w.b + hT_.b, writes=ps.b)

                hT2 = [hT, sb(st, "hTb", [128, KC, 512], BF16)]
                xt2 = [xt, [sb(st, "xtb%d" % i, [128, D], F32) for i in range(4)]]

                def a1_pre(blk):
                    t0 = blk * 512
                    xs = xt2[blk % 2]
                    for i in range(4):
                        x_ = xs[i]
                        dma("sp", x_[:], xcur[b, t0 + i * 128:t0 + (i + 1) * 128, :], reads=xcur_b, writes=x_.b)
                        op("act", lambda e, x_=x_, i=i: e.activation(out=junk[:], in_=x_[:], func=AF.Square, accum_out=stat[:, i:i + 1]), reads=x_.b, writes=junk.b + stat.b)
                        op("dve", lambda e, i=i: e.tensor_scalar(out=stat[:, 4 + i:5 + i], in0=stat[:, i:i + 1], scalar1=1.0 / D, scalar2=EPS, op0=ALU.mult, op1=ALU.add), reads=stat.b, writes=stat.b)
                        op("act", lambda e, i=i: e.activation(out=stat[:, 4 + i:5 + i], in_=stat[:, 4 + i:5 + i], func=AF.Sqrt), reads=stat.b, writes=stat.b)
                        op("dve", lambda e, i=i: e.reciprocal(out=stat[:, 4 + i:5 + i], in_=stat[:, 4 + i:5 + i]), reads=stat.b, writes=stat.b)
                        op("act", lambda e, x_=x_, i=i: e.activation(out=x_[:], in_=x_[:], func=AF.Copy, scale=stat[:, 4 + i:5 + i]), reads=x_.b + stat.b, writes=x_.b)

                def a1_trans(blk):
                    xs = xt2[blk % 2]
                    hT_ = hT2[blk % 2]
                    for kc in range(KC):
                        ps = nb()
                        for i in range(4):
                            op("pe", lambda e, i=i, kc=kc, ps=ps: e.transpose(out=ps[:, i * 128:(i + 1) * 128], in_=xs[i][:, kc * 128:(kc + 1) * 128], identity=ident_f[:]), reads=xs[i].b + ident_f.b, writes=ps.b)
                        if kc % 2 == 0:
                            op("act", lambda e, kc=kc, ps=ps: e.activation(out=hT_[:, kc, :], in_=ps[:, :], func=AF.Identity, scale=A_scale[:, kc, b:b + 1], bias=A_shift[:, kc, b:b + 1]),
                               reads=ps.b + A_scale.b + A_shift.b, writes=hT_.b)
                        else:
                            op("dve", lambda e, kc=kc, ps=ps: e.tensor_scalar(out=hT_[:, kc, :], in0=ps[:, :], scalar1=A_scale[:, kc, b:b + 1], scalar2=A_shift[:, kc, b:b + 1], op0=ALU.mult, op1=ALU.add),
                               reads=ps.b + A_scale.b + A_shift.b, writes=hT_.b)

                cur = {"hT": hT}
                a1_pre(0)
                a1_trans(0)
                for blk in range(NQB):
                    t0 = blk * 512
                    tsl = slice(t0, t0 + 512)
                    cur["hT"] = hT2[blk % 2]
                    if blk + 1 < NQB:
                        a1_pre(blk + 1)
                    wv0 = loadw(CH_V, 4)
                    wv1 = loadw(CH_V + 4, 4)
                    for i in range(4):
                        p0, p1 = nb(), nb()
                        mm_tm(p0, wv0, 0, 4, i)
                        mm_tm(p1, wv1, 0, 4, i)
                        op("act", lambda e, p0=p0: e.activation(out=gv[:, 0:512], in_=p0[:, :], func=AF.Gelu_apprx_tanh), reads=p0.b, writes=gv.b)
                        op("act", lambda e, p1=p1: e.activation(out=gv[:, 512:1024], in_=p1[:, :], func=AF.Gelu_apprx_tanh), reads=p1.b, writes=gv.b)
                        op("dve", lambda e: e.bn_stats(out=bst[:, 0:6], in_=gv[:, 0:512]), reads=gv.b, writes=bst.b)
                        op("dve", lambda e: e.bn_stats(out=bst[:, 6:12], in_=gv[:, 512:1024]), reads=gv.b, writes=bst.b)
                        op("dve", lambda e: e.bn_aggr(out=bst[:, 12:14], in_=bst[:, 0:12]), reads=bst.b, writes=bst.b)
                        op("dve", lambda e: e.tensor_scalar(out=bst[:, 14:15], in0=bst[:, 13:14], scalar1=EPS, scalar2=None, op0=ALU.add), reads=bst.b, writes=bst.b)
                        op("act", lambda e: e.activation(out=bst[:, 14:15], in_=bst[:, 14:15], func=AF.Sqrt), reads=bst.b, writes=bst.b)
                        op("dve", lambda e: e.reciprocal(out=bst[:, 14:15], in_=bst[:, 14:15]), reads=bst.b, writes=bst.b)
                        op("dve", lambda e, i=i: e.tensor_scalar(out=vn[i][:], in0=gv[:], scalar1=bst[:, 12:13], scalar2=bst[:, 14:15], op0=ALU.subtract, op1=ALU.mult), reads=gv.b + bst.b, writes=vn[i].b)
                    xin_sb, ybuf, acc, sz, t1, t2 = tmp
                    for j in range(8):
                        w = loadw(4 * j, 4)
                        pb, pcg, pxin, pz = nb(), nb(), nb(), nb()
                        mm_fm(pcg, w, 1, 0, 128)
                        mm_fm(pxin, w, 2, 0, 128)
                        mm_fm(pz, w, 3, 0, 128)
                        mm_fm(pb, w, 0, 0, 128)
                        op("act", lambda e, pxin=pxin: e.copy(out=xin_sb[:, 0:512], in_=pxin[:, :]), reads=pxin.b, writes=xin_sb.b)
                        op("dve", lambda e, j=j: e.tensor_copy(out=ybuf[:, 0:2], in_=carry[:, j, :]), reads=carry.b, writes=ybuf.b)
                        op("dve", lambda e, pcg=pcg: e.tensor_tensor(out=ybuf[:, 2:514], in0=pcg[:, :], in1=xin_sb[:, 0:512], op=ALU.mult), reads=pcg.b + xin_sb.b, writes=ybuf.b)
                        op("dve", lambda e, j=j: e.tensor_copy(out=carry[:, j, :], in_=ybuf[:, 512:514]), reads=ybuf.b, writes=carry.b)
                        op("dve", lambda e, j=j: e.tensor_scalar(out=acc[:, 0:512], in0=ybuf[:, 2:514], scalar1=cw_t[:, j, 2:3], scalar2=cb_t[:, j:j + 1], op0=ALU.mult, op1=ALU.add),
                           reads=ybuf.b + cw_t.b + cb_t.b, writes=acc.b)
                        op("dve", lambda e, j=j: e.scalar_tensor_tensor(out=acc[:, 0:512], in0=ybuf[:, 1:513], scalar=cw_t[:, j, 1:2], in1=acc[:, 0:512], op0=ALU.mult, op1=ALU.add),
                           reads=ybuf.b + cw_t.b + acc.b, writes=acc.b)
                        op("dve", lambda e, j=j: e.scalar_tensor_tensor(out=acc[:, 0:512], in0=ybuf[:, 0:512], scalar=cw_t[:, j, 0:1], in1=acc[:, 0:512], op0=ALU.mult, op1=ALU.add),
                           reads=ybuf.b + cw_t.b + acc.b, writes=acc.b)
                        op("act", lambda e, pz=pz: e.activation(out=sz[:, 0:512], in_=pz[:, :], func=AF.Silu), reads=pz.b, writes=sz.b)
                        op("dve", lambda e: e.tensor_tensor(out=t1[:, 0:512], in0=acc[:, 0:512], in1=sz[:, 0:512], op=ALU.mult), reads=acc.b + sz.b, writes=t1.b)
                        op("dve", lambda e, j=j, pb=pb: e.tensor_tensor(out=yAT[:, j, :], in0=pb[:, :], in1=t1[:, 0:512], op=ALU.mult), reads=pb.b + t1.b, writes=yAT.b)
                    for (ch0, kind) in ((CH_GA, "A"), (CH_GC, "C"), (CH_GB, "B"), (CH_ZB, "Z")):
                        for half in range(2):
                            w = loadw(ch0 + 4 * half, 4)
                            for ci in range(4):
                                j = 4 * half + ci
                                ps = nb()
                                mm_fm(ps, w, ci, 0, 128)
                                if kind == "A":
                                    op("act", lambda e, j=j, ps=ps: e.activation(out=sgA[:, j, :], in_=ps[:, :], func=AF.Sigmoid), reads=ps.b, writes=sgA.b)
                                elif kind == "C":
                                    op("act", lambda e, j=j, ps=ps: e.activation(out=sgC[:, j, :], in_=ps[:, :], func=AF.Sigmoid), reads=ps.b, writes=sgC.b)
                                else:
                                    cs = cst[rr["cs"] % 8]
                                    rr["cs"] += 1
                                    fn = AF.Sigmoid if kind == "B" else AF.Silu
                                    op("act", lambda e, cs=cs, ps=ps, fn=fn: e.activation(out=cs[:], in_=ps[:, :], func=fn), reads=ps.b, writes=cs.b)
                                    dst = s_gB if kind == "B" else s_szB
                                    dma("act", dst[j, :, tsl], cs[:], reads=cs.b, writes=[SB["gB" if kind == "B" else "szB"]])
                    szc, tc_, sp2 = sz, t1, t2
                    for half in range(2):
                        wu = loadw(CH_U + 4 * half, 4)
                        wz = loadw(CH_ZC + 4 * half, 4)
                        for ci in range(4):
                            pu = nb()
                            mm_fm(pu, wu, ci, 0, 128)
                            op("act", lambda e, pu=pu, ci=ci: e.activation(out=ugs[ci][:], in_=pu[:, :], func=AF.Gelu_apprx_tanh), reads=pu.b, writes=ugs[ci].b)
                        for ci in range(4):
                            g = 4 * half + ci
                            psp, pzc = nb(), nb()
                            for i in range(4):
                                op("pe", lambda e, i=i, g=g, psp=psp: e.matmul(psp[:, i * 128:(i + 1) * 128], lhsT=vn[i][:, g * 128:(g + 1) * 128], rhs=wmT[:, g, :], start=True, stop=True),
                                   reads=vn[i].b + wmT.b, writes=psp.b)
                            mm_fm(pzc, wz, ci, 0, 128)
                            for i in range(4):
                                op("dve", lambda e, i=i, g=g, psp=psp: e.scalar_tensor_tensor(out=sp2[:, i * 128:(i + 1) * 128], in0=psp[:, i * 128:(i + 1) * 128], scalar=lng_t[:, g:g + 1], in1=Bc[:, g, :], op0=ALU.mult, op1=ALU.add),
                                   reads=psp.b + lng_t.b + Bc.b, writes=sp2.b)
                            op("act", lambda e, pzc=pzc: e.activation(out=szc[:, 0:512], in_=pzc[:, :], func=AF.Silu), reads=pzc.b, writes=szc.b)
                            op("dve", lambda e, ci=ci: e.tensor_tensor(out=tc_[:, 0:512], in0=ugs[ci][:], in1=szc[:, 0:512], op=ALU.mult), reads=ugs[ci].b + szc.b, writes=tc_.b)
                            op("dve", lambda e, g=g: e.tensor_tensor(out=yCT[:, g, :], in0=tc_[:, 0:512], in1=sp2[:, 0:512], op=ALU.mult), reads=tc_.b + sp2.b, writes=yCT.b)
                    for half in range(2):
                        w = loadw(CH_Q + 4 * half, 4)
                        q_ = qst[rr["cs"] % 3]
                        rr["cs"] += 1
                        qe = rr["cs"] % 2
                        for ci in range(4):
                            ps = nb()
                            mm_fm(ps, w, ci, 0, 128)
                            evac_copy(qe, q_[:, ci, :], ps[:, :], ps.b, q_.b)
                        qn = ("act", "pool")[qe]
                        dma(qn, s_qT[:, 8 * half:8 * half + 8:2, tsl], q_[0:64, :, :], reads=q_.b, writes=[SB["qT"]])
                        dma(qn, s_qT[:, 8 * half + 1:8 * half + 8:2, tsl], q_[64:128, :, :], reads=q_.b, writes=[SB["qT"]])
                    kcnt = 0
                    for (ch0, knm, vnm) in ((CH_KC, "kc", "vc"), (CH_KS, "ks", None), (CH_KW, "kw", None)):
                        w = loadw(ch0, 4)
                        for (c_off, nm) in ((0, knm), (2, vnm)):
                            if nm is None:
                                continue
                            q_ = qst[rr["cs"] % 3]
                            rr["cs"] += 1
                            kcnt += 1
                            qe = rr["cs"] % 2
                            for i in range(2):
                                ps = nb()
                                mm_fm(ps, w, c_off + i, 0, 128)
                                evac_copy(qe, q_[:, i, :], ps[:, :], ps.b, q_.b)
                            qn = ("act", "pool")[qe]
                            dma(qn, s_kT[nm][:, 0:4:2, tsl], q_[0:64, 0:2, :], reads=q_.b, writes=[SB[nm]])
                            dma(qn, s_kT[nm][:, 1:4:2, tsl], q_[64:128, 0:2, :], reads=q_.b, writes=[SB[nm]])
                        if vnm is None:
                            typ = 0 if knm == "ks" else 1
                            for i in range(4):
                                ps = nb()
                                mm_tm(ps, w, 2, 2, i)
                                evac_copy(typ, vst[:, i, typ, :].rearrange("p (g f) -> p g f", g=4)[:, :, 0:64], ps[:, 0:256].rearrange("p (g f) -> p g f", g=4), ps.b, vst.b)
                            nm = "vs" if typ == 0 else "vw"
                            dma(("act", "pool")[typ], s_vX[nm][blk * 4:(blk + 1) * 4].rearrange("i p f -> p i f"), vst[:, :, typ, :], reads=vst.b, writes=[SB[nm]])
                    w = loadw(CH_GL, 1)
                    for i in range(4):
                        ps = nb()
                        mm_tm(ps, w, 0, 1, i, ncol=48)
                        op("act", lambda e, i=i, ps=ps: e.activation(out=gst[:, i, :], in_=ps[:, 0:48], func=AF.Sigmoid), reads=ps.b, writes=gst.b)
                    dma("act", s_glg[blk * 4:(blk + 1) * 4].rearrange("i p f -> p i f"), gst[:], reads=gst.b, writes=[SB["glg"]])
                    m1, m2 = acc, ybuf
                    for dch in range(8):
                        pa, pc = nb(), nb()
                        for kc in range(KC):
                            op("pe", lambda e, kc=kc, dch=dch, pa=pa: e.matmul(pa[:, :], lhsT=wbr0[:, dch, kc * 128:(kc + 1) * 128], rhs=yAT[:, kc, :], start=(kc == 0), stop=(kc == KC - 1)), reads=wbr0.b + yAT.b, writes=pa.b)
                        for kc in range(KC):
                            op("pe", lambda e, kc=kc, dch=dch, pc=pc: e.matmul(pc[:, :], lhsT=wbr2[:, dch, kc * 128:(kc + 1) * 128], rhs=yCT[:, kc, :], start=(kc == 0), stop=(kc == KC - 1)), reads=wbr2.b + yCT.b, writes=pc.b)
                        op("dve", lambda e, dch=dch, pa=pa: e.tensor_tensor(out=m1[:, 0:512], in0=pa[:, :], in1=sgA[:, dch, :], op=ALU.mult), reads=pa.b + sgA.b, writes=m1.b)
                        op("dve", lambda e, dch=dch, pc=pc: e.tensor_tensor(out=m2[:, 0:512], in0=pc[:, :], in1=sgC[:, dch, :], op=ALU.mult), reads=pc.b + sgC.b, writes=m2.b)
                        cs = cst[rr["cs"] % 8]
                        rr["cs"] += 1
                        op("dve", lambda e, cs=cs: e.tensor_tensor(out=cs[:], in0=m1[:, 0:512], in1=m2[:, 0:512], op=ALU.add), reads=m1.b + m2.b, writes=cs.b)
                        dma("pool", s_mp[dch, :, tsl], cs[:], reads=cs.b, writes=[SB["mp"]])
                    if blk + 1 < NQB:
                        a1_trans(blk + 1)
                sc.barrier()


        SLOPES = [2.0 ** (-8.0 * (i + 1) / 16.0) for i in range(16)]
        SKIP_T = 45.0
        XW = 65 + NSEL
        NS2 = NSEL + 2

        def phaseB(l, b):
            with ExitStack() as st:
                kcmpT = sb(st, "kcmpT", [67, 4, NCT * 128], BF16)
                VCX = sb(st, "VCX", [128, NCT, 4, XW + 2], BF16)
                pbias = sb(st, "pbias", [128, 2], F32)
                op("dve", lambda e: e.memset(kcmpT[0:64], 0.0), writes=kcmpT.b)
                op("dve", lambda e: e.memset(kcmpT[64:67], 1.0), writes=kcmpT.b)
                op("dve", lambda e: e.memset(VCX[:], 0.0), writes=VCX.b)
                op("dve", lambda e: e.memset(VCX[:, :, :, 64:65], 1.0), writes=VCX.b)
                for g in range(4):
                    dma("pool", VCX[:, :, g, 65:XW], k_ov[:, :, :], writes=VCX.b)
                with ExitStack() as st0:
                    xcT = [sb(st0, "xcT%d" % kv, [64, 4, S], BF16) for kv in range(2)]
                    w1t = [sb(st0, "w1t%d" % kv, [64, 32 * 128], BF16) for kv in range(2)]
                    w2t = [sb(st0, "w2t%d" % kv, [128, 64], BF16) for kv in range(2)]
                    post = sb(st0, "post", [64, 2, 32], BF16)
                    hid = sb(st0, "hid", [128, 128], BF16)
                    for kv, nm in enumerate(("kc", "vc")):
                        dma("sp", xcT[kv][:], s_kT[nm][:, :, :], reads=[SB[nm]], writes=xcT[kv].b)
                        dma("pool", w1t[kv][:], w1[l, kv], writes=w1t[kv].b)
                        dma("pool", w2t[kv][:], w2[l, kv], writes=w2t[kv].b)
                        dma("pool", post[:, kv, :], posT[l, kv], writes=post.b)
                    op("dve", lambda e: e.memset(hid[:], 0.0), writes=hid.b)
                    for kv in range(2):
                        ps = nb()
                        for lq in range(32):
                            op("pe", lambda e, lq=lq, kv=kv, ps=ps: e.matmul(ps[:, 0:1], lhsT=w1t[kv][0:64, lq * 128:(lq + 1) * 128], rhs=post[0:64, kv, lq:lq + 1], start=(lq == 0), stop=(lq == 31)),
                               reads=w1t[kv].b + post.b, writes=ps.b)
                        op("dve", lambda e, kv=kv, ps=ps: e.tensor_copy(out=pbias[:, kv:kv + 1], in_=ps[:, 0:1]), reads=ps.b, writes=pbias.b)
                    for kv in range(2):
                        for g in range(4):
                            for ct in range(NCT):
                                n_i = min(128, NCMP - ct * 128)
                                ps = nb()
                                for lq in range(32):
                                    s0 = 16 * ct * 128 + lq
                                    op("pe", lambda e, lq=lq, kv=kv, g=g, ps=ps, s0=s0, n_i=n_i: e.matmul(ps[:, 0:n_i], lhsT=w1t[kv][0:64, lq * 128:(lq + 1) * 128], rhs=xcT[kv][0:64, g, s0:s0 + 16 * (n_i - 1) + 1:16], start=(lq == 0), stop=(lq == 31)),
                                       reads=w1t[kv].b + xcT[kv].b, writes=ps.b)
                                op("act", lambda e, kv=kv, ps=ps, n_i=n_i: e.activation(out=hid[:, 0:n_i], in_=ps[:, 0:n_i], func=AF.Silu, bias=pbias[:, kv:kv + 1]), reads=ps.b + pbias.b, writes=hid.b)
                                ps2 = nb()
                                if kv == 0:
                                    op("pe", lambda e, ps2=ps2, n_i=n_i: e.matmul(ps2[0:64, 0:n_i], lhsT=w2t[0][:, 0:64], rhs=hid[:, 0:n_i], start=True, stop=True), reads=w2t[0].b + hid.b, writes=ps2.b)
                                    op("dve", lambda e, ps2=ps2, n_i=n_i, g=g, ct=ct: e.tensor_copy(out=kcmpT[0:64, g, ct * 128:ct * 128 + n_i], in_=ps2[0:64, 0:n_i]), reads=ps2.b, writes=kcmpT.b)
                                else:
                                    op("pe", lambda e, ps2=ps2, n_i=n_i: e.matmul(ps2[:, 0:64], lhsT=hid[:, 0:128], rhs=w2t[1][:, 0:64], start=True, stop=True), reads=w2t[1].b + hid.b, writes=ps2.b)
                                    op("dve", lambda e, ps2=ps2, n_i=n_i, g=g, ct=ct: e.tensor_copy(out=VCX[0:n_i, ct, g, 0:64], in_=ps2[0:n_i, 0:64]), reads=ps2.b, writes=VCX.b)
                    sc.barrier()
                KTs = sb(st, "KTs", [67, 4, S], BF16)
                KTw = [sb(st, "KTw%d" % i, [67, 4, 1024], BF16) for i in range(2)]
                Vs = sb(st, "Vs", [128, NT, 260], BF16)
                Vw = [sb(st, "Vw%d" % i, [128, 8, 260], BF16) for i in range(2)]
                QT = [sb(st, "QT%d" % i, [67, 16, 512], BF16) for i in range(2)]
                glt = [sb(st, "glt%d" % i, [128, 4, 48], F32) for i in range(2)]
                OACC2 = [sb(st, "OACC%d" % i, [128, 4, D], F32) for i in range(2)]
                cur_o = {"o": OACC2[0], "qb": 0}
                PT = [sb(st, "PT%d" % i, [128, 512], BF16) for i in range(6)]
                cmt = sb(st, "cmt", [128, 13, 512], BF16)
                eselt = sb(st, "eselt", [128, NT * 128], BF16)
                faddq = [sb(st, "faddq%d" % i, [128, 4, NSEL], F32) for i in range(2)]
                albt = sb(st, "albt", [128, 16, NT], F32)
                albct = sb(st, "albct", [128, 16, 8], F32)
                impacc = sb(st, "impacc", [128, 4, 4, NSEL], F32)
                sm = sb(st, "sm", [128, 8], F32)
                scr_ = sb(st, "scr_", [128, NSEL], F32)
                scr2 = sb(st, "scr2", [128, NSEL], F32)
                m8 = sb(st, "m8", [128, 16], F32)
                mbs = sb(st, "mbs", [128, 16, NSEL], BF16)
                MBTs = [sb(st, "MBT%d" % i, [128, 512], BF16) for i in range(4)]
                dma("pool", cmt[:], k_cm[:, :, :], writes=cmt.b)
                op("dve", lambda e: e.memset(eselt[:], 0.0), writes=eselt.b)
                for i in range(4):
                    op("dve", lambda e, i=i: e.memset(MBTs[i][:], 0.0), writes=MBTs[i].b)
                dma("pool", eselt[0:NSEL], k_esel[:, :], writes=eselt.b)
                dma("sp", albt[:], k_alb[:, :, :], writes=albt.b)
                dma("sp", albct[:], k_albc[:, :, :], writes=albct.b)
                dma("sp", KTs[0:64], s_kT["ks"][:, :, :], reads=[SB["ks"]], writes=KTs.b)
                op("dve", lambda e: e.memset(KTs[64:67], 1.0), writes=KTs.b)
                dma("sp", Vs[:], s_vX["vs"].rearrange("i p f -> p i f"), reads=[SB["vs"]], writes=Vs.b)
                for i in range(2):
                    op("dve", lambda e, i=i: e.memset(KTw[i][64:67], 1.0), writes=KTw[i].b)
                    dma("pool", QT[i][64:67], k_aq[:, :, :], writes=QT[i].b)
                ctr = {"s": 0, "p": 0, "o": 0, "c": 0}
                SPB = [PS[0], PS[1], PS[4]]
                pipe = []
                LAG = 4

                def push(fn):
                    pipe.append(fn)
                    while len(pipe) > LAG:
                        pipe.pop(0)()

                def flush():
                    while pipe:
                        pipe.pop(0)()

                OSB = [sb(st, "osb%d" % i, [66, 512], F32) for i in range(5)]
                for i in range(5):
                    op("dve", lambda e, i=i: e.memset(OSB[i][:], 0.0), writes=OSB[i].b)
                ISB = [sb(st, "isb%d" % i, [NS2, 512], F32) for i in range(2)]
                for i in range(2):
                    op("dve", lambda e, i=i: e.memset(ISB[i][:], 0.0), writes=ISB[i].b)
                PTR = PS[6]
                PIMS = [PS[7], PS[5]]

                def attend(tiles, Q, h, g, gate_col, G_, first_branch, is_cmp):
                    ob = PS[2 + ctr["o"] % 2]
                    osb = OSB[ctr["o"] % 5]
                    ctr["o"] += 1
                    if is_cmp:
                        PIM = PIMS[ctr["c"] % 2]
                        isb = ISB[ctr["c"] % 2]
                        ctr["c"] += 1
                    nt = len(tiles)

                    def epilogue2():
                        for sub in range(4):
                            op("pe", lambda e, sub=sub: e.transpose(out=PTR[:, sub * 66:(sub + 1) * 66], in_=osb[0:66, sub * 128:(sub + 1) * 128], identity=ident_f[0:66, 0:66]), reads=osb.b + ident_f.b, writes=PTR.b)
                        if is_cmp:
                            for sub in range(4):
                                op("pe", lambda e, sub=sub: e.transpose(out=PIM[:, sub * NS2:(sub + 1) * NS2], in_=isb[0:NS2, sub * 128:(sub + 1) * 128], identity=ident_f[0:NS2, 0:NS2]), reads=isb.b + ident_f.b, writes=PIM.b)
                        op("dve", lambda e: e.tensor_scalar(out=sm[:, 0:4], in0=PTR[:, 0:264].rearrange("p (s f) -> p s f", s=4)[:, :, 64], scalar1=1e-30, scalar2=None, op0=ALU.add), reads=PTR.b, writes=sm.b)
                        op("dve", lambda e: e.reciprocal(out=sm[:, 0:4], in_=sm[:, 0:4]), reads=sm.b, writes=sm.b)
                        op("dve", lambda e: e.tensor_tensor(out=sm[:, 4:8], in0=sm[:, 0:4], in1=G_[:, :, gate_col], op=ALU.mult), reads=sm.b + G_.b, writes=sm.b)
                        for sub in range(4):
                            c0_ = sub * 66
                            OACC = cur_o["o"]
                            osl = OACC[:, sub, h * 64:(h + 1) * 64]
                            if first_branch:
                                op("dve", lambda e, osl=osl, c0_=c0_, sub=sub: e.tensor_scalar(out=osl, in0=PTR[:, c0_:c0_ + 64], scalar1=sm[:, 4 + sub:5 + sub], scalar2=None, op0=ALU.mult), reads=PTR.b + sm.b, writes=OACC.b)
                            else:
                                op("dve", lambda e, osl=osl, c0_=c0_, sub=sub: e.scalar_tensor_tensor(out=osl, in0=PTR[:, c0_:c0_ + 64], scalar=sm[:, 4 + sub:5 + sub], in1=osl, op0=ALU.mult, op1=ALU.add), reads=PTR.b + sm.b + OACC.b, writes=OACC.b)
                            if is_cmp:
                                i0 = sub * NS2
                                if h % 4 == 0:
                                    op("dve", lambda e, sub=sub, i0=i0: e.tensor_scalar(out=impacc[:, g, sub, :], in0=PIM[:, i0:i0 + NSEL], scalar1=sm[:, sub:sub + 1], scalar2=None, op0=ALU.mult), reads=PIM.b + sm.b, writes=impacc.b)
                                else:
                                    op("dve", lambda e, sub=sub, i0=i0: e.scalar_tensor_tensor(out=impacc[:, g, sub, :], in0=PIM[:, i0:i0 + NSEL], scalar=sm[:, sub:sub + 1], in1=impacc[:, g, sub, :], op0=ALU.mult, op1=ALU.add), reads=PIM.b + sm.b + impacc.b, writes=impacc.b)

                    tiles = sorted(tiles, key=lambda t_: 0 if t_[8] == (0, 512) else 1)
                    assert tiles[0][8] == (0, 512)
                    for ti, (k_ap, k_rd, extras, bias_ap, bias_rd, v_ap, ov_ap, v_rd, (lo, hi)) in enumerate(tiles):
                        sp = SPB[ctr["s"] % 3]
                        ctr["s"] += 1
                        op("pe", lambda e, sp=sp, k_ap=k_ap, lo=lo, hi=hi: e.matmul(sp[:, lo:hi], lhsT=k_ap, rhs=Q[0:67, h, lo:hi], start=True, stop=(len(extras) == 0)), reads=k_rd + Q.b, writes=sp.b)
                        for xi, (xl, xr, xrd) in enumerate(extras):
                            op("pe", lambda e, sp=sp, xl=xl, xr=xr, xi=xi, lo=lo, hi=hi: e.matmul(sp[:, lo:hi], lhsT=xl, rhs=xr[:, lo:hi], start=False, stop=(xi == len(extras) - 1)), reads=xrd, writes=sp.b)
                        pt = PT[ctr["p"] % 6]
                        ctr["p"] += 1
                        op("act", lambda e, sp=sp, pt=pt, bias_ap=bias_ap, lo=lo, hi=hi: e.activation(out=pt[:, lo:hi], in_=sp[:, lo:hi], func=AF.Exp, scale=0.125, bias=bias_ap), reads=sp.b + bias_rd, writes=pt.b)

                        def stage2(ti=ti, pt=pt, v_ap=v_ap, ov_ap=ov_ap, v_rd=v_rd, lo=lo, hi=hi):
                            op("pe", lambda e: e.matmul(ob[0:65, lo:hi], lhsT=v_ap, rhs=pt[:, lo:hi], start=(ti == 0), stop=(ti == nt - 1)), reads=pt.b + v_rd, writes=ob.b)
                            if is_cmp:
                                op("pe", lambda e: e.matmul(PIM[0:NS2, :], lhsT=ov_ap, rhs=pt[:], start=(ti == 0), stop=(ti == nt - 1)), reads=pt.b + v_rd, writes=PIM.b)
                            if ti == nt - 1:
                                if cur_o["qb"] <= 2 and ctr["o"] % 2 == 0:
                                    op("act", lambda e: e.copy(out=osb[0:65, :], in_=ob[0:65, :]), reads=ob.b, writes=osb.b)
                                else:
                                    op("dve", lambda e: e.tensor_copy(out=osb[0:65, :], in_=ob[0:65, :]), reads=ob.b, writes=osb.b)
                                if is_cmp:
                                    op("dve", lambda e: e.tensor_copy(out=isb[0:NSEL, :], in_=PIM[0:NSEL, :]), reads=PIM.b, writes=isb.b)
                                if is_cmp:
                                    epilogue2()
                                else:
                                    push(epilogue2)

                        push(stage2)

                for qb in range(NQB):
                    t0 = qb * 512
                    Q = QT[qb % 2]
                    cur_o["o"] = OACC2[qb % 2]
                    cur_o["qb"] = qb
                    OACC = OACC2[qb % 2]
                    Kw = KTw[qb % 2]
                    Vw_ = Vw[qb % 2]
                    G_ = glt[qb % 2]
                    dma("sp", Q[0:64], s_qT[:, :, t0:t0 + 512], reads=[SB["qT"]], writes=Q.b)
                    k0 = max(0, t0 - 512)
                    dma("sp", Kw[0:64, :, k0 - (t0 - 512):1024], s_kT["kw"][:, :, k0:t0 + 512], reads=[SB["kw"]], writes=Kw.b)
                    c0 = max(0, 4 * qb - 4)
                    dma("sp", Vw_[:, c0 - (4 * qb - 4):8, :], s_vX["vw"][c0:4 * qb + 4].rearrange("i p f -> p i f"), reads=[SB["vw"]], writes=Vw_.b)
                    dma("sp", G_[:], s_glg[4 * qb:4 * qb + 4].rearrange("i p f -> p i f"), reads=[SB["glg"]], writes=G_.b)
                    faddt = faddq[qb % 2]
                    dma("sp", faddt[:], k_fadd[:, 4 * qb:4 * qb + 4, :], writes=faddt.b)
                    for g in range(4):
                        for n in range(4):
                            h = 4 * g + n
                            tiles = []
                            for c in range(NCT):
                                m = qb - 4 * c
                                if m < 0:
                                    continue
                                extras = []
                                if m <= 4:
                                    extras.append((ident_b[:], cmt[:, 8 + m, :], ident_b.b + cmt.b))
                                tiles.append((kcmpT[0:67, g, c * 128:(c + 1) * 128], kcmpT.b, extras, albct[:, h, m:m + 1], albct.b, VCX[:, c, g, 0:65], VCX[:, c, g, 65:XW + 2], VCX.b, (0, 512)))
                            attend(tiles, Q, h, g, 3 * h, G_, True, True)
                    flush()
                    for g in range(4):
                        for n in range(4):
                            h = 4 * g + n
                            sub = n
                            mb_ = mbs[:, g * 4 + sub, :]
                            op("dve", lambda e, sub=sub, g=g: e.tensor_tensor(out=scr_[:], in0=impacc[:, g, sub, :], in1=faddt[:, sub, :], op=ALU.add), reads=impacc.b + faddt.b, writes=scr_.b)
                            op("dve", lambda e: e.max(out=m8[:, 0:8], in_=scr_[:]), reads=scr_.b, writes=m8.b)
                            op("dve", lambda e: e.match_replace(out=scr2[:], in_to_replace=m8[:, 0:8], in_values=scr_[:], imm_value=-1e30), reads=scr_.b + m8.b, writes=scr2.b)
                            op("dve", lambda e: e.max(out=m8[:, 8:16], in_=scr2[:]), reads=scr2.b, writes=m8.b)
                            op("dve", lambda e, mb_=mb_: e.tensor_scalar(out=mb_, in0=scr_[:], scalar1=m8[:, 15:16], scalar2=MNEG, op0=ALU.is_lt, op1=ALU.mult), reads=scr_.b + m8.b, writes=mbs.b)
                            tiles = []
                            for c in range(max(0, 4 * qb - 4), 4 * qb + 4):
                                o = c - 4 * qb
                                dmin = t0 - (128 * c + 127)
                                if SLOPES[h] * dmin > SKIP_T:
                                    continue
                                li = c - (4 * qb - 4)
                                extras = [(ident_b[:], cmt[:, 4 + o, :], ident_b.b + cmt.b)]
                                tiles.append((Kw[0:67, g, li * 128:(li + 1) * 128], Kw.b, extras, albt[:, h, o + NT - 4:o + NT - 3], albt.b, Vw_[:, li, g * 65:(g + 1) * 65], None, Vw_.b, ((128 * o, 512) if o >= 0 else (0, 128 * (o + 5)))))
                            attend(tiles, Q, h, g, 3 * h + 2, G_, False, False)
                    for g in range(4):
                        pm = PIMS[g % 2]
                        pmb = pm[:].bitcast(BF16)
                        for sub in range(4):
                            op("pe", lambda e, sub=sub, g=g, pmb=pmb: e.transpose(out=pmb[0:NSEL, sub * 128:(sub + 1) * 128], in_=mbs[:, g * 4 + sub, :], identity=ident_b[:]), reads=mbs.b + ident_b.b, writes=pm.b)
                        op("act", lambda e, pmb=pmb, g=g: e.copy(out=MBTs[g][0:NSEL, :], in_=pmb[0:NSEL, 0:512]), reads=pm.b, writes=MBTs[g].b)
                    for g in range(4):
                        MBT = MBTs[g]
                        for n in range(4):
                            h = 4 * g + n
                            tiles = []
                            for c in range(4 * qb + 4):
                                o = c - 4 * qb
                                dmin = t0 - (128 * c + 127)
                                if SLOPES[h] * dmin > SKIP_T:
                                    continue
                                extras = [(eselt[:, c * 128:(c + 1) * 128], MBT[:, :], eselt.b + MBT.b)]
                                if o >= 0:
                                    extras.append((ident_b[:], cmt[:, 4 + o, :], ident_b.b + cmt.b))
                                tiles.append((KTs[0:67, g, c * 128:(c + 1) * 128], KTs.b, extras, albt[:, h, o + NT - 4:o + NT - 3], albt.b, Vs[:, c, g * 65:(g + 1) * 65], None, Vs.b, ((128 * o, 512) if o >= 0 else (0, 512))))
                            attend(tiles, Q, h, g, 3 * h + 1, G_, False, False)
                    flush()
                    dma("pool", s_o[4 * qb:4 * qb + 4].rearrange("i p f -> p i f"), OACC[:], reads=OACC.b, writes=[SB["o"]])
                sc.barrier()

        def phaseC(l, b, xcur, xcur_b, xnext, xnext_b):
            with ExitStack() as st:
                wbr1 = sb(st, "wbr1", [128, 8, KC * 128], BF16)
                wout = sb(st, "wout", [128, KC * D], BF16)
                ot2 = [[sb(st, "ot%d_%d" % (k, i), [128, D], F32) for i in range(4)] for k in range(2)]
                szBt2 = [sb(st, "szBt%d" % k, [128, 8, 512], BF16) for k in range(2)]
                gBt2 = [sb(st, "gBt%d" % k, [128, 8, 512], BF16) for k in range(2)]
                mpt2 = [sb(st, "mpt%d" % k, [128, 8, 512], BF16) for k in range(2)]

                def c_loads(blk):
                    tsl = slice(blk * 512, blk * 512 + 512)
                    k = blk % 2
                    for i in range(4):
                        dma("sp", ot2[k][i][:], s_o[4 * blk + i], reads=[SB["o"]], writes=ot2[k][i].b)
                    dma("sp", szBt2[k][:], s_szB[:, :, tsl].rearrange("j p t -> p j t"), reads=[SB["szB"]], writes=szBt2[k].b)
                    dma("sp", gBt2[k][:], s_gB[:, :, tsl].rearrange("j p t -> p j t"), reads=[SB["gB"]], writes=gBt2[k].b)
                    dma("sp", mpt2[k][:], s_mp[:, :, tsl].rearrange("j p t -> p j t"), reads=[SB["mp"]], writes=mpt2[k].b)
                yBT = sb(st, "yBT", [128, 8, 512], BF16)
                mT = sb(st, "mT", [128, 8, 512], BF16)
                m1 = sb(st, "m1c", [128, 512], F32)
                xt = [sb(st, "xtc%d" % i, [128, D], F32) for i in range(3)]
                res = [sb(st, "res%d" % i, [128, D], F32) for i in range(3)]
                junk = sb(st, "junkc", [128, 512], BF16)
                stc = sb(st, "stc", [128, 4], F32)
                dma("sp", wbr1[:], wbrB[l, 8:16].rearrange("c p f -> p c f"), reads=[SB["wbrB"]], writes=wbr1.b)
                dma("sp", wout[:], woutB[l], reads=[SB["woutB"]], writes=wout.b)
                for blk in range(NQB):
                    t0 = blk * 512
                    tsl = slice(t0, t0 + 512)
                    if blk == 0:
                        c_loads(0)
                    if blk + 1 < NQB:
                        c_loads(blk + 1)
                    ot, szBt, gBt, mpt = ot2[blk % 2], szBt2[blk % 2], gBt2[blk % 2], mpt2[blk % 2]
                    for j in range(8):
                        ps = nb()
                        for i in range(4):
                            op("pe", lambda e, i=i, j=j, ps=ps: e.transpose(out=ps[:, i * 128:(i + 1) * 128], in_=ot[i][:, j * 128:(j + 1) * 128], identity=ident_f[:]), reads=ot[i].b + ident_f.b, writes=ps.b)
                        op("dve", lambda e, j=j, ps=ps: e.tensor_tensor(out=yBT[:, j, :], in0=ps[:, :], in1=szBt[:, j, :], op=ALU.mult), reads=ps.b + szBt.b, writes=yBT.b)
                    for dch in range(8):
                        ps = nb()
                        for kc in range(KC):
                            op("pe", lambda e, kc=kc, dch=dch, ps=ps: e.matmul(ps[:, :], lhsT=wbr1[:, dch, kc * 128:(kc + 1) * 128], rhs=yBT[:, kc, :], start=(kc == 0), stop=(kc == KC - 1)), reads=wbr1.b + yBT.b, writes=ps.b)
                        op("dve", lambda e, dch=dch, ps=ps: e.tensor_tensor(out=m1[:], in0=ps[:, :], in1=gBt[:, dch, :], op=ALU.mult), reads=ps.b + gBt.b, writes=m1.b)
                        op("pool", lambda e, dch=dch: e.tensor_tensor(out=mT[:, dch, :], in0=m1[:], in1=mpt[:, dch, :], op=ALU.add), reads=m1.b + mpt.b, writes=mT.b)
                    for i in range(4):
                        x_ = xt[(4 * blk + i) % 3]
                        r_ = res[(4 * blk + i) % 3]
                        rows = slice(t0 + i * 128, t0 + (i + 1) * 128)
                        dma("sp", x_[:], xcur[b, rows, :], reads=xcur_b, writes=x_.b)
                        pp = [nb(), nb()]
                        for n_ in range(2):
                            for kc in range(KC):
                                op("pe", lambda e, kc=kc, n_=n_, i=i, p_=pp[n_]: e.matmul(p_[:, :], lhsT=mT[:, kc, i * 128:(i + 1) * 128], rhs=wout[:, kc * D + n_ * 512:kc * D + (n_ + 1) * 512], start=(kc == 0), stop=(kc == KC - 1)),
                                   reads=mT.b + wout.b, writes=pp[n_].b)
                            op("act", lambda e, n_=n_, p_=pp[n_]: e.activation(out=junk[:], in_=p_[:, :], func=AF.Square, accum_out=stc[:, n_:n_ + 1]), reads=pp[n_].b, writes=junk.b + stc.b)
                        op("dve", lambda e: e.tensor_tensor(out=stc[:, 2:3], in0=stc[:, 0:1], in1=stc[:, 1:2], op=ALU.add), reads=stc.b, writes=stc.b)
                        op("dve", lambda e: e.tensor_scalar(out=stc[:, 2:3], in0=stc[:, 2:3], scalar1=1.0 / D, scalar2=EPS, op0=ALU.mult, op1=ALU.add), reads=stc.b, writes=stc.b)
                        op("act", lambda e: e.activation(out=stc[:, 2:3], in_=stc[:, 2:3], func=AF.Sqrt), reads=stc.b, writes=stc.b)
                        op("dve", lambda e: e.reciprocal(out=stc[:, 3:4], in_=stc[:, 2:3]), reads=stc.b, writes=stc.b)
                        for n_ in range(2):
                            sl = slice(n_ * 512, (n_ + 1) * 512)
                            op("dve", lambda e, n_=n_, sl=sl, p_=pp[n_], r_=r_: e.scalar_tensor_tensor(out=r_[:, sl], in0=p_[:, :], scalar=stc[:, 3:4], in1=GP[:, b, sl], op0=ALU.mult, op1=ALU.mult), reads=pp[n_].b + stc.b + GP.b, writes=r_.b)
                        op("pool", lambda e, r_=r_, x_=x_: e.tensor_tensor(out=r_[:], in0=r_[:], in1=x_[:], op=ALU.add), reads=r_.b + x_.b, writes=r_.b)
                        dma("pool", xnext[b, rows, :], r_[:], reads=r_.b, writes=xnext_b)
                sc.barrier()

        for l in range(DEPTH):
            xcur = x_in if l == 0 else xmid
            xnext = y_out if l == DEPTH - 1 else xmid
            xcur_b = [] if l == 0 else [SB["xmid"]]
            xnext_b = [SB["y"]] if l == DEPTH - 1 else [SB["xmid"]]
            with ExitStack() as st:
                sct = sb(st, "sct", [128, KC, NB], F32)
                scb = sb(st, "scb", [128, NB, KC, 128], F32)
                ones = sb(st, "ones", [128, 128], F32)
                gpre_t = sb(st, "gpre_t", [128, KC], F32)
                badaf_t = sb(st, "badaf_t", [128, 16], F32)
                ss_t = sb(st, "ss_t", [128, 16, NB], F32)
                gpost_bc = sb(st, "gpost_bc", [128, D], F32)
                badag_bc = sb(st, "badag_bc", [128, D], F32)
                wa = [sb(st, "wa%d" % i, [128, 4, KC * 128], F32) for i in range(2)]
                dma("sp", sct[:], cT[:, :, :], writes=sct.b)
                dma("sp", gpre_t[:], gpre[l], writes=gpre_t.b)
                dma("sp", badaf_t[:], badaf[l], writes=badaf_t.b)
                dma("sp", gpost_bc[:], gpost[l].broadcast_to([128, D]), writes=gpost_bc.b)
                dma("sp", badag_bc[:], badag[l].broadcast_to([128, D]), writes=badag_bc.b)
                op("act", lambda e: e.activation(out=sct[:], in_=sct[:], func=AF.Silu), reads=sct.b, writes=sct.b)
                op("dve", lambda e: e.memset(ones[:], 1.0), writes=ones.b)
                for b in range(NB):
                    for kc in range(KC):
                        op("dve", lambda e, b=b, kc=kc: e.tensor_scalar(out=scb[:, b, kc, :], in0=ones[:], scalar1=sct[:, kc, b:b + 1], scalar2=None, op0=ALU.mult),
                           reads=ones.b + sct.b, writes=scb.b)
                for grp in range(6):
                    w = wa[grp % 2]
                    dma("sp", w[:], wadaT[l, grp * 4:(grp + 1) * 4].rearrange("c p f -> p c f"), writes=w.b)
                    if grp < 4:
                        ps = PS[grp % 2]
                        for ci in range(4):
                            for kc in range(KC):
                                op("pe", lambda e, w=w, ci=ci, kc=kc, ps=ps: e.matmul(ps[:, ci * NB:(ci + 1) * NB], lhsT=w[:, ci, kc * 128:(kc + 1) * 128], rhs=sct[:, kc, :], start=(kc == 0), stop=(kc == KC - 1)),
                                   reads=w.b + sct.b, writes=ps.b)
                        for ci in range(4):
                            f = grp * 4 + ci
                            op("dve", lambda e, ps=ps, ci=ci, f=f: e.tensor_scalar(out=ss_t[:, f, :], in0=ps[:, ci * NB:(ci + 1) * NB], scalar1=badaf_t[:, f:f + 1], scalar2=None, op0=ALU.add),
                               reads=ps.b + badaf_t.b, writes=ss_t.b)
                    else:
                        half = grp - 4
                        for b in range(NB):
                            ps = PS[2 + b]
                            for kc in range(KC):
                                op("pe", lambda e, w=w, kc=kc, ps=ps, b=b: e.matmul(ps[:, :], lhsT=scb[:, b, kc, :], rhs=w[:, :, kc * 128:(kc + 1) * 128], start=(kc == 0), stop=(kc == KC - 1)),
                                   reads=w.b + scb.b, writes=ps.b)
                            sl = slice(half * 512, (half + 1) * 512)
                            op("dve", lambda e, ps=ps, b=b, sl=sl: e.tensor_tensor(out=GP[:, b, sl], in0=ps[:, :], in1=badag_bc[:, sl], op=ALU.add), reads=ps.b + badag_bc.b, writes=GP.b)
                            op("dve", lambda e, b=b, sl=sl: e.tensor_tensor(out=GP[:, b, sl], in0=GP[:, b, sl], in1=gpost_bc[:, sl], op=ALU.mult), reads=GP.b + gpost_bc.b, writes=GP.b)
                for b in range(NB):
                    op("dve", lambda e, b=b: e.tensor_copy(out=A_shift[:, :, b], in_=ss_t[:, 0:8, b]), reads=ss_t.b, writes=A_shift.b)
                    op("dve", lambda e, b=b: e.scalar_tensor_tensor(out=A_scale[:, :, b], in0=ss_t[:, 8:16, b], scalar=1.0, in1=gpre_t[:, :], op0=ALU.add, op1=ALU.mult),
                       reads=ss_t.b + gpre_t.b, writes=A_scale.b)
                sc.barrier()

            for b in range(NB):
                phaseA(l, b, xcur, xcur_b)
                if "stopA" in dbg:
                    continue
                phaseB(l, b)
                if "stopB" in dbg:
                    continue
                phaseC(l, b, xcur, xcur_b, xnext, xnext_b)
        sc.finish()
    return sc


def _chunked(W):
    N = W.shape[1]
    return np.ascontiguousarray(W.reshape(8, 128, N // 128, 128).transpose(2, 1, 0, 3)).reshape(N // 128, 128, 1024)


def prep_shared(inp, S, DEPTH):
    f = np.float32
    g = {k: np.asarray(v, dtype=f) for k, v in inp.items() if k not in ("x", "c")}
    out = {}
    win = []
    for l in range(DEPTH):
        W = g["w_in"][l]
        cols = []
        for j in range(8):
            for kind in range(4):
                cols.append(W[:, kind * 1024 + j * 128: kind * 1024 + (j + 1) * 128])
        cols.append(W[:, 4096:7680])
        pad = np.zeros((1024, 128), f)
        pad[:, :48] = W[:, 7680:7728]
        cols.append(pad)
        cols.append(W[:, 7728:])
        Wp = np.concatenate(cols, axis=1)
        assert Wp.shape[1] == NCH * 128
        win.append(_chunked(Wp))
    out["winT"] = np.stack(win)
    out["wadaT"] = np.stack([_chunked(g["w_ada"][l]) for l in range(DEPTH)])
    out["wbrT"] = np.stack([np.concatenate([_chunked(g["w_br"][l, i]) for i in range(3)], 0) for l in range(DEPTH)])
    out["woutT"] = np.stack([np.ascontiguousarray(g["w_out"][l].reshape(8, 128, 1024).transpose(1, 0, 2)).reshape(128, 8192) for l in range(DEPTH)])
    out["gpre"] = np.ascontiguousarray(g["g_pre"][:DEPTH].reshape(DEPTH, 8, 128).transpose(0, 2, 1))
    out["gpost"] = g["g_post"][:DEPTH].reshape(DEPTH, 1, D)
    out["badag"] = np.ascontiguousarray(g["b_ada"][:DEPTH, 2048:]).reshape(DEPTH, 1, D)
    out["badaf"] = np.ascontiguousarray(g["b_ada"][:DEPTH, :2048].reshape(DEPTH, 16, 128).transpose(0, 2, 1))
    out["cw"] = np.ascontiguousarray(g["conv_w"][:DEPTH].reshape(DEPTH, 3, 8, 128).transpose(0, 3, 2, 1))
    out["cb"] = np.ascontiguousarray(g["conv_b"][:DEPTH].reshape(DEPTH, 8, 128).transpose(0, 2, 1))
    out["posT"] = np.ascontiguousarray(np.stack([g["pos_ck"][:DEPTH], g["pos_cv"][:DEPTH]], 1).transpose(0, 1, 3, 2))
    w1 = np.stack([g["w_ck1"][:DEPTH], g["w_cv1"][:DEPTH]], 1)
    out["w1"] = np.ascontiguousarray(w1.reshape(DEPTH, 2, 32, 64, 128).transpose(0, 1, 3, 2, 4)).reshape(DEPTH, 2, 64, 4096)
    out["w2"] = np.ascontiguousarray(np.stack([g["w_ck2"][:DEPTH], g["w_cv2"][:DEPTH]], 1))
    out["lng"] = np.ascontiguousarray(g["ln_g"][:DEPTH].reshape(DEPTH, 8, 128).transpose(0, 2, 1))
    out["lnb"] = g["ln_b"][:DEPTH].reshape(DEPTH, 1, D)
    out["wsT"] = np.ascontiguousarray(g["w_s"][:DEPTH].transpose(0, 3, 1, 2))
    out["bs"] = g["b_s"][:DEPTH].reshape(DEPTH, 1, D)
    for k, v in host_consts(S).items():
        out["k_" + k] = v
    return out


def prep_core(x, c, b0, NB):
    xs = np.ascontiguousarray(np.asarray(x[b0:b0 + NB], dtype=np.float32))
    cs = np.asarray(c[b0:b0 + NB], dtype=np.float32)
    cT = np.ascontiguousarray(cs.reshape(NB, 8, 128).transpose(2, 1, 0))
    return {"x": xs, "cT": cT}


_CACHE = {}


def run(inputs, S, NB, DEPTH, ncores, dbg=None):
    key = (S, NB, DEPTH, tuple(sorted(dbg)) if dbg else None)
    nc = bass.Bass("TRN2", target_bir_lowering=False)
    build(nc, S, NB, DEPTH, dbg)
    shared = prep_shared(inputs, S, DEPTH)
    in_maps = []
    for core in range(ncores):
        m = dict(shared)
        m.update(prep_core(inputs["x"], inputs["c"], core * NB, NB))
        in_maps.append(m)
    res = run_bass_kernel_spmd(nc, in_maps, core_ids=list(range(ncores)))
    return res


def kernel(**inputs):
    S, NB, DEPTH, NCORES = 4096, 2, 2, 8
    res = run(inputs, S, NB, DEPTH, NCORES)
    return np.concatenate([np.asarray(r["y"]) for r in res.results], axis=0).astype(np.float32)
```

```python
from contextlib import ExitStack
import numpy as np
import concourse.bass as bass
import concourse.mybir as mybir
from concourse.bass_utils import run_bass_kernel_spmd

F32 = mybir.dt.float32
BF16 = mybir.dt.bfloat16
AF = mybir.ActivationFunctionType
ALU = mybir.AluOpType

D = 1024
KC = 8
NCH = 109
EPS = 1e-6
MNEG = -32768.0
CH_B, CH_CG, CH_XIN, CH_ZA = 0, 8, 16, 24
CH_Q, CH_KC, CH_VC, CH_KS, CH_VS, CH_KW, CH_VW, CH_ZB, CH_GL = 32, 40, 42, 44, 46, 48, 50, 52, 60
CH_U, CH_V, CH_ZC, CH_GA, CH_GB, CH_GC = 61, 69, 77, 85, 93, 101


class Buf:
    __slots__ = ("w", "r")

    def __init__(self):
        self.w = {}
        self.r = {}


class Sched:
    def __init__(self, nc, ndma=8):
        self.nc = nc
        self.engs = {"pe": nc.tensor, "act": nc.scalar, "dve": nc.vector, "pool": nc.gpsimd, "sp": nc.sync}
        self.sems = {}
        self.cnt = {}
        for e in ("pe", "act", "dve", "pool"):
            self.sems[e] = nc.alloc_semaphore("c_" + e)
            self.cnt[e] = 0
        self.dq = {}
        for q in ("sp", "pool", "act"):
            keys = []
            for i in range(ndma):
                k = "d_%s%d" % (q, i)
                self.sems[k] = nc.alloc_semaphore(k)
                keys.append(k)
            self.dq[q] = [keys, 0]
        self.known = {e: {} for e in self.engs}
        self.last = {}
        self.ninst = 0

    def _wait(self, eng, deps):
        kn = self.known[eng]
        for k, v in deps.items():
            if kn.get(k, 0) >= v:
                continue
            self.engs[eng].wait_ge(self.sems[k], v)
            kn[k] = v
            self.ninst += 1

    def _deps(self, reads, writes, skip):
        deps = {}
        for b in reads:
            for k, v in b.w.items():
                if k != skip and deps.get(k, 0) < v:
                    deps[k] = v
        for b in writes:
            for dd in (b.w, b.r):
                for k, v in dd.items():
                    if k != skip and deps.get(k, 0) < v:
                        deps[k] = v
        return deps

    def _commit(self, k, v, reads, writes):
        self.last[k] = v
        for b in reads:
            if b.r.get(k, 0) < v:
                b.r[k] = v
        for b in writes:
            if b.w.get(k, 0) < v:
                b.w[k] = v

    def op(self, eng, fn, reads=(), writes=()):
        deps = self._deps(reads, writes, "pe" if eng == "pe" else None)
        self._wait(eng, deps)
        inst = fn(self.engs[eng])
        self.cnt[eng] += 1
        inst.then_inc(self.sems[eng], 1)
        self.ninst += 1
        self._commit(eng, self.cnt[eng], reads, writes)

    def dma(self, q, out, in_, reads=(), writes=()):
        keys, i = self.dq[q]
        self.dq[q][1] = i + 1
        k = keys[i % len(keys)]
        v = 16 * (i // len(keys) + 1)
        deps = self._deps(reads, writes, None)
        if v > 16 and deps.get(k, 0) < v - 16:
            deps[k] = v - 16
        self._wait(q, deps)
        self.engs[q].dma_start(out=out, in_=in_).then_inc(self.sems[k], 16)
        self.ninst += 1
        self._commit(k, v, reads, writes)

    def barrier(self):
        for e in self.engs:
            self._wait(e, dict(self.last))

    def finish(self):
        self._wait("sp", dict(self.last))


class TT:
    def __init__(self, t, nparts=1):
        self.t = t
        self.b = [Buf() for _ in range(nparts)]

    def __getitem__(self, idx):
        return self.t[idx]


def host_consts(S):
    NT = S // 128
    NSEL = S // 64
    NCMP = (S - 32) // 16 + 1
    NCT = (NCMP + 127) // 128
    NQB = S // 512
    c = {}
    c["ident"] = np.eye(128, dtype=np.float32)
    p = np.arange(128)[:, None]
    tt = np.arange(512)[None, :]
    cm = np.zeros((13, 128, 512), np.float32)
    for o in range(-4, 4):
        kk = 128 * o + p
        valid = (tt < kk + 512) if o < 0 else (tt >= kk)
        cm[o + 4] = np.where(valid, 0.0, MNEG)
    for m in range(5):
        cm[8 + m] = np.where(tt + 512 * m >= 16 * p + 31, 0.0, MNEG)
    c["cm"] = np.ascontiguousarray(cm.transpose(1, 0, 2))
    i = np.arange(NCT * 128)[:, None]
    j = np.arange(NSEL)[None, :]
    ov = ((16 * i <= 64 * j + 63) & (16 * i + 31 >= 64 * j) & (i < NCMP)).astype(np.float32)
    c["ov"] = np.ascontiguousarray(ov.reshape(NCT, 128, NSEL).transpose(1, 0, 2))
    kp = np.arange(NT * 128)[None, :]
    c["esel"] = (np.arange(NSEL)[:, None] == kp // 64).astype(np.float32)
    t = np.arange(S)[:, None]
    cur = t // 64
    forced = (j == 0) | (j == cur) | (j == cur - 1)
    fadd = np.where(forced, 1e4, np.where(j * 64 <= t, 0.0, -1e4)).astype(np.float32)
    c["fadd"] = np.ascontiguousarray(fadd.reshape(NT, 128, NSEL).transpose(1, 0, 2))
    slopes = 2.0 ** (-8.0 * np.arange(1, 17) / 16.0)
    import ml_dtypes
    v = (-8.0 * slopes[:, None] * np.arange(512)[None, :]).astype(np.float64)
    rows = []
    rem = v.copy()
    for _ in range(3):
        hi = rem.astype(np.float32).astype(ml_dtypes.bfloat16).astype(np.float64)
        rows.append(hi)
        rem = rem - hi
    c["aq"] = np.stack(rows, 0).astype(np.float32)
    o = np.arange(NT) - (NT - 4)
    c["alb"] = (slopes[None, :, None] * (128.0 * o[None, None, :] + np.arange(128)[:, None, None])).astype(np.float32)
    m = np.arange(8)
    c["albc"] = (slopes[None, :, None] * (16.0 * np.arange(128)[:, None, None] + 31.0 - 512.0 * m[None, None, :])).astype(np.float32)
    c["tri"] = (np.arange(128)[None, :] >= np.arange(128)[:, None]).astype(np.float32)
    return c


def build(nc, S, NB, DEPTH, dbg=None):
    NT = S // 128
    NSEL = S // 64
    NCMP = (S - 32) // 16 + 1
    NCT = (NCMP + 127) // 128
    NQB = S // 512
    dbg = dbg or set()

    def din(name, shape, dt=F32):
        return nc.dram_tensor(name, list(shape), dt, kind="ExternalInput").ap()

    def dscr(name, shape, dt):
        return nc.dram_tensor(name, list(shape), dt, kind="ExternalOutput" if name in dbg else "Internal").ap()

    x_in = din("x", [NB, S, D])
    cT = din("cT", [128, KC, NB])
    gpre = din("gpre", [DEPTH, 128, KC])
    gpost = din("gpost", [DEPTH, 1, D])
    badag = din("badag", [DEPTH, 1, D])
    badaf = din("badaf", [DEPTH, 128, 16])
    wadaT = din("wadaT", [DEPTH, 24, 128, KC * 128])
    winT = din("winT", [DEPTH, NCH, 128, KC * 128])
    wbrT = din("wbrT", [DEPTH, 24, 128, KC * 128])
    woutT = din("woutT", [DEPTH, 128, KC * D])
    cw = din("cw", [DEPTH, 128, 8, 3])
    cb = din("cb", [DEPTH, 128, 8])
    posT = din("posT", [DEPTH, 2, 64, 32])
    w1 = din("w1", [DEPTH, 2, 64, 32 * 128])
    w2 = din("w2", [DEPTH, 2, 128, 64])
    lng = din("lng", [DEPTH, 128, 8])
    lnb = din("lnb", [DEPTH, 1, D])
    wsT = din("wsT", [DEPTH, 128, 8, 128])
    bs = din("bs", [DEPTH, 1, D])
    k_ident = din("k_ident", [128, 128])
    k_cm = din("k_cm", [128, 13, 512])
    k_ov = din("k_ov", [128, NCT, NSEL])
    k_esel = din("k_esel", [NSEL, NT * 128])
    k_fadd = din("k_fadd", [128, NT, NSEL])
    k_aq = din("k_aq", [3, 16, 512])
    k_alb = din("k_alb", [128, 16, NT])
    k_albc = din("k_albc", [128, 16, 8])
    k_tri = din("k_tri", [128, 128])
    y_out = nc.dram_tensor("y", [NB, S, D], F32, kind="ExternalOutput").ap()

    winB = dscr("winB", [DEPTH, NCH, 128, KC * 128], BF16)
    wbrB = dscr("wbrB", [DEPTH, 24, 128, KC * 128], BF16)
    woutB = dscr("woutB", [DEPTH, 128, KC * D], BF16)
    s_qT = dscr("s_qT", [64, 16, S], BF16)
    s_kT = {nm: dscr("s_" + nm, [64, 4, S], BF16) for nm in ("kc", "vc", "ks", "kw")}
    s_vX = {nm: dscr("s_" + nm, [NT, 128, 260], BF16) for nm in ("vs", "vw")}
    s_szB = dscr("s_szB", [8, 128, S], BF16)
    s_gB = dscr("s_gB", [8, 128, S], BF16)
    s_glg = dscr("s_glg", [NT, 128, 48], F32)
    s_mp = dscr("s_mp", [8, 128, S], BF16)
    s_o = dscr("s_o", [NT, 128, D], F32)
    xmid = dscr("xmid", [NB, S, D], F32)
    SB = {k: Buf() for k in ("winB", "wbrB", "woutB", "qT", "kc", "vc", "ks", "kw", "vs", "vw", "szB", "gB", "glg", "mp", "o", "xmid", "y")}

    sc = Sched(nc)
    op, dma = sc.op, sc.dma

    with ExitStack() as top:
        uniq = [0]

        def sb(st, name, shape, dt, nparts=1):
            uniq[0] += 1
            return TT(st.enter_context(nc.sbuf_tensor("%s_%d" % (name, uniq[0]), list(shape), dt)), nparts)

        PS = [TT(top.enter_context(nc.psum_tensor("ps%d" % i, [128, 512], F32))) for i in range(8)]
        ident_f = sb(top, "ident_f", [128, 128], F32)
        ident_b = sb(top, "ident_b", [128, 128], BF16)
        dma("sp", ident_f[:], k_ident[:, :], writes=ident_f.b)
        dma("pool", ident_b[:], k_ident[:, :], writes=ident_b.b)
        A_scale = sb(top, "A_scale", [128, KC, NB], F32)
        A_shift = sb(top, "A_shift", [128, KC, NB], F32)
        GP = sb(top, "GP", [128, NB, D], F32)

        with ExitStack() as st:
            G = 4
            stg = [sb(st, "cv_f%d" % i, [128, G, 1024], F32) for i in range(2)]
            stb = [sb(st, "cv_b%d" % i, [128, G, 1024], BF16) for i in range(2)]
            jobs = []
            for l in range(DEPTH):
                for c0 in range(0, NCH, G):
                    n = min(G, NCH - c0)
                    jobs.append((winT[l, c0:c0 + n], winB[l, c0:c0 + n], n, SB["winB"]))
                for c0 in range(0, 24, G):
                    jobs.append((wbrT[l, c0:c0 + G], wbrB[l, c0:c0 + G], G, SB["wbrB"]))
                for c0 in range(0, 8, G):
                    jobs.append((woutT[l, :, c0 * 1024:(c0 + G) * 1024], woutB[l, :, c0 * 1024:(c0 + G) * 1024], -G, SB["woutB"]))
            for ji, (src, dst, n, bf) in enumerate(jobs):
                f, b_ = stg[ji % 2], stb[ji % 2]
                if n > 0:
                    dma("sp", f[:, 0:n, :], src.rearrange("c p f -> p c f"), writes=f.b)
                else:
                    n = -n
                    dma("sp", f[:, 0:n, :], src.rearrange("p (c f) -> p c f", c=n), writes=f.b)
                    dst = dst.rearrange("p (c f) -> c p f", c=n)
                eng = ("dve", "act", "pool")[ji % 3]
                if eng == "act":
                    op("act", lambda e, f=f, b_=b_, n=n: e.copy(out=b_[:, 0:n, :], in_=f[:, 0:n, :]), reads=f.b, writes=b_.b)
                else:
                    op(eng, lambda e, f=f, b_=b_, n=n: e.tensor_copy(out=b_[:, 0:n, :], in_=f[:, 0:n, :]), reads=f.b, writes=b_.b)
                dma("pool", dst.rearrange("c p f -> p c f"), b_[:, 0:n, :], reads=b_.b, writes=[bf])
            sc.barrier()

        rr = {"ps": 0, "wt": 0, "cs": 0}

        def nb():
            rr["ps"] = (rr["ps"] + 1) % 8
            return PS[rr["ps"]]

        def evac_copy(i, out, in_, reads, writes):
            if i % 2 == 0:
                op("act", lambda e: e.copy(out=out, in_=in_), reads=reads, writes=writes)
            else:
                op("dve", lambda e: e.tensor_copy(out=out, in_=in_), reads=reads, writes=writes)

        def phaseA(l, b, xcur, xcur_b):
            with ExitStack() as st:
                hT = sb(st, "hT", [128, KC, 512], BF16)
                xt = [sb(st, "xt%d" % i, [128, D], F32) for i in range(4)]
                junk = sb(st, "junk", [128, D], BF16)
                stat = sb(st, "stat", [128, 8], F32)
                wt = [sb(st, "wt%d" % i, [128, 4, KC * 128], BF16) for i in range(3)]
                wbr0 = sb(st, "wbr0", [128, 8, KC * 128], BF16)
                wbr2 = sb(st, "wbr2", [128, 8, KC * 128], BF16)
                yAT = sb(st, "yAT", [128, 8, 512], BF16)
                yCT = sb(st, "yCT", [128, 8, 512], BF16)
                sgA = sb(st, "sgA", [128, 8, 512], BF16)
                sgC = sb(st, "sgC", [128, 8, 512], BF16)
                carry = sb(st, "carry", [128, 8, 2], F32)
                cw_t = sb(st, "cw_t", [128, 8, 3], F32)
                cb_t = sb(st, "cb_t", [128, 8], F32)
                lng_t = sb(st, "lng_t", [128, 8], F32)
                wmT = sb(st, "wmT", [128, 8, 128], BF16)
                Bc = sb(st, "Bc", [128, 8, 128], F32)
                vn = [sb(st, "vn%d" % i, [128, D], BF16) for i in range(4)]
                gv = sb(st, "gv", [128, D], F32)
                bst = sb(st, "bst", [128, 16], F32)
                tmp = [sb(st, "tmpA%d" % i, [128, 516], F32) for i in range(6)]
                qst = [sb(st, "qst%d" % i, [128, 4, 512], BF16) for i in range(3)]
                vst = sb(st, "vst", [128, 4, 2, 260], BF16)
                gst = sb(st, "gst", [128, 4, 48], F32)
                cst = [sb(st, "cst%d" % i, [128, 512], BF16) for i in range(8)]
                ugs = [sb(st, "ugs%d" % i, [128, 512], BF16) for i in range(4)]
                dma("sp", wbr0[:], wbrB[l, 0:8].rearrange("c p f -> p c f"), reads=[SB["wbrB"]], writes=wbr0.b)
                dma("sp", wbr2[:], wbrB[l, 16:24].rearrange("c p f -> p c f"), reads=[SB["wbrB"]], writes=wbr2.b)
                dma("sp", cw_t[:], cw[l], writes=cw_t.b)
                dma("sp", cb_t[:], cb[l], writes=cb_t.b)
                dma("sp", lng_t[:], lng[l], writes=lng_t.b)
                op("dve", lambda e: e.memset(vst[:], 1.0), writes=vst.b)
                op("dve", lambda e: e.memset(carry[:], 0.0), writes=carry.b)
                wsf, trif, lnbf, bsf = xt[0], xt[1], xt[2], xt[3]
                lnbb = junk
                dma("sp", wsf[:, 0:1024].rearrange("p (g i) -> p g i", g=8), wsT[l], writes=wsf.b)
                dma("sp", trif[:, 0:128], k_tri[:, :], writes=trif.b)
                dma("sp", lnbf[:], lnb[l].broadcast_to([128, D]), writes=lnbf.b)
                dma("sp", bsf[:], bs[l].broadcast_to([128, D]), writes=bsf.b)
                op("dve", lambda e: e.tensor_copy(out=lnbb[:], in_=lnbf[:]), reads=lnbf.b, writes=lnbb.b)
                for g in range(8):
                    op("dve", lambda e, g=g: e.tensor_tensor(out=wmT[:, g, :], in0=wsf[:, g * 128:(g + 1) * 128], in1=trif[:, 0:128], op=ALU.mult), reads=wsf.b + trif.b, writes=wmT.b)
                for g in range(8):
                    ps = nb()
                    op("pe", lambda e, g=g, ps=ps: e.matmul(ps[:, 0:128], lhsT=lnbb[:, g * 128:(g + 1) * 128], rhs=wmT[:, g, :], start=True, stop=True), reads=lnbb.b + wmT.b, writes=ps.b)
                    op("dve", lambda e, g=g, ps=ps: e.tensor_tensor(out=Bc[:, g, :], in0=ps[:, 0:128], in1=bsf[:, g * 128:(g + 1) * 128], op=ALU.add), reads=ps.b + bsf.b, writes=Bc.b)

                def loadw(ch0, n):
                    w = wt[rr["wt"] % 3]
                    rr["wt"] += 1
                    dma("sp", w[:, 0:n, :], winB[l, ch0:ch0 + n].rearrange("c p f -> p c f"), reads=[SB["winB"]], writes=w.b)
                    return w

                def mm_fm(ps, w, ci, m0, M):
                    hT_ = cur["hT"]
                    for kc in range(KC):
                        op("pe", lambda e, kc=kc: e.matmul(ps[0:M, :], lhsT=w[:, ci, kc * 128 + m0:kc * 128 + m0 + M], rhs=hT_[:, kc, :], start=(kc == 0), stop=(kc == KC - 1)),
                           reads=w.b + hT_.b, writes=ps.b)

                def mm_tm(ps, w, c0, n, i, ncol=None):
                    hT_ = cur["hT"]
                    for kc in range(KC):
                        if ncol is None:
                            o_ap = ps[:, 0:n * 128].rearrange("p (c m) -> p c m", c=n)
                            r_ap = w[:, c0:c0 + n, kc * 128:(kc + 1) * 128]
                        else:
                            o_ap = ps[:, 0:ncol]
                            r_ap = w[:, c0, kc * 128:kc * 128 + ncol]
                        op("pe", lambda e, kc=kc, o_ap=o_ap, r_ap=r_ap: e.matmul(o_ap, lhsT=hT_[:, kc, i * 128:(i + 1) * 128], rhs=r_ap, start=(kc == 0), stop=(kc == KC - 1)),
                           reads=w.b + hT_.b, writes=ps.b)

                hT2 = [hT, sb(st, "hTb", [128, KC, 512], BF16)]
                xt2 = [xt, [sb(st, "xtb%d" % i, [128, D], F32) for i in range(4)]]

                def a1_pre(blk):
                    t0 = blk * 512
                    xs = xt2[blk % 2]
                    for i in range(4):
                        x_ = xs[i]
                        dma("sp", x_[:], xcur[b, t0 + i * 128:t0 + (i + 1) * 128, :], reads=xcur_b, writes=x_.b)
                        op("act", lambda e, x_=x_, i=i: e.activation(out=junk[:], in_=x_[:], func=AF.Square, accum_out=stat[:, i:i + 1]), reads=x_.b, writes=junk.b + stat.b)
                        op("dve", lambda e, i=i: e.tensor_scalar(out=stat[:, 4 + i:5 + i], in0=stat[:, i:i + 1], scalar1=1.0 / D, scalar2=EPS, op0=ALU.mult, op1=ALU.add), reads=stat.b, writes=stat.b)
                        op("act", lambda e, i=i: e.activation(out=stat[:, 4 + i:5 + i], in_=stat[:, 4 + i:5 + i], func=AF.Sqrt), reads=stat.b, writes=stat.b)
                        op("dve", lambda e, i=i: e.reciprocal(out=stat[:, 4 + i:5 + i], in_=stat[:, 4 + i:5 + i]), reads=stat.b, writes=stat.b)
                        op("act", lambda e, x_=x_, i=i: e.activation(out=x_[:], in_=x_[:], func=AF.Copy, scale=stat[:, 4 + i:5 + i]), reads=x_.b + stat.b, writes=x_.b)

                def a1_trans(blk):
                    xs = xt2[blk % 2]
                    hT_ = hT2[blk % 2]
                    for kc in range(KC):
                        ps = nb()
                        for i in range(4):
                            op("pe", lambda e, i=i, kc=kc, ps=ps: e.transpose(out=ps[:, i * 128:(i + 1) * 128], in_=xs[i][:, kc * 128:(kc + 1) * 128], identity=ident_f[:]), reads=xs[i].b + ident_f.b, writes=ps.b)
                        if kc % 2 == 0:
                            op("act", lambda e, kc=kc, ps=ps: e.activation(out=hT_[:, kc, :], in_=ps[:, :], func=AF.Identity, scale=A_scale[:, kc, b:b + 1], bias=A_shift[:, kc, b:b + 1]),
                               reads=ps.b + A_scale.b + A_shift.b, writes=hT_.b)
                        else:
                            op("dve", lambda e, kc=kc, ps=ps: e.tensor_scalar(out=hT_[:, kc, :], in0=ps[:, :], scalar1=A_scale[:, kc, b:b + 1], scalar2=A_shift[:, kc, b:b + 1], op0=ALU.mult, op1=ALU.add),
                               reads=ps.b + A_scale.b + A_shift.b, writes=hT_.b)

                cur = {"hT": hT}
                a1_pre(0)
                a1_trans(0)
                for blk in range(NQB):
                    t0 = blk * 512
                    tsl = slice(t0, t0 + 512)
                    cur["hT"] = hT2[blk % 2]
                    if blk + 1 < NQB:
                        a1_pre(blk + 1)
                    wv0 = loadw(CH_V, 4)
                    wv1 = loadw(CH_V + 4, 4)
                    for i in range(4):
                        p0, p1 = nb(), nb()
                        mm_tm(p0, wv0, 0, 4, i)
                        mm_tm(p1, wv1, 0, 4, i)
                        op("act", lambda e, p0=p0: e.activation(out=gv[:, 0:512], in_=p0[:, :], func=AF.Gelu_apprx_tanh), reads=p0.b, writes=gv.b)
                        op("act", lambda e, p1=p1: e.activation(out=gv[:, 512:1024], in_=p1[:, :], func=AF.Gelu_apprx_tanh), reads=p1.b, writes=gv.b)
                        op("dve", lambda e: e.bn_stats(out=bst[:, 0:6], in_=gv[:, 0:512]), reads=gv.b, writes=bst.b)
                        op("dve", lambda e: e.bn_stats(out=bst[:, 6:12], in_=gv[:, 512:1024]), reads=gv.b, writes=bst.b)
                        op("dve", lambda e: e.bn_aggr(out=bst[:, 12:14], in_=bst[:, 0:12]), reads=bst.b, writes=bst.b)
                        op("dve", lambda e: e.tensor_scalar(out=bst[:, 14:15], in0=bst[:, 13:14], scalar1=EPS, scalar2=None, op0=ALU.add), reads=bst.b, writes=bst.b)
                        op("act", lambda e: e.activation(out=bst[:, 14:15], in_=bst[:, 14:15], func=AF.Sqrt), reads=bst.b, writes=bst.b)
                        op("dve", lambda e: e.reciprocal(out=bst[:, 14:15], in_=bst[:, 14:15]), reads=bst.b, writes=bst.b)
                        op("dve", lambda e, i=i: e.tensor_scalar(out=vn[i][:], in0=gv[:], scalar1=bst[:, 12:13], scalar2=bst[:, 14:15], op0=ALU.subtract, op1=ALU.mult), reads=gv.b + bst.b, writes=vn[i].b)
                    xin_sb, ybuf, acc, sz, t1, t2 = tmp
                    for j in range(8):
                        w = loadw(4 * j, 4)
                        pb, pcg, pxin, pz = nb(), nb(), nb(), nb()
                        mm_fm(pcg, w, 1, 0, 128)
                        mm_fm(pxin, w, 2, 0, 128)
                        mm_fm(pz, w, 3, 0, 128)
                        mm_fm(pb, w, 0, 0, 128)
                        op("act", lambda e, pxin=pxin: e.copy(out=xin_sb[:, 0:512], in_=pxin[:, :]), reads=pxin.b, writes=xin_sb.b)
                        op("dve", lambda e, j=j: e.tensor_copy(out=ybuf[:, 0:2], in_=carry[:, j, :]), reads=carry.b, writes=ybuf.b)
                        op("dve", lambda e, pcg=pcg: e.tensor_tensor(out=ybuf[:, 2:514], in0=pcg[:, :], in1=xin_sb[:, 0:512], op=ALU.mult), reads=pcg.b + xin_sb.b, writes=ybuf.b)
                        op("dve", lambda e, j=j: e.tensor_copy(out=carry[:, j, :], in_=ybuf[:, 512:514]), reads=ybuf.b, writes=carry.b)
                        op("dve", lambda e, j=j: e.tensor_scalar(out=acc[:, 0:512], in0=ybuf[:, 2:514], scalar1=cw_t[:, j, 2:3], scalar2=cb_t[:, j:j + 1], op0=ALU.mult, op1=ALU.add),
                           reads=ybuf.b + cw_t.b + cb_t.b, writes=acc.b)
                        op("dve", lambda e, j=j: e.scalar_tensor_tensor(out=acc[:, 0:512], in0=ybuf[:, 1:513], scalar=cw_t[:, j, 1:2], in1=acc[:, 0:512], op0=ALU.mult, op1=ALU.add),
                           reads=ybuf.b + cw_t.b + acc.b, writes=acc.b)
                        op("dve", lambda e, j=j: e.scalar_tensor_tensor(out=acc[:, 0:512], in0=ybuf[:, 0:512], scalar=cw_t[:, j, 0:1], in1=acc[:, 0:512], op0=ALU.mult, op1=ALU.add),
                           reads=ybuf.b + cw_t.b + acc.b, writes=acc.b)
                        op("act", lambda e, pz=pz: e.activation(out=sz[:, 0:512], in_=pz[:, :], func=AF.Silu), reads=pz.b, writes=sz.b)
                        op("dve", lambda e: e.tensor_tensor(out=t1[:, 0:512], in0=acc[:, 0:512], in1=sz[:, 0:512], op=ALU.mult), reads=acc.b + sz.b, writes=t1.b)
                        op("dve", lambda e, j=j, pb=pb: e.tensor_tensor(out=yAT[:, j, :], in0=pb[:, :], in1=t1[:, 0:512], op=ALU.mult), reads=pb.b + t1.b, writes=yAT.b)
                    for (ch0, kind) in ((CH_GA, "A"), (CH_GC, "C"), (CH_GB, "B"), (CH_ZB, "Z")):
                        for half in range(2):
                            w = loadw(ch0 + 4 * half, 4)
                            for ci in range(4):
                                j = 4 * half + ci
                                ps = nb()
                                mm_fm(ps, w, ci, 0, 128)
                                if kind == "A":
                                    op("act", lambda e, j=j, ps=ps: e.activation(out=sgA[:, j, :], in_=ps[:, :], func=AF.Sigmoid), reads=ps.b, writes=sgA.b)
                                elif kind == "C":
                                    op("act", lambda e, j=j, ps=ps: e.activation(out=sgC[:, j, :], in_=ps[:, :], func=AF.Sigmoid), reads=ps.b, writes=sgC.b)
                                else:
                                    cs = cst[rr["cs"] % 8]
                                    rr["cs"] += 1
                                    fn = AF.Sigmoid if kind == "B" else AF.Silu
                                    op("act", lambda e, cs=cs, ps=ps, fn=fn: e.activation(out=cs[:], in_=ps[:, :], func=fn), reads=ps.b, writes=cs.b)
                                    dst = s_gB if kind == "B" else s_szB
                                    dma("act", dst[j, :, tsl], cs[:], reads=cs.b, writes=[SB["gB" if kind == "B" else "szB"]])
                    szc, tc_, sp2 = sz, t1, t2
                    for half in range(2):
                        wu = loadw(CH_U + 4 * half, 4)
                        wz = loadw(CH_ZC + 4 * half, 4)
                        for ci in range(4):
                            pu = nb()
                            mm_fm(pu, wu, ci, 0, 128)
                            op("act", lambda e, pu=pu, ci=ci: e.activation(out=ugs[ci][:], in_=pu[:, :], func=AF.Gelu_apprx_tanh), reads=pu.b, writes=ugs[ci].b)
                        for ci in range(4):
                            g = 4 * half + ci
                            psp, pzc = nb(), nb()
                            for i in range(4):
                                op("pe", lambda e, i=i, g=g, psp=psp: e.matmul(psp[:, i * 128:(i + 1) * 128], lhsT=vn[i][:, g * 128:(g + 1) * 128], rhs=wmT[:, g, :], start=True, stop=True),
                                   reads=vn[i].b + wmT.b, writes=psp.b)
                            mm_fm(pzc, wz, ci, 0, 128)
                            for i in range(4):
                                op("dve", lambda e, i=i, g=g, psp=psp: e.scalar_tensor_tensor(out=sp2[:, i * 128:(i + 1) * 128], in0=psp[:, i * 128:(i + 1) * 128], scalar=lng_t[:, g:g + 1], in1=Bc[:, g, :], op0=ALU.mult, op1=ALU.add),
                                   reads=psp.b + lng_t.b + Bc.b, writes=sp2.b)
                            op("act", lambda e, pzc=pzc: e.activation(out=szc[:, 0:512], in_=pzc[:, :], func=AF.Silu), reads=pzc.b, writes=szc.b)
                            op("dve", lambda e, ci=ci: e.tensor_tensor(out=tc_[:, 0:512], in0=ugs[ci][:], in1=szc[:, 0:512], op=ALU.mult), reads=ugs[ci].b + szc.b, writes=tc_.b)
                            op("dve", lambda e, g=g: e.tensor_tensor(out=yCT[:, g, :], in0=tc_[:, 0:512], in1=sp2[:, 0:512], op=ALU.mult), reads=tc_.b + sp2.b, writes=yCT.b)
                    for half in range(2):
                        w = loadw(CH_Q + 4 * half, 4)
                        q_ = qst[rr["cs"] % 3]
                        rr["cs"] += 1
                        qe = rr["cs"] % 2
                        for ci in range(4):
                            ps = nb()
                            mm_fm(ps, w, ci, 0, 128)
                            evac_copy(qe, q_[:, ci, :], ps[:, :], ps.b, q_.b)
                        qn = ("act", "pool")[qe]
                        dma(qn, s_qT[:, 8 * half:8 * half + 8:2, tsl], q_[0:64, :, :], reads=q_.b, writes=[SB["qT"]])
                        dma(qn, s_qT[:, 8 * half + 1:8 * half + 8:2, tsl], q_[64:128, :, :], reads=q_.b, writes=[SB["qT"]])
                    kcnt = 0
                    for (ch0, knm, vnm) in ((CH_KC, "kc", "vc"), (CH_KS, "ks", None), (CH_KW, "kw", None)):
                        w = loadw(ch0, 4)
                        for (c_off, nm) in ((0, knm), (2, vnm)):
                            if nm is None:
                                continue
                            q_ = qst[rr["cs"] % 3]
                            rr["cs"] += 1
                            kcnt += 1
                            qe = rr["cs"] % 2
                            for i in range(2):
                                ps = nb()
                                mm_fm(ps, w, c_off + i, 0, 128)
                                evac_copy(qe, q_[:, i, :], ps[:, :], ps.b, q_.b)
                            qn = ("act", "pool")[qe]
                            dma(qn, s_kT[nm][:, 0:4:2, tsl], q_[0:64, 0:2, :], reads=q_.b, writes=[SB[nm]])
                            dma(qn, s_kT[nm][:, 1:4:2, tsl], q_[64:128, 0:2, :], reads=q_.b, writes=[SB[nm]])
                        if vnm is None:
                            typ = 0 if knm == "ks" else 1
                            for i in range(4):
                                ps = nb()
                                mm_tm(ps, w, 2, 2, i)
                                evac_copy(typ, vst[:, i, typ, :].rearrange("p (g f) -> p g f", g=4)[:, :, 0:64], ps[:, 0:256].rearrange("p (g f) -> p g f", g=4), ps.b, vst.b)
                            nm = "vs" if typ == 0 else "vw"
                            dma(("act", "pool")[typ], s_vX[nm][blk * 4:(blk + 1) * 4].rearrange("i p f -> p i f"), vst[:, :, typ, :], reads=vst.b, writes=[SB[nm]])
                    w = loadw(CH_GL, 1)
                    for i in range(4):
                        ps = nb()
                        mm_tm(ps, w, 0, 1, i, ncol=48)
                        op("act", lambda e, i=i, ps=ps: e.activation(out=gst[:, i, :], in_=ps[:, 0:48], func=AF.Sigmoid), reads=ps.b, writes=gst.b)
                    dma("act", s_glg[blk * 4:(blk + 1) * 4].rearrange("i p f -> p i f"), gst[:], reads=gst.b, writes=[SB["glg"]])
                    m1, m2 = acc, ybuf
                    for dch in range(8):
                        pa, pc = nb(), nb()
                        for kc in range(KC):
                            op("pe", lambda e, kc=kc, dch=dch, pa=pa: e.matmul(pa[:, :], lhsT=wbr0[:, dch, kc * 128:(kc + 1) * 128], rhs=yAT[:, kc, :], start=(kc == 0), stop=(kc == KC - 1)), reads=wbr0.b + yAT.b, writes=pa.b)
                        for kc in range(KC):
                            op("pe", lambda e, kc=kc, dch=dch, pc=pc: e.matmul(pc[:, :], lhsT=wbr2[:, dch, kc * 128:(kc + 1) * 128], rhs=yCT[:, kc, :], start=(kc == 0), stop=(kc == KC - 1)), reads=wbr2.b + yCT.b, writes=pc.b)
                        op("dve", lambda e, dch=dch, pa=pa: e.tensor_tensor(out=m1[:, 0:512], in0=pa[:, :], in1=sgA[:, dch, :], op=ALU.mult), reads=pa.b + sgA.b, writes=m1.b)
                        op("dve", lambda e, dch=dch, pc=pc: e.tensor_tensor(out=m2[:, 0:512], in0=pc[:, :], in1=sgC[:, dch, :], op=ALU.mult), reads=pc.b + sgC.b, writes=m2.b)
                        cs = cst[rr["cs"] % 8]
                        rr["cs"] += 1
                        op("dve", lambda e, cs=cs: e.tensor_tensor(out=cs[:], in0=m1[:, 0:512], in1=m2[:, 0:512], op=ALU.add), reads=m1.b + m2.b, writes=cs.b)
                        dma("pool", s_mp[dch, :, tsl], cs[:], reads=cs.b, writes=[SB["mp"]])
                    if blk + 1 < NQB:
                        a1_trans(blk + 1)
                sc.barrier()


        SLOPES = [2.0 ** (-8.0 * (i + 1) / 16.0) for i in range(16)]
        SKIP_T = 35.0
        XW = 65 + NSEL
        NS2 = NSEL + 2

        def phaseB(l, b):
            with ExitStack() as st:
                kcmpT = sb(st, "kcmpT", [67, 4, NCT * 128], BF16)
                VCX = sb(st, "VCX", [128, NCT, 4, XW + 2], BF16)
                pbias = sb(st, "pbias", [128, 2], F32)
                op("dve", lambda e: e.memset(kcmpT[0:64], 0.0), writes=kcmpT.b)
                op("dve", lambda e: e.memset(kcmpT[64:67], 1.0), writes=kcmpT.b)
                op("dve", lambda e: e.memset(VCX[:], 0.0), writes=VCX.b)
                op("dve", lambda e: e.memset(VCX[:, :, :, 64:65], 1.0), writes=VCX.b)
                for g in range(4):
                    dma("pool", VCX[:, :, g, 65:XW], k_ov[:, :, :], writes=VCX.b)
                with ExitStack() as st0:
                    xcT = [sb(st0, "xcT%d" % kv, [64, 4, S], BF16) for kv in range(2)]
                    w1t = [sb(st0, "w1t%d" % kv, [64, 32 * 128], BF16) for kv in range(2)]
                    w2t = [sb(st0, "w2t%d" % kv, [128, 64], BF16) for kv in range(2)]
                    post = sb(st0, "post", [64, 2, 32], BF16)
                    hid = sb(st0, "hid", [128, 128], BF16)
                    for kv, nm in enumerate(("kc", "vc")):
                        dma("sp", xcT[kv][:], s_kT[nm][:, :, :], reads=[SB[nm]], writes=xcT[kv].b)
                        dma("pool", w1t[kv][:], w1[l, kv], writes=w1t[kv].b)
                        dma("pool", w2t[kv][:], w2[l, kv], writes=w2t[kv].b)
                        dma("pool", post[:, kv, :], posT[l, kv], writes=post.b)
                    op("dve", lambda e: e.memset(hid[:], 0.0), writes=hid.b)
                    for kv in range(2):
                        ps = nb()
                        for lq in range(32):
                            op("pe", lambda e, lq=lq, kv=kv, ps=ps: e.matmul(ps[:, 0:1], lhsT=w1t[kv][0:64, lq * 128:(lq + 1) * 128], rhs=post[0:64, kv, lq:lq + 1], start=(lq == 0), stop=(lq == 31)),
                               reads=w1t[kv].b + post.b, writes=ps.b)
                        op("dve", lambda e, kv=kv, ps=ps: e.tensor_copy(out=pbias[:, kv:kv + 1], in_=ps[:, 0:1]), reads=ps.b, writes=pbias.b)
                    for kv in range(2):
                        for g in range(4):
                            for ct in range(NCT):
                                n_i = min(128, NCMP - ct * 128)
                                ps = nb()
                                for lq in range(32):
                                    s0 = 16 * ct * 128 + lq
                                    op("pe", lambda e, lq=lq, kv=kv, g=g, ps=ps, s0=s0, n_i=n_i: e.matmul(ps[:, 0:n_i], lhsT=w1t[kv][0:64, lq * 128:(lq + 1) * 128], rhs=xcT[kv][0:64, g, s0:s0 + 16 * (n_i - 1) + 1:16], start=(lq == 0), stop=(lq == 31)),
                                       reads=w1t[kv].b + xcT[kv].b, writes=ps.b)
                                op("act", lambda e, kv=kv, ps=ps, n_i=n_i: e.activation(out=hid[:, 0:n_i], in_=ps[:, 0:n_i], func=AF.Silu, bias=pbias[:, kv:kv + 1]), reads=ps.b + pbias.b, writes=hid.b)
                                ps2 = nb()
                                if kv == 0:
                                    op("pe", lambda e, ps2=ps2, n_i=n_i: e.matmul(ps2[0:64, 0:n_i], lhsT=w2t[0][:, 0:64], rhs=hid[:, 0:n_i], start=True, stop=True), reads=w2t[0].b + hid.b, writes=ps2.b)
                                    op("dve", lambda e, ps2=ps2, n_i=n_i, g=g, ct=ct: e.tensor_copy(out=kcmpT[0:64, g, ct * 128:ct * 128 + n_i], in_=ps2[0:64, 0:n_i]), reads=ps2.b, writes=kcmpT.b)
                                else:
                                    op("pe", lambda e, ps2=ps2, n_i=n_i: e.matmul(ps2[:, 0:64], lhsT=hid[:, 0:128], rhs=w2t[1][:, 0:64], start=True, stop=True), reads=w2t[1].b + hid.b, writes=ps2.b)
                                    op("dve", lambda e, ps2=ps2, n_i=n_i, g=g, ct=ct: e.tensor_copy(out=VCX[0:n_i, ct, g, 0:64], in_=ps2[0:n_i, 0:64]), reads=ps2.b, writes=VCX.b)
                    sc.barrier()
                KTs = sb(st, "KTs", [67, 4, S], BF16)
                KTw = [sb(st, "KTw%d" % i, [67, 4, 1024], BF16) for i in range(2)]
                Vs = sb(st, "Vs", [128, NT, 260], BF16)
                Vw = [sb(st, "Vw%d" % i, [128, 8, 260], BF16) for i in range(2)]
                QT = [sb(st, "QT%d" % i, [67, 16, 512], BF16) for i in range(2)]
                glt = [sb(st, "glt%d" % i, [128, 4, 48], F32) for i in range(2)]
                OACC2 = [sb(st, "OACC%d" % i, [128, 4, D], F32) for i in range(2)]
                cur_o = {"o": OACC2[0], "qb": 0}
                PT = [sb(st, "PT%d" % i, [128, 512], BF16) for i in range(6)]
                cmt = sb(st, "cmt", [128, 13, 512], BF16)
                eselt = sb(st, "eselt", [128, NT * 128], BF16)
                faddq = [sb(st, "faddq%d" % i, [128, 4, NSEL], F32) for i in range(2)]
                albt = sb(st, "albt", [128, 16, NT], F32)
                albct = sb(st, "albct", [128, 16, 8], F32)
                impacc = sb(st, "impacc", [128, 4, 4, NSEL], F32)
                sm = sb(st, "sm", [128, 8], F32)
                scr_ = sb(st, "scr_", [128, NSEL], F32)
                scr2 = sb(st, "scr2", [128, NSEL], F32)
                m8 = sb(st, "m8", [128, 16], F32)
                mbs = sb(st, "mbs", [128, 16, NSEL], BF16)
                MBTs = [sb(st, "MBT%d" % i, [128, 512], BF16) for i in range(4)]
                dma("pool", cmt[:], k_cm[:, :, :], writes=cmt.b)
                op("dve", lambda e: e.memset(eselt[:], 0.0), writes=eselt.b)
                for i in range(4):
                    op("dve", lambda e, i=i: e.memset(MBTs[i][:], 0.0), writes=MBTs[i].b)
                dma("pool", eselt[0:NSEL], k_esel[:, :], writes=eselt.b)
                dma("sp", albt[:], k_alb[:, :, :], writes=albt.b)
                dma("sp", albct[:], k_albc[:, :, :], writes=albct.b)
                dma("sp", KTs[0:64], s_kT["ks"][:, :, :], reads=[SB["ks"]], writes=KTs.b)
                op("dve", lambda e: e.memset(KTs[64:67], 1.0), writes=KTs.b)
                dma("sp", Vs[:], s_vX["vs"].rearrange("i p f -> p i f"), reads=[SB["vs"]], writes=Vs.b)
                for i in range(2):
                    op("dve", lambda e, i=i: e.memset(KTw[i][64:67], 1.0), writes=KTw[i].b)
                    dma("pool", QT[i][64:67], k_aq[:, :, :], writes=QT[i].b)
                ctr = {"s": 0, "p": 0, "o": 0, "c": 0}
                SPB = [PS[0], PS[1], PS[4]]
                pipe = []
                LAG = 3

                def push(fn):
                    pipe.append(fn)
                    while len(pipe) > LAG:
                        pipe.pop(0)()

                def flush():
                    while pipe:
                        pipe.pop(0)()

                OSB = [sb(st, "osb%d" % i, [66, 512], F32) for i in range(5)]
                for i in range(5):
                    op("dve", lambda e, i=i: e.memset(OSB[i][:], 0.0), writes=OSB[i].b)
                ISB = [sb(st, "isb%d" % i, [NS2, 512], F32) for i in range(2)]
                for i in range(2):
                    op("dve", lambda e, i=i: e.memset(ISB[i][:], 0.0), writes=ISB[i].b)
                PTR = PS[6]
                PIMS = [PS[7], PS[5]]

                def attend(tiles, Q, h, g, gate_col, G_, first_branch, is_cmp):
                    ob = PS[2 + ctr["o"] % 2]
                    osb = OSB[ctr["o"] % 5]
                    ctr["o"] += 1
                    if is_cmp:
                        PIM = PIMS[ctr["c"] % 2]
                        isb = ISB[ctr["c"] % 2]
                        ctr["c"] += 1
                    nt = len(tiles)

                    def epilogue2():
                        for sub in range(4):
                            op("pe", lambda e, sub=sub: e.transpose(out=PTR[:, sub * 66:(sub + 1) * 66], in_=osb[0:66, sub * 128:(sub + 1) * 128], identity=ident_f[0:66, 0:66]), reads=osb.b + ident_f.b, writes=PTR.b)
                        if is_cmp:
                            for sub in range(4):
                                op("pe", lambda e, sub=sub: e.transpose(out=PIM[:, sub * NS2:(sub + 1) * NS2], in_=isb[0:NS2, sub * 128:(sub + 1) * 128], identity=ident_f[0:NS2, 0:NS2]), reads=isb.b + ident_f.b, writes=PIM.b)
                        op("dve", lambda e: e.tensor_scalar(out=sm[:, 0:4], in0=PTR[:, 0:264].rearrange("p (s f) -> p s f", s=4)[:, :, 64], scalar1=1e-30, scalar2=None, op0=ALU.add), reads=PTR.b, writes=sm.b)
                        op("dve", lambda e: e.reciprocal(out=sm[:, 0:4], in_=sm[:, 0:4]), reads=sm.b, writes=sm.b)
                        op("dve", lambda e: e.tensor_tensor(out=sm[:, 4:8], in0=sm[:, 0:4], in1=G_[:, :, gate_col], op=ALU.mult), reads=sm.b + G_.b, writes=sm.b)
                        for sub in range(4):
                            c0_ = sub * 66
                            OACC = cur_o["o"]
                            osl = OACC[:, sub, h * 64:(h + 1) * 64]
                            if first_branch:
                                op("dve", lambda e, osl=osl, c0_=c0_, sub=sub: e.tensor_scalar(out=osl, in0=PTR[:, c0_:c0_ + 64], scalar1=sm[:, 4 + sub:5 + sub], scalar2=None, op0=ALU.mult), reads=PTR.b + sm.b, writes=OACC.b)
                            else:
                                op("dve", lambda e, osl=osl, c0_=c0_, sub=sub: e.scalar_tensor_tensor(out=osl, in0=PTR[:, c0_:c0_ + 64], scalar=sm[:, 4 + sub:5 + sub], in1=osl, op0=ALU.mult, op1=ALU.add), reads=PTR.b + sm.b + OACC.b, writes=OACC.b)
                            if is_cmp:
                                i0 = sub * NS2
                                if h % 4 == 0:
                                    op("dve", lambda e, sub=sub, i0=i0: e.tensor_scalar(out=impacc[:, g, sub, :], in0=PIM[:, i0:i0 + NSEL], scalar1=sm[:, sub:sub + 1], scalar2=None, op0=ALU.mult), reads=PIM.b + sm.b, writes=impacc.b)
                                else:
                                    op("dve", lambda e, sub=sub, i0=i0: e.scalar_tensor_tensor(out=impacc[:, g, sub, :], in0=PIM[:, i0:i0 + NSEL], scalar=sm[:, sub:sub + 1], in1=impacc[:, g, sub, :], op0=ALU.mult, op1=ALU.add), reads=PIM.b + sm.b + impacc.b, writes=impacc.b)

                    tiles = sorted(tiles, key=lambda t_: 0 if t_[8] == (0, 512) else 1)
                    assert tiles[0][8] == (0, 512)
                    for ti, (k_ap, k_rd, extras, bias_ap, bias_rd, v_ap, ov_ap, v_rd, (lo, hi)) in enumerate(tiles):
                        sp = SPB[ctr["s"] % 3]
                        ctr["s"] += 1
                        op("pe", lambda e, sp=sp, k_ap=k_ap, lo=lo, hi=hi: e.matmul(sp[:, lo:hi], lhsT=k_ap, rhs=Q[0:67, h, lo:hi], start=True, stop=(len(extras) == 0)), reads=k_rd + Q.b, writes=sp.b)
                        for xi, (xl, xr, xrd) in enumerate(extras):
                            op("pe", lambda e, sp=sp, xl=xl, xr=xr, xi=xi, lo=lo, hi=hi: e.matmul(sp[:, lo:hi], lhsT=xl, rhs=xr[:, lo:hi], start=False, stop=(xi == len(extras) - 1)), reads=xrd, writes=sp.b)
                        pt = PT[ctr["p"] % 6]
                        ctr["p"] += 1
                        op("act", lambda e, sp=sp, pt=pt, bias_ap=bias_ap, lo=lo, hi=hi: e.activation(out=pt[:, lo:hi], in_=sp[:, lo:hi], func=AF.Exp, scale=0.125, bias=bias_ap), reads=sp.b + bias_rd, writes=pt.b)

                        def stage2(ti=ti, pt=pt, v_ap=v_ap, ov_ap=ov_ap, v_rd=v_rd, lo=lo, hi=hi):
                            op("pe", lambda e: e.matmul(ob[0:65, lo:hi], lhsT=v_ap, rhs=pt[:, lo:hi], start=(ti == 0), stop=(ti == nt - 1)), reads=pt.b + v_rd, writes=ob.b)
                            if is_cmp:
                                op("pe", lambda e: e.matmul(PIM[0:NS2, :], lhsT=ov_ap, rhs=pt[:], start=(ti == 0), stop=(ti == nt - 1)), reads=pt.b + v_rd, writes=PIM.b)
                            if ti == nt - 1:
                                if cur_o["qb"] <= 2 and ctr["o"] % 2 == 0:
                                    op("act", lambda e: e.copy(out=osb[0:65, :], in_=ob[0:65, :]), reads=ob.b, writes=osb.b)
                                else:
                                    op("dve", lambda e: e.tensor_copy(out=osb[0:65, :], in_=ob[0:65, :]), reads=ob.b, writes=osb.b)
                                if is_cmp:
                                    op("dve", lambda e: e.tensor_copy(out=isb[0:NSEL, :], in_=PIM[0:NSEL, :]), reads=PIM.b, writes=isb.b)
                                if is_cmp:
                                    epilogue2()
                                else:
                                    push(epilogue2)

                        push(stage2)

                for qb in range(NQB):
                    t0 = qb * 512
                    Q = QT[qb % 2]
                    cur_o["o"] = OACC2[qb % 2]
                    cur_o["qb"] = qb
                    OACC = OACC2[qb % 2]
                    Kw = KTw[qb % 2]
                    Vw_ = Vw[qb % 2]
                    G_ = glt[qb % 2]
                    dma("sp", Q[0:64], s_qT[:, :, t0:t0 + 512], reads=[SB["qT"]], writes=Q.b)
                    k0 = max(0, t0 - 512)
                    dma("sp", Kw[0:64, :, k0 - (t0 - 512):1024], s_kT["kw"][:, :, k0:t0 + 512], reads=[SB["kw"]], writes=Kw.b)
                    c0 = max(0, 4 * qb - 4)
                    dma("sp", Vw_[:, c0 - (4 * qb - 4):8, :], s_vX["vw"][c0:4 * qb + 4].rearrange("i p f -> p i f"), reads=[SB["vw"]], writes=Vw_.b)
                    dma("sp", G_[:], s_glg[4 * qb:4 * qb + 4].rearrange("i p f -> p i f"), reads=[SB["glg"]], writes=G_.b)
                    faddt = faddq[qb % 2]
                    dma("sp", faddt[:], k_fadd[:, 4 * qb:4 * qb + 4, :], writes=faddt.b)
                    for g in range(4):
                        for n in range(4):
                            h = 4 * g + n
                            tiles = []
                            for c in range(NCT):
                                m = qb - 4 * c
                                if m < 0:
                                    continue
                                extras = []
                                if m <= 4:
                                    extras.append((ident_b[:], cmt[:, 8 + m, :], ident_b.b + cmt.b))
                                tiles.append((kcmpT[0:67, g, c * 128:(c + 1) * 128], kcmpT.b, extras, albct[:, h, m:m + 1], albct.b, VCX[:, c, g, 0:65], VCX[:, c, g, 65:XW + 2], VCX.b, (0, 512)))
                            attend(tiles, Q, h, g, 3 * h, G_, True, True)
                    flush()
                    for g in range(4):
                        for n in range(4):
                            h = 4 * g + n
                            sub = n
                            mb_ = mbs[:, g * 4 + sub, :]
                            op("dve", lambda e, sub=sub, g=g: e.tensor_tensor(out=scr_[:], in0=impacc[:, g, sub, :], in1=faddt[:, sub, :], op=ALU.add), reads=impacc.b + faddt.b, writes=scr_.b)
                            op("dve", lambda e: e.max(out=m8[:, 0:8], in_=scr_[:]), reads=scr_.b, writes=m8.b)
                            op("dve", lambda e: e.match_replace(out=scr2[:], in_to_replace=m8[:, 0:8], in_values=scr_[:], imm_value=-1e30), reads=scr_.b + m8.b, writes=scr2.b)
                            op("dve", lambda e: e.max(out=m8[:, 8:16], in_=scr2[:]), reads=scr2.b, writes=m8.b)
                            op("dve", lambda e, mb_=mb_: e.tensor_scalar(out=mb_, in0=scr_[:], scalar1=m8[:, 15:16], scalar2=MNEG, op0=ALU.is_lt, op1=ALU.mult), reads=scr_.b + m8.b, writes=mbs.b)
                            tiles = []
                            for c in range(max(0, 4 * qb - 4), 4 * qb + 4):
                                o = c - 4 * qb
                                dmin = t0 - (128 * c + 127)
                                if SLOPES[h] * dmin > SKIP_T:
                                    continue
                                li = c - (4 * qb - 4)
                                extras = [(ident_b[:], cmt[:, 4 + o, :], ident_b.b + cmt.b)]
                                tiles.append((Kw[0:67, g, li * 128:(li + 1) * 128], Kw.b, extras, albt[:, h, o + NT - 4:o + NT - 3], albt.b, Vw_[:, li, g * 65:(g + 1) * 65], None, Vw_.b, ((128 * o, 512) if o >= 0 else (0, 128 * (o + 5)))))
                            attend(tiles, Q, h, g, 3 * h + 2, G_, False, False)
                    for g in range(4):
                        pm = PIMS[g % 2]
                        pmb = pm[:].bitcast(BF16)
                        for sub in range(4):
                            op("pe", lambda e, sub=sub, g=g, pmb=pmb: e.transpose(out=pmb[0:NSEL, sub * 128:(sub + 1) * 128], in_=mbs[:, g * 4 + sub, :], identity=ident_b[:]), reads=mbs.b + ident_b.b, writes=pm.b)
                        op("act", lambda e, pmb=pmb, g=g: e.copy(out=MBTs[g][0:NSEL, :], in_=pmb[0:NSEL, 0:512]), reads=pm.b, writes=MBTs[g].b)
                    for g in range(4):
                        MBT = MBTs[g]
                        for n in range(4):
                            h = 4 * g + n
                            tiles = []
                            for c in range(4 * qb + 4):
                                o = c - 4 * qb
                                dmin = t0 - (128 * c + 127)
                                if SLOPES[h] * dmin > SKIP_T:
                                    continue
                                extras = [(eselt[:, c * 128:(c + 1) * 128], MBT[:, :], eselt.b + MBT.b)]
                                if o >= 0:
                                    extras.append((ident_b[:], cmt[:, 4 + o, :], ident_b.b + cmt.b))
                                tiles.append((KTs[0:67, g, c * 128:(c + 1) * 128], KTs.b, extras, albt[:, h, o + NT - 4:o + NT - 3], albt.b, Vs[:, c, g * 65:(g + 1) * 65], None, Vs.b, ((128 * o, 512) if o >= 0 else (0, 512))))
                            attend(tiles, Q, h, g, 3 * h + 1, G_, False, False)
                    flush()
                    dma("pool", s_o[4 * qb:4 * qb + 4].rearrange("i p f -> p i f"), OACC[:], reads=OACC.b, writes=[SB["o"]])
                sc.barrier()

        def phaseC(l, b, xcur, xcur_b, xnext, xnext_b):
            with ExitStack() as st:
                wbr1 = sb(st, "wbr1", [128, 8, KC * 128], BF16)
                wout = sb(st, "wout", [128, KC * D], BF16)
                ot2 = [[sb(st, "ot%d_%d" % (k, i), [128, D], F32) for i in range(4)] for k in range(2)]
                szBt2 = [sb(st, "szBt%d" % k, [128, 8, 512], BF16) for k in range(2)]
                gBt2 = [sb(st, "gBt%d" % k, [128, 8, 512], BF16) for k in range(2)]
                mpt2 = [sb(st, "mpt%d" % k, [128, 8, 512], BF16) for k in range(2)]

                def c_loads(blk):
                    tsl = slice(blk * 512, blk * 512 + 512)
                    k = blk % 2
                    for i in range(4):
                        dma("sp", ot2[k][i][:], s_o[4 * blk + i], reads=[SB["o"]], writes=ot2[k][i].b)
                    dma("sp", szBt2[k][:], s_szB[:, :, tsl].rearrange("j p t -> p j t"), reads=[SB["szB"]], writes=szBt2[k].b)
                    dma("sp", gBt2[k][:], s_gB[:, :, tsl].rearrange("j p t -> p j t"), reads=[SB["gB"]], writes=gBt2[k].b)
                    dma("sp", mpt2[k][:], s_mp[:, :, tsl].rearrange("j p t -> p j t"), reads=[SB["mp"]], writes=mpt2[k].b)
                yBT = sb(st, "yBT", [128, 8, 512], BF16)
                mT = sb(st, "mT", [128, 8, 512], BF16)
                m1 = sb(st, "m1c", [128, 512], F32)
                xt = [sb(st, "xtc%d" % i, [128, D], F32) for i in range(3)]
                res = [sb(st, "res%d" % i, [128, D], F32) for i in range(3)]
                junk = sb(st, "junkc", [128, 512], BF16)
                stc = sb(st, "stc", [128, 4], F32)
                dma("sp", wbr1[:], wbrB[l, 8:16].rearrange("c p f -> p c f"), reads=[SB["wbrB"]], writes=wbr1.b)
                dma("sp", wout[:], woutB[l], reads=[SB["woutB"]], writes=wout.b)
                for blk in range(NQB):
                    t0 = blk * 512
                    tsl = slice(t0, t0 + 512)
                    if blk == 0:
                        c_loads(0)
                    if blk + 1 < NQB:
                        c_loads(blk + 1)
                    ot, szBt, gBt, mpt = ot2[blk % 2], szBt2[blk % 2], gBt2[blk % 2], mpt2[blk % 2]
                    for j in range(8):
                        ps = nb()
                        for i in range(4):
                            op("pe", lambda e, i=i, j=j, ps=ps: e.transpose(out=ps[:, i * 128:(i + 1) * 128], in_=ot[i][:, j * 128:(j + 1) * 128], identity=ident_f[:]), reads=ot[i].b + ident_f.b, writes=ps.b)
                        op("dve", lambda e, j=j, ps=ps: e.tensor_tensor(out=yBT[:, j, :], in0=ps[:, :], in1=szBt[:, j, :], op=ALU.mult), reads=ps.b + szBt.b, writes=yBT.b)
                    for dch in range(8):
                        ps = nb()
                        for kc in range(KC):
                            op("pe", lambda e, kc=kc, dch=dch, ps=ps: e.matmul(ps[:, :], lhsT=wbr1[:, dch, kc * 128:(kc + 1) * 128], rhs=yBT[:, kc, :], start=(kc == 0), stop=(kc == KC - 1)), reads=wbr1.b + yBT.b, writes=ps.b)
                        op("dve", lambda e, dch=dch, ps=ps: e.tensor_tensor(out=m1[:], in0=ps[:, :], in1=gBt[:, dch, :], op=ALU.mult), reads=ps.b + gBt.b, writes=m1.b)
                        op("pool", lambda e, dch=dch: e.tensor_tensor(out=mT[:, dch, :], in0=m1[:], in1=mpt[:, dch, :], op=ALU.add), reads=m1.b + mpt.b, writes=mT.b)
                    for i in range(4):
                        x_ = xt[(4 * blk + i) % 3]
                        r_ = res[(4 * blk + i) % 3]
                        rows = slice(t0 + i * 128, t0 + (i + 1) * 128)
                        dma("sp", x_[:], xcur[b, rows, :], reads=xcur_b, writes=x_.b)
                        pp = [nb(), nb()]
                        for n_ in range(2):
                            for kc in range(KC):
                                op("pe", lambda e, kc=kc, n_=n_, i=i, p_=pp[n_]: e.matmul(p_[:, :], lhsT=mT[:, kc, i * 128:(i + 1) * 128], rhs=wout[:, kc * D + n_ * 512:kc * D + (n_ + 1) * 512], start=(kc == 0), stop=(kc == KC - 1)),
                                   reads=mT.b + wout.b, writes=pp[n_].b)
                            op("act", lambda e, n_=n_, p_=pp[n_]: e.activation(out=junk[:], in_=p_[:, :], func=AF.Square, accum_out=stc[:, n_:n_ + 1]), reads=pp[n_].b, writes=junk.b + stc.b)
                        op("dve", lambda e: e.tensor_tensor(out=stc[:, 2:3], in0=stc[:, 0:1], in1=stc[:, 1:2], op=ALU.add), reads=stc.b, writes=stc.b)
                        op("dve", lambda e: e.tensor_scalar(out=stc[:, 2:3], in0=stc[:, 2:3], scalar1=1.0 / D, scalar2=EPS, op0=ALU.mult, op1=ALU.add), reads=stc.b, writes=stc.b)
                        op("act", lambda e: e.activation(out=stc[:, 2:3], in_=stc[:, 2:3], func=AF.Sqrt), reads=stc.b, writes=stc.b)
                        op("dve", lambda e: e.reciprocal(out=stc[:, 3:4], in_=stc[:, 2:3]), reads=stc.b, writes=stc.b)
                        for n_ in range(2):
                            sl = slice(n_ * 512, (n_ + 1) * 512)
                            op("dve", lambda e, n_=n_, sl=sl, p_=pp[n_], r_=r_: e.scalar_tensor_tensor(out=r_[:, sl], in0=p_[:, :], scalar=stc[:, 3:4], in1=GP[:, b, sl], op0=ALU.mult, op1=ALU.mult), reads=pp[n_].b + stc.b + GP.b, writes=r_.b)
                        op("pool", lambda e, r_=r_, x_=x_: e.tensor_tensor(out=r_[:], in0=r_[:], in1=x_[:], op=ALU.add), reads=r_.b + x_.b, writes=r_.b)
                        dma("pool", xnext[b, rows, :], r_[:], reads=r_.b, writes=xnext_b)
                sc.barrier()

        for l in range(DEPTH):
            xcur = x_in if l == 0 else xmid
            xnext = y_out if l == DEPTH - 1 else xmid
            xcur_b = [] if l == 0 else [SB["xmid"]]
            xnext_b = [SB["y"]] if l == DEPTH - 1 else [SB["xmid"]]
            with ExitStack() as st:
                sct = sb(st, "sct", [128, KC, NB], F32)
                scb = sb(st, "scb", [128, NB, KC, 128], F32)
                ones = sb(st, "ones", [128, 128], F32)
                gpre_t = sb(st, "gpre_t", [128, KC], F32)
                badaf_t = sb(st, "badaf_t", [128, 16], F32)
                ss_t = sb(st, "ss_t", [128, 16, NB], F32)
                gpost_bc = sb(st, "gpost_bc", [128, D], F32)
                badag_bc = sb(st, "badag_bc", [128, D], F32)
                wa = [sb(st, "wa%d" % i, [128, 4, KC * 128], F32) for i in range(2)]
                dma("sp", sct[:], cT[:, :, :], writes=sct.b)
                dma("sp", gpre_t[:], gpre[l], writes=gpre_t.b)
                dma("sp", badaf_t[:], badaf[l], writes=badaf_t.b)
                dma("sp", gpost_bc[:], gpost[l].broadcast_to([128, D]), writes=gpost_bc.b)
                dma("sp", badag_bc[:], badag[l].broadcast_to([128, D]), writes=badag_bc.b)
                op("act", lambda e: e.activation(out=sct[:], in_=sct[:], func=AF.Silu), reads=sct.b, writes=sct.b)
                op("dve", lambda e: e.memset(ones[:], 1.0), writes=ones.b)
                for b in range(NB):
                    for kc in range(KC):
                        op("dve", lambda e, b=b, kc=kc: e.tensor_scalar(out=scb[:, b, kc, :], in0=ones[:], scalar1=sct[:, kc, b:b + 1], scalar2=None, op0=ALU.mult),
                           reads=ones.b + sct.b, writes=scb.b)
                for grp in range(6):
                    w = wa[grp % 2]
                    dma("sp", w[:], wadaT[l, grp * 4:(grp + 1) * 4].rearrange("c p f -> p c f"), writes=w.b)
                    if grp < 4:
                        ps = PS[grp % 2]
                        for ci in range(4):
                            for kc in range(KC):
                                op("pe", lambda e, w=w, ci=ci, kc=kc, ps=ps: e.matmul(ps[:, ci * NB:(ci + 1) * NB], lhsT=w[:, ci, kc * 128:(kc + 1) * 128], rhs=sct[:, kc, :], start=(kc == 0), stop=(kc == KC - 1)),
                                   reads=w.b + sct.b, writes=ps.b)
                        for ci in range(4):
                            f = grp * 4 + ci
                            op("dve", lambda e, ps=ps, ci=ci, f=f: e.tensor_scalar(out=ss_t[:, f, :], in0=ps[:, ci * NB:(ci + 1) * NB], scalar1=badaf_t[:, f:f + 1], scalar2=None, op0=ALU.add),
                               reads=ps.b + badaf_t.b, writes=ss_t.b)
                    else:
                        half = grp - 4
                        for b in range(NB):
                            ps = PS[2 + b]
                            for kc in range(KC):
                                op("pe", lambda e, w=w, kc=kc, ps=ps, b=b: e.matmul(ps[:, :], lhsT=scb[:, b, kc, :], rhs=w[:, :, kc * 128:(kc + 1) * 128], start=(kc == 0), stop=(kc == KC - 1)),
                                   reads=w.b + scb.b, writes=ps.b)
                            sl = slice(half * 512, (half + 1) * 512)
                            op("dve", lambda e, ps=ps, b=b, sl=sl: e.tensor_tensor(out=GP[:, b, sl], in0=ps[:, :], in1=badag_bc[:, sl], op=ALU.add), reads=ps.b + badag_bc.b, writes=GP.b)
                            op("dve", lambda e, b=b, sl=sl: e.tensor_tensor(out=GP[:, b, sl], in0=GP[:, b, sl], in1=gpost_bc[:, sl], op=ALU.mult), reads=GP.b + gpost_bc.b, writes=GP.b)
                for b in range(NB):
                    op("dve", lambda e, b=b: e.tensor_copy(out=A_shift[:, :, b], in_=ss_t[:, 0:8, b]), reads=ss_t.b, writes=A_shift.b)
                    op("dve", lambda e, b=b: e.scalar_tensor_tensor(out=A_scale[:, :, b], in0=ss_t[:, 8:16, b], scalar=1.0, in1=gpre_t[:, :], op0=ALU.add, op1=ALU.mult),
                       reads=ss_t.b + gpre_t.b, writes=A_scale.b)
                sc.barrier()

            for b in range(NB):
                phaseA(l, b, xcur, xcur_b)
                if "stopA" in dbg:
                    continue
                phaseB(l, b)
                if "stopB" in dbg:
                    continue
                phaseC(l, b, xcur, xcur_b, xnext, xnext_b)
        sc.finish()
    return sc


def _chunked(W):
    N = W.shape[1]
    return np.ascontiguousarray(W.reshape(8, 128, N // 128, 128).transpose(2, 1, 0, 3)).reshape(N // 128, 128, 1024)


def prep_shared(inp, S, DEPTH):
    f = np.float32
    g = {k: np.asarray(v, dtype=f) for k, v in inp.items() if k not in ("x", "c")}
    out = {}
    win = []
    for l in range(DEPTH):
        W = g["w_in"][l]
        cols = []
        for j in range(8):
            for kind in range(4):
                cols.append(W[:, kind * 1024 + j * 128: kind * 1024 + (j + 1) * 128])
        cols.append(W[:, 4096:7680])
        pad = np.zeros((1024, 128), f)
        pad[:, :48] = W[:, 7680:7728]
        cols.append(pad)
        cols.append(W[:, 7728:])
        Wp = np.concatenate(cols, axis=1)
        assert Wp.shape[1] == NCH * 128
        win.append(_chunked(Wp))
    out["winT"] = np.stack(win)
    out["wadaT"] = np.stack([_chunked(g["w_ada"][l]) for l in range(DEPTH)])
    out["wbrT"] = np.stack([np.concatenate([_chunked(g["w_br"][l, i]) for i in range(3)], 0) for l in range(DEPTH)])
    out["woutT"] = np.stack([np.ascontiguousarray(g["w_out"][l].reshape(8, 128, 1024).transpose(1, 0, 2)).reshape(128, 8192) for l in range(DEPTH)])
    out["gpre"] = np.ascontiguousarray(g["g_pre"][:DEPTH].reshape(DEPTH, 8, 128).transpose(0, 2, 1))
    out["gpost"] = g["g_post"][:DEPTH].reshape(DEPTH, 1, D)
    out["badag"] = np.ascontiguousarray(g["b_ada"][:DEPTH, 2048:]).reshape(DEPTH, 1, D)
    out["badaf"] = np.ascontiguousarray(g["b_ada"][:DEPTH, :2048].reshape(DEPTH, 16, 128).transpose(0, 2, 1))
    out["cw"] = np.ascontiguousarray(g["conv_w"][:DEPTH].reshape(DEPTH, 3, 8, 128).transpose(0, 3, 2, 1))
    out["cb"] = np.ascontiguousarray(g["conv_b"][:DEPTH].reshape(DEPTH, 8, 128).transpose(0, 2, 1))
    out["posT"] = np.ascontiguousarray(np.stack([g["pos_ck"][:DEPTH], g["pos_cv"][:DEPTH]], 1).transpose(0, 1, 3, 2))
    w1 = np.stack([g["w_ck1"][:DEPTH], g["w_cv1"][:DEPTH]], 1)
    out["w1"] = np.ascontiguousarray(w1.reshape(DEPTH, 2, 32, 64, 128).transpose(0, 1, 3, 2, 4)).reshape(DEPTH, 2, 64, 4096)
    out["w2"] = np.ascontiguousarray(np.stack([g["w_ck2"][:DEPTH], g["w_cv2"][:DEPTH]], 1))
    out["lng"] = np.ascontiguousarray(g["ln_g"][:DEPTH].reshape(DEPTH, 8, 128).transpose(0, 2, 1))
    out["lnb"] = g["ln_b"][:DEPTH].reshape(DEPTH, 1, D)
    out["wsT"] = np.ascontiguousarray(g["w_s"][:DEPTH].transpose(0, 3, 1, 2))
    out["bs"] = g["b_s"][:DEPTH].reshape(DEPTH, 1, D)
    for k, v in host_consts(S).items():
        out["k_" + k] = v
    return out


def prep_core(x, c, b0, NB):
    xs = np.ascontiguousarray(np.asarray(x[b0:b0 + NB], dtype=np.float32))
    cs = np.asarray(c[b0:b0 + NB], dtype=np.float32)
    cT = np.ascontiguousarray(cs.reshape(NB, 8, 128).transpose(2, 1, 0))
    return {"x": xs, "cT": cT}


_CACHE = {}


def run(inputs, S, NB, DEPTH, ncores, dbg=None):
    key = (S, NB, DEPTH, tuple(sorted(dbg)) if dbg else None)
    nc = bass.Bass("TRN2", target_bir_lowering=False)
    build(nc, S, NB, DEPTH, dbg)
    shared = prep_shared(inputs, S, DEPTH)
    in_maps = []
    for core in range(ncores):
        m = dict(shared)
        m.update(prep_core(inputs["x"], inputs["c"], core * NB, NB))
        in_maps.append(m)
    res = run_bass_kernel_spmd(nc, in_maps, core_ids=list(range(ncores)))
    return res


def kernel(**inputs):
    S, NB, DEPTH, NCORES = 4096, 2, 2, 8
    res = run(inputs, S, NB, DEPTH, NCORES)
    return np.concatenate([np.asarray(r["y"]) for r in res.results], axis=0).astype(np.float32)
```

```python
from contextlib import ExitStack
import numpy as np
import concourse.bass as bass
import concourse.mybir as mybir
from concourse.bass_utils import run_bass_kernel_spmd

F32 = mybir.dt.float32
BF16 = mybir.dt.bfloat16
AF = mybir.ActivationFunctionType
ALU = mybir.AluOpType

D = 1024
KC = 8
NCH = 109
EPS = 1e-6
MNEG = -32768.0
CH_B, CH_CG, CH_XIN, CH_ZA = 0, 8, 16, 24
CH_Q, CH_KC, CH_VC, CH_KS, CH_VS, CH_KW, CH_VW, CH_ZB, CH_GL = 32, 40, 42, 44, 46, 48, 50, 52, 60
CH_U, CH_V, CH_ZC, CH_GA, CH_GB, CH_GC = 61, 69, 77, 85, 93, 101


class Buf:
    __slots__ = ("w", "r")

    def __init__(self):
        self.w = {}
        self.r = {}


class Sched:
    def __init__(self, nc, ndma=8):
        self.nc = nc
        self.engs = {"pe": nc.tensor, "act": nc.scalar, "dve": nc.vector, "pool": nc.gpsimd, "sp": nc.sync}
        self.sems = {}
        self.cnt = {}
        for e in ("pe", "act", "dve", "pool"):
            self.sems[e] = nc.alloc_semaphore("c_" + e)
            self.cnt[e] = 0
        self.dq = {}
        for q in ("sp", "pool", "act"):
            keys = []
            for i in range(ndma):
                k = "d_%s%d" % (q, i)
                self.sems[k] = nc.alloc_semaphore(k)
                keys.append(k)
            self.dq[q] = [keys, 0]
        self.known = {e: {} for e in self.engs}
        self.last = {}
        self.ninst = 0

    def _wait(self, eng, deps):
        kn = self.known[eng]
        for k, v in deps.items():
            if kn.get(k, 0) >= v:
                continue
            self.engs[eng].wait_ge(self.sems[k], v)
            kn[k] = v
            self.ninst += 1

    def _deps(self, reads, writes, skip):
        deps = {}
        for b in reads:
            for k, v in b.w.items():
                if k != skip and deps.get(k, 0) < v:
                    deps[k] = v
        for b in writes:
            for dd in (b.w, b.r):
                for k, v in dd.items():
                    if k != skip and deps.get(k, 0) < v:
                        deps[k] = v
        return deps

    def _commit(self, k, v, reads, writes):
        self.last[k] = v
        for b in reads:
            if b.r.get(k, 0) < v:
                b.r[k] = v
        for b in writes:
            if b.w.get(k, 0) < v:
                b.w[k] = v

    def op(self, eng, fn, reads=(), writes=()):
        deps = self._deps(reads, writes, "pe" if eng == "pe" else None)
        self._wait(eng, deps)
        inst = fn(self.engs[eng])
        self.cnt[eng] += 1
        inst.then_inc(self.sems[eng], 1)
        self.ninst += 1
        self._commit(eng, self.cnt[eng], reads, writes)

    def dma(self, q, out, in_, reads=(), writes=()):
        keys, i = self.dq[q]
        self.dq[q][1] = i + 1
        k = keys[i % len(keys)]
        v = 16 * (i // len(keys) + 1)
        deps = self._deps(reads, writes, None)
        if v > 16 and deps.get(k, 0) < v - 16:
            deps[k] = v - 16
        self._wait(q, deps)
        self.engs[q].dma_start(out=out, in_=in_).then_inc(self.sems[k], 16)
        self.ninst += 1
        self._commit(k, v, reads, writes)

    def barrier(self):
        for e in self.engs:
            self._wait(e, dict(self.last))

    def finish(self):
        self._wait("sp", dict(self.last))


class TT:
    def __init__(self, t, nparts=1):
        self.t = t
        self.b = [Buf() for _ in range(nparts)]

    def __getitem__(self, idx):
        return self.t[idx]


def host_consts(S):
    NT = S // 128
    NSEL = S // 64
    NCMP = (S - 32) // 16 + 1
    NCT = (NCMP + 127) // 128
    NQB = S // 512
    c = {}
    c["ident"] = np.eye(128, dtype=np.float32)
    p = np.arange(128)[:, None]
    tt = np.arange(512)[None, :]
    cm = np.zeros((13, 128, 512), np.float32)
    for o in range(-4, 4):
        kk = 128 * o + p
        valid = (tt < kk + 512) if o < 0 else (tt >= kk)
        cm[o + 4] = np.where(valid, 0.0, MNEG)
    for m in range(5):
        cm[8 + m] = np.where(tt + 512 * m >= 16 * p + 31, 0.0, MNEG)
    c["cm"] = np.ascontiguousarray(cm.transpose(1, 0, 2))
    i = np.arange(NCT * 128)[:, None]
    j = np.arange(NSEL)[None, :]
    ov = ((16 * i <= 64 * j + 63) & (16 * i + 31 >= 64 * j) & (i < NCMP)).astype(np.float32)
    c["ov"] = np.ascontiguousarray(ov.reshape(NCT, 128, NSEL).transpose(1, 0, 2))
    kp = np.arange(NT * 128)[None, :]
    c["esel"] = (np.arange(NSEL)[:, None] == kp // 64).astype(np.float32)
    t = np.arange(S)[:, None]
    cur = t // 64
    forced = (j == 0) | (j == cur) | (j == cur - 1)
    fadd = np.where(forced, 1e4, np.where(j * 64 <= t, 0.0, -1e4)).astype(np.float32)
    c["fadd"] = np.ascontiguousarray(fadd.reshape(NT, 128, NSEL).transpose(1, 0, 2))
    slopes = 2.0 ** (-8.0 * np.arange(1, 17) / 16.0)
    import ml_dtypes
    v = (-8.0 * slopes[:, None] * np.arange(512)[None, :]).astype(np.float64)
    rows = []
    rem = v.copy()
    for _ in range(3):
        hi = rem.astype(np.float32).astype(ml_dtypes.bfloat16).astype(np.float64)
        rows.append(hi)
        rem = rem - hi
    c["aq"] = np.stack(rows, 0).astype(np.float32)
    o = np.arange(NT) - (NT - 4)
    c["alb"] = (slopes[None, :, None] * (128.0 * o[None, None, :] + np.arange(128)[:, None, None])).astype(np.float32)
    m = np.arange(8)
    c["albc"] = (slopes[None, :, None] * (16.0 * np.arange(128)[:, None, None] + 31.0 - 512.0 * m[None, None, :])).astype(np.float32)
    c["tri"] = (np.arange(128)[None, :] >= np.arange(128)[:, None]).astype(np.float32)
    return c


def build(nc, S, NB, DEPTH, dbg=None):
    NT = S // 128
    NSEL = S // 64
    NCMP = (S - 32) // 16 + 1
    NCT = (NCMP + 127) // 128
    NQB = S // 512
    dbg = dbg or set()

    def din(name, shape, dt=F32):
        return nc.dram_tensor(name, list(shape), dt, kind="ExternalInput").ap()

    def dscr(name, shape, dt):
        return nc.dram_tensor(name, list(shape), dt, kind="ExternalOutput" if name in dbg else "Internal").ap()

    x_in = din("x", [NB, S, D])
    cT = din("cT", [128, KC, NB])
    gpre = din("gpre", [DEPTH, 128, KC])
    gpost = din("gpost", [DEPTH, 1, D])
    badag = din("badag", [DEPTH, 1, D])
    badaf = din("badaf", [DEPTH, 128, 16])
    wadaT = din("wadaT", [DEPTH, 24, 128, KC * 128])
    winT = din("winT", [DEPTH, NCH, 128, KC * 128])
    wbrT = din("wbrT", [DEPTH, 24, 128, KC * 128])
    woutT = din("woutT", [DEPTH, 128, KC * D])
    cw = din("cw", [DEPTH, 128, 8, 3])
    cb = din("cb", [DEPTH, 128, 8])
    posT = din("posT", [DEPTH, 2, 64, 32])
    w1 = din("w1", [DEPTH, 2, 64, 32 * 128])
    w2 = din("w2", [DEPTH, 2, 128, 64])
    lng = din("lng", [DEPTH, 128, 8])
    lnb = din("lnb", [DEPTH, 1, D])
    wsT = din("wsT", [DEPTH, 128, 8, 128])
    bs = din("bs", [DEPTH, 1, D])
    k_ident = din("k_ident", [128, 128])
    k_cm = din("k_cm", [128, 13, 512])
    k_ov = din("k_ov", [128, NCT, NSEL])
    k_esel = din("k_esel", [NSEL, NT * 128])
    k_fadd = din("k_fadd", [128, NT, NSEL])
    k_aq = din("k_aq", [3, 16, 512])
    k_alb = din("k_alb", [128, 16, NT])
    k_albc = din("k_albc", [128, 16, 8])
    k_tri = din("k_tri", [128, 128])
    y_out = nc.dram_tensor("y", [NB, S, D], F32, kind="ExternalOutput").ap()

    winB = dscr("winB", [DEPTH, NCH, 128, KC * 128], BF16)
    wbrB = dscr("wbrB", [DEPTH, 24, 128, KC * 128], BF16)
    woutB = dscr("woutB", [DEPTH, 128, KC * D], BF16)
    s_qT = dscr("s_qT", [64, 16, S], BF16)
    s_kT = {nm: dscr("s_" + nm, [64, 4, S], BF16) for nm in ("kc", "vc", "ks", "kw")}
    s_vX = {nm: dscr("s_" + nm, [NT, 128, 260], BF16) for nm in ("vs", "vw")}
    s_szB = dscr("s_szB", [8, 128, S], BF16)
    s_gB = dscr("s_gB", [8, 128, S], BF16)
    s_glg = dscr("s_glg", [NT, 128, 48], F32)
    s_mp = dscr("s_mp", [8, 128, S], BF16)
    s_o = dscr("s_o", [NT, 128, D], F32)
    xmid = dscr("xmid", [NB, S, D], F32)
    SB = {k: Buf() for k in ("winB", "wbrB", "woutB", "qT", "kc", "vc", "ks", "kw", "vs", "vw", "szB", "gB", "glg", "mp", "o", "xmid", "y")}

    sc = Sched(nc)
    op, dma = sc.op, sc.dma

    with ExitStack() as top:
        uniq = [0]

        def sb(st, name, shape, dt, nparts=1):
            uniq[0] += 1
            return TT(st.enter_context(nc.sbuf_tensor("%s_%d" % (name, uniq[0]), list(shape), dt)), nparts)

        PS = [TT(top.enter_context(nc.psum_tensor("ps%d" % i, [128, 512], F32))) for i in range(8)]
        ident_f = sb(top, "ident_f", [128, 128], F32)
        ident_b = sb(top, "ident_b", [128, 128], BF16)
        dma("sp", ident_f[:], k_ident[:, :], writes=ident_f.b)
        dma("pool", ident_b[:], k_ident[:, :], writes=ident_b.b)
        A_scale = sb(top, "A_scale", [128, KC, NB], F32)
        A_shift = sb(top, "A_shift", [128, KC, NB], F32)
        GP = sb(top, "GP", [128, NB, D], F32)

        with ExitStack() as st:
            G = 4
            stg = [sb(st, "cv_f%d" % i, [128, G, 1024], F32) for i in range(2)]
            stb = [sb(st, "cv_b%d" % i, [128, G, 1024], BF16) for i in range(2)]
            jobs = []
            for l in range(DEPTH):
                for c0 in range(0, NCH, G):
                    n = min(G, NCH - c0)
                    jobs.append((winT[l, c0:c0 + n], winB[l, c0:c0 + n], n, SB["winB"]))
                for c0 in range(0, 24, G):
                    jobs.append((wbrT[l, c0:c0 + G], wbrB[l, c0:c0 + G], G, SB["wbrB"]))
                for c0 in range(0, 8, G):
                    jobs.append((woutT[l, :, c0 * 1024:(c0 + G) * 1024], woutB[l, :, c0 * 1024:(c0 + G) * 1024], -G, SB["woutB"]))
            for ji, (src, dst, n, bf) in enumerate(jobs):
                f, b_ = stg[ji % 2], stb[ji % 2]
                if n > 0:
                    dma("sp", f[:, 0:n, :], src.rearrange("c p f -> p c f"), writes=f.b)
                else:
                    n = -n
                    dma("sp", f[:, 0:n, :], src.rearrange("p (c f) -> p c f", c=n), writes=f.b)
                    dst = dst.rearrange("p (c f) -> c p f", c=n)
                eng = ("dve", "act", "pool")[ji % 3]
                if eng == "act":
                    op("act", lambda e, f=f, b_=b_, n=n: e.copy(out=b_[:, 0:n, :], in_=f[:, 0:n, :]), reads=f.b, writes=b_.b)
                else:
                    op(eng, lambda e, f=f, b_=b_, n=n: e.tensor_copy(out=b_[:, 0:n, :], in_=f[:, 0:n, :]), reads=f.b, writes=b_.b)
                dma("pool", dst.rearrange("c p f -> p c f"), b_[:, 0:n, :], reads=b_.b, writes=[bf])
            sc.barrier()

        rr = {"ps": 0, "wt": 0, "cs": 0}

        def nb():
            rr["ps"] = (rr["ps"] + 1) % 8
            return PS[rr["ps"]]

        def evac_copy(i, out, in_, reads, writes):
            if i % 2 == 0:
                op("act", lambda e: e.copy(out=out, in_=in_), reads=reads, writes=writes)
            else:
                op("dve", lambda e: e.tensor_copy(out=out, in_=in_), reads=reads, writes=writes)

        def phaseA(l, b, xcur, xcur_b):
            with ExitStack() as st:
                hT = sb(st, "hT", [128, KC, 512], BF16)
                xt = [sb(st, "xt%d" % i, [128, D], F32) for i in range(4)]
                junk = sb(st, "junk", [128, D], BF16)
                stat = sb(st, "stat", [128, 8], F32)
                wt = [sb(st, "wt%d" % i, [128, 4, KC * 128], BF16) for i in range(3)]
                wbr0 = sb(st, "wbr0", [128, 8, KC * 128], BF16)
                wbr2 = sb(st, "wbr2", [128, 8, KC * 128], BF16)
                yAT = sb(st, "yAT", [128, 8, 512], BF16)
                yCT = sb(st, "yCT", [128, 8, 512], BF16)
                sgA = sb(st, "sgA", [128, 8, 512], BF16)
                sgC = sb(st, "sgC", [128, 8, 512], BF16)
                carry = sb(st, "carry", [128, 8, 2], F32)
                cw_t = sb(st, "cw_t", [128, 8, 3], F32)
                cb_t = sb(st, "cb_t", [128, 8], F32)
                lng_t = sb(st, "lng_t", [128, 8], F32)
                wmT = sb(st, "wmT", [128, 8, 128], BF16)
                Bc = sb(st, "Bc", [128, 8, 128], F32)
                vn = [sb(st, "vn%d" % i, [128, D], BF16) for i in range(4)]
                gv = sb(st, "gv", [128, D], F32)
                bst = sb(st, "bst", [128, 16], F32)
                tmp = [sb(st, "tmpA%d" % i, [128, 516], F32) for i in range(6)]
                qst = [sb(st, "qst%d" % i, [128, 4, 512], BF16) for i in range(3)]
                vst = sb(st, "vst", [128, 4, 2, 260], BF16)
                gst = sb(st, "gst", [128, 4, 48], F32)
                cst = [sb(st, "cst%d" % i, [128, 512], BF16) for i in range(8)]
                ugs = [sb(st, "ugs%d" % i, [128, 512], BF16) for i in range(4)]
                dma("sp", wbr0[:], wbrB[l, 0:8].rearrange("c p f -> p c f"), reads=[SB["wbrB"]], writes=wbr0.b)
                dma("sp", wbr2[:], wbrB[l, 16:24].rearrange("c p f -> p c f"), reads=[SB["wbrB"]], writes=wbr2.b)
                dma("sp", cw_t[:], cw[l], writes=cw_t.b)
                dma("sp", cb_t[:], cb[l], writes=cb_t.b)
                dma("sp", lng_t[:], lng[l], writes=lng_t.b)
                op("dve", lambda e: e.memset(vst[:], 1.0), writes=vst.b)
                op("dve", lambda e: e.memset(carry[:], 0.0), writes=carry.b)
                wsf, trif, lnbf, bsf = xt[0], xt[1], xt[2], xt[3]
                lnbb = junk
                dma("sp", wsf[:, 0:1024].rearrange("p (g i) -> p g i", g=8), wsT[l], writes=wsf.b)
                dma("sp", trif[:, 0:128], k_tri[:, :], writes=trif.b)
                dma("sp", lnbf[:], lnb[l].broadcast_to([128, D]), writes=lnbf.b)
                dma("sp", bsf[:], bs[l].broadcast_to([128, D]), writes=bsf.b)
                op("dve", lambda e: e.tensor_copy(out=lnbb[:], in_=lnbf[:]), reads=lnbf.b, writes=lnbb.b)
                for g in range(8):
                    op("dve", lambda e, g=g: e.tensor_tensor(out=wmT[:, g, :], in0=wsf[:, g * 128:(g + 1) * 128], in1=trif[:, 0:128], op=ALU.mult), reads=wsf.b + trif.b, writes=wmT.b)
                for g in range(8):
                    ps = nb()
                    op("pe", lambda e, g=g, ps=ps: e.matmul(ps[:, 0:128], lhsT=lnbb[:, g * 128:(g + 1) * 128], rhs=wmT[:, g, :], start=True, stop=True), reads=lnbb.b + wmT.b, writes=ps.b)
                    op("dve", lambda e, g=g, ps=ps: e.tensor_tensor(out=Bc[:, g, :], in0=ps[:, 0:128], in1=bsf[:, g * 128:(g + 1) * 128], op=ALU.add), reads=ps.b + bsf.b, writes=Bc.b)

                def loadw(ch0, n):
                    w = wt[rr["wt"] % 3]
                    rr["wt"] += 1
                    dma("sp", w[:, 0:n, :], winB[l, ch0:ch0 + n].rearrange("c p f -> p c f"), reads=[SB["winB"]], writes=w.b)
                    return w

                def mm_fm(ps, w, ci, m0, M):
                    hT_ = cur["hT"]
                    for kc in range(KC):
                        op("pe", lambda e, kc=kc: e.matmul(ps[0:M, :], lhsT=w[:, ci, kc * 128 + m0:kc * 128 + m0 + M], rhs=hT_[:, kc, :], start=(kc == 0), stop=(kc == KC - 1)),
                           reads=w.b + hT_.b, writes=ps.b)

                def mm_tm(ps, w, c0, n, i, ncol=None):
                    hT_ = cur["hT"]
                    for kc in range(KC):
                        if ncol is None:
                            o_ap = ps[:, 0:n * 128].rearrange("p (c m) -> p c m", c=n)
                            r_ap = w[:, c0:c0 + n, kc * 128:(kc + 1) * 128]
                        else:
                            o_ap = ps[:, 0:ncol]
                            r_ap = w[:, c0, kc * 128:kc * 128 + ncol]
                        op("pe", lambda e, kc=kc, o_ap=o_ap, r_ap=r_ap: e.matmul(o_ap, lhsT=hT_[:, kc, i * 128:(i + 1) * 128], rhs=r_ap, start=(kc == 0), stop=(kc == KC - 1)),
                           reads=w.b + hT_.b, writes=ps.b)

                hT2 = [hT, sb(st, "hTb", [128, KC, 512], BF16)]
                xt2 = [xt, [sb(st, "xtb%d" % i, [128, D], F32) for i in range(4)]]

                def a1_pre(blk):
                    t0 = blk * 512
                    xs = xt2[blk % 2]
                    for i in range(4):
                        x_ = xs[i]
                        dma("sp", x_[:], xcur[b, t0 + i * 128:t0 + (i + 1) * 128, :], reads=xcur_b, writes=x_.b)
                        op("act", lambda e, x_=x_, i=i: e.activation(out=junk[:], in_=x_[:], func=AF.Square, accum_out=stat[:, i:i + 1]), reads=x_.b, writes=junk.b + stat.b)
                        op("dve", lambda e, i=i: e.tensor_scalar(out=stat[:, 4 + i:5 + i], in0=stat[:, i:i + 1], scalar1=1.0 / D, scalar2=EPS, op0=ALU.mult, op1=ALU.add), reads=stat.b, writes=stat.b)
                        op("act", lambda e, i=i: e.activation(out=stat[:, 4 + i:5 + i], in_=stat[:, 4 + i:5 + i], func=AF.Sqrt), reads=stat.b, writes=stat.b)
                        op("dve", lambda e, i=i: e.reciprocal(out=stat[:, 4 + i:5 + i], in_=stat[:, 4 + i:5 + i]), reads=stat.b, writes=stat.b)
                        op("act", lambda e, x_=x_, i=i: e.activation(out=x_[:], in_=x_[:], func=AF.Copy, scale=stat[:, 4 + i:5 + i]), reads=x_.b + stat.b, writes=x_.b)

                def a1_trans(blk):
                    xs = xt2[blk % 2]
                    hT_ = hT2[blk % 2]
                    for kc in range(KC):
                        ps = nb()
                        for i in range(4):
                            op("pe", lambda e, i=i, kc=kc, ps=ps: e.transpose(out=ps[:, i * 128:(i + 1) * 128], in_=xs[i][:, kc * 128:(kc + 1) * 128], identity=ident_f[:]), reads=xs[i].b + ident_f.b, writes=ps.b)
                        if kc % 2 == 0:
                            op("act", lambda e, kc=kc, ps=ps: e.activation(out=hT_[:, kc, :], in_=ps[:, :], func=AF.Identity, scale=A_scale[:, kc, b:b + 1], bias=A_shift[:, kc, b:b + 1]),
                               reads=ps.b + A_scale.b + A_shift.b, writes=hT_.b)
                        else:
                            op("dve", lambda e, kc=kc, ps=ps: e.tensor_scalar(out=hT_[:, kc, :], in0=ps[:, :], scalar1=A_scale[:, kc, b:b + 1], scalar2=A_shift[:, kc, b:b + 1], op0=ALU.mult, op1=ALU.add),
                               reads=ps.b + A_scale.b + A_shift.b, writes=hT_.b)

                cur = {"hT": hT}
                a1_pre(0)
                a1_trans(0)
                for blk in range(NQB):
                    t0 = blk * 512
                    tsl = slice(t0, t0 + 512)
                    cur["hT"] = hT2[blk % 2]
                    if blk + 1 < NQB:
                        a1_pre(blk + 1)
                    wv0 = loadw(CH_V, 4)
                    wv1 = loadw(CH_V + 4, 4)
                    for i in range(4):
                        p0, p1 = nb(), nb()
                        mm_tm(p0, wv0, 0, 4, i)
                        mm_tm(p1, wv1, 0, 4, i)
                        op("act", lambda e, p0=p0: e.activation(out=gv[:, 0:512], in_=p0[:, :], func=AF.Gelu_apprx_tanh), reads=p0.b, writes=gv.b)
                        op("act", lambda e, p1=p1: e.activation(out=gv[:, 512:1024], in_=p1[:, :], func=AF.Gelu_apprx_tanh), reads=p1.b, writes=gv.b)
                        op("dve", lambda e: e.bn_stats(out=bst[:, 0:6], in_=gv[:, 0:512]), reads=gv.b, writes=bst.b)
                        op("dve", lambda e: e.bn_stats(out=bst[:, 6:12], in_=gv[:, 512:1024]), reads=gv.b, writes=bst.b)
                        op("dve", lambda e: e.bn_aggr(out=bst[:, 12:14], in_=bst[:, 0:12]), reads=bst.b, writes=bst.b)
                        op("dve", lambda e: e.tensor_scalar(out=bst[:, 14:15], in0=bst[:, 13:14], scalar1=EPS, scalar2=None, op0=ALU.add), reads=bst.b, writes=bst.b)
                        op("act", lambda e: e.activation(out=bst[:, 14:15], in_=bst[:, 14:15], func=AF.Sqrt), reads=bst.b, writes=bst.b)
                        op("dve", lambda e: e.reciprocal(out=bst[:, 14:15], in_=bst[:, 14:15]), reads=bst.b, writes=bst.b)
                        op("dve", lambda e, i=i: e.tensor_scalar(out=vn[i][:], in0=gv[:], scalar1=bst[:, 12:13], scalar2=bst[:, 14:15], op0=ALU.subtract, op1=ALU.mult), reads=gv.b + bst.b, writes=vn[i].b)
                    xin_sb, ybuf, acc, sz, t1, t2 = tmp
                    for j in range(8):
                        w = loadw(4 * j, 4)
                        pb, pcg, pxin, pz = nb(), nb(), nb(), nb()
                        mm_fm(pcg, w, 1, 0, 128)
                        mm_fm(pxin, w, 2, 0, 128)
                        mm_fm(pz, w, 3, 0, 128)
                        mm_fm(pb, w, 0, 0, 128)
                        op("act", lambda e, pxin=pxin: e.copy(out=xin_sb[:, 0:512], in_=pxin[:, :]), reads=pxin.b, writes=xin_sb.b)
                        op("dve", lambda e, j=j: e.tensor_copy(out=ybuf[:, 0:2], in_=carry[:, j, :]), reads=carry.b, writes=ybuf.b)
                        op("dve", lambda e, pcg=pcg: e.tensor_tensor(out=ybuf[:, 2:514], in0=pcg[:, :], in1=xin_sb[:, 0:512], op=ALU.mult), reads=pcg.b + xin_sb.b, writes=ybuf.b)
                        op("dve", lambda e, j=j: e.tensor_copy(out=carry[:, j, :], in_=ybuf[:, 512:514]), reads=ybuf.b, writes=carry.b)
                        op("dve", lambda e, j=j: e.tensor_scalar(out=acc[:, 0:512], in0=ybuf[:, 2:514], scalar1=cw_t[:, j, 2:3], scalar2=cb_t[:, j:j + 1], op0=ALU.mult, op1=ALU.add),
                           reads=ybuf.b + cw_t.b + cb_t.b, writes=acc.b)
                        op("dve", lambda e, j=j: e.scalar_tensor_tensor(out=acc[:, 0:512], in0=ybuf[:, 1:513], scalar=cw_t[:, j, 1:2], in1=acc[:, 0:512], op0=ALU.mult, op1=ALU.add),
                           reads=ybuf.b + cw_t.b + acc.b, writes=acc.b)
                        op("dve", lambda e, j=j: e.scalar_tensor_tensor(out=acc[:, 0:512], in0=ybuf[:, 0:512], scalar=cw_t[:, j, 0:1], in1=acc[:, 0:512], op0=ALU.mult, op1=ALU.add),
                           reads=ybuf.b + cw_t.b + acc.b, writes=acc.b)
                        op("act", lambda e, pz=pz: e.activation(out=sz[:, 0:512], in_=pz[:, :], func=AF.Silu), reads=pz.b, writes=sz.b)
                        op("dve", lambda e: e.tensor_tensor(out=t1[:, 0:512], in0=acc[:, 0:512], in1=sz[:, 0:512], op=ALU.mult), reads=acc.b + sz.b, writes=t1.b)
                        op("dve", lambda e, j=j, pb=pb: e.tensor_tensor(out=yAT[:, j, :], in0=pb[:, :], in1=t1[:, 0:512], op=ALU.mult), reads=pb.b + t1.b, writes=yAT.b)
                    for (ch0, kind) in ((CH_GA, "A"), (CH_GC, "C"), (CH_GB, "B"), (CH_ZB, "Z")):
                        for half in range(2):
                            w = loadw(ch0 + 4 * half, 4)
                            for ci in range(4):
                                j = 4 * half + ci
                                ps = nb()
                                mm_fm(ps, w, ci, 0, 128)
                                if kind == "A":
                                    op("act", lambda e, j=j, ps=ps: e.activation(out=sgA[:, j, :], in_=ps[:, :], func=AF.Sigmoid), reads=ps.b, writes=sgA.b)
                                elif kind == "C":
                                    op("act", lambda e, j=j, ps=ps: e.activation(out=sgC[:, j, :], in_=ps[:, :], func=AF.Sigmoid), reads=ps.b, writes=sgC.b)
                                else:
                                    cs = cst[rr["cs"] % 8]
                                    rr["cs"] += 1
                                    fn = AF.Sigmoid if kind == "B" else AF.Silu
                                    op("act", lambda e, cs=cs, ps=ps, fn=fn: e.activation(out=cs[:], in_=ps[:, :], func=fn), reads=ps.b, writes=cs.b)
                                    dst = s_gB if kind == "B" else s_szB
                                    dma("act", dst[j, :, tsl], cs[:], reads=cs.b, writes=[SB["gB" if kind == "B" else "szB"]])
                    szc, tc_, sp2 = sz, t1, t2
                    for half in range(2):
                        wu = loadw(CH_U + 4 * half, 4)
                        wz = loadw(CH_ZC + 4 * half, 4)
                        for ci in range(4):
                            pu = nb()
                            mm_fm(pu, wu, ci, 0, 128)
                            op("act", lambda e, pu=pu, ci=ci: e.activation(out=ugs[ci][:], in_=pu[:, :], func=AF.Gelu_apprx_tanh), reads=pu.b, writes=ugs[ci].b)
                        for ci in range(4):
                            g = 4 * half + ci
                            psp, pzc = nb(), nb()
                            for i in range(4):
                                op("pe", lambda e, i=i, g=g, psp=psp: e.matmul(psp[:, i * 128:(i + 1) * 128], lhsT=vn[i][:, g * 128:(g + 1) * 128], rhs=wmT[:, g, :], start=True, stop=True),
                                   reads=vn[i].b + wmT.b, writes=psp.b)
                            mm_fm(pzc, wz, ci, 0, 128)
                            for i in range(4):
                                op("dve", lambda e, i=i, g=g, psp=psp: e.scalar_tensor_tensor(out=sp2[:, i * 128:(i + 1) * 128], in0=psp[:, i * 128:(i + 1) * 128], scalar=lng_t[:, g:g + 1], in1=Bc[:, g, :], op0=ALU.mult, op1=ALU.add),
                                   reads=psp.b + lng_t.b + Bc.b, writes=sp2.b)
                            op("act", lambda e, pzc=pzc: e.activation(out=szc[:, 0:512], in_=pzc[:, :], func=AF.Silu), reads=pzc.b, writes=szc.b)
                            op("dve", lambda e, ci=ci: e.tensor_tensor(out=tc_[:, 0:512], in0=ugs[ci][:], in1=szc[:, 0:512], op=ALU.mult), reads=ugs[ci].b + szc.b, writes=tc_.b)
                            op("dve", lambda e, g=g: e.tensor_tensor(out=yCT[:, g, :], in0=tc_[:, 0:512], in1=sp2[:, 0:512], op=ALU.mult), reads=tc_.b + sp2.b, writes=yCT.b)
                    for half in range(2):
                        w = loadw(CH_Q + 4 * half, 4)
                        q_ = qst[rr["cs"] % 3]
                        rr["cs"] += 1
                        qe = rr["cs"] % 2
                        for ci in range(4):
                            ps = nb()
                            mm_fm(ps, w, ci, 0, 128)
                            evac_copy(qe, q_[:, ci, :], ps[:, :], ps.b, q_.b)
                        qn = ("act", "pool")[qe]
                        dma(qn, s_qT[:, 8 * half:8 * half + 8:2, tsl], q_[0:64, :, :], reads=q_.b, writes=[SB["qT"]])
                        dma(qn, s_qT[:, 8 * half + 1:8 * half + 8:2, tsl], q_[64:128, :, :], reads=q_.b, writes=[SB["qT"]])
                    kcnt = 0
                    for (ch0, knm, vnm) in ((CH_KC, "kc", "vc"), (CH_KS, "ks", None), (CH_KW, "kw", None)):
                        w = loadw(ch0, 4)
                        for (c_off, nm) in ((0, knm), (2, vnm)):
                            if nm is None:
                                continue
                            q_ = qst[rr["cs"] % 3]
                            rr["cs"] += 1
                            kcnt += 1
                            qe = rr["cs"] % 2
                            for i in range(2):
                                ps = nb()
                                mm_fm(ps, w, c_off + i, 0, 128)
                                evac_copy(qe, q_[:, i, :], ps[:, :], ps.b, q_.b)
                            qn = ("act", "pool")[qe]
                            dma(qn, s_kT[nm][:, 0:4:2, tsl], q_[0:64, 0:2, :], reads=q_.b, writes=[SB[nm]])
                            dma(qn, s_kT[nm][:, 1:4:2, tsl], q_[64:128, 0:2, :], reads=q_.b, writes=[SB[nm]])
                        if vnm is None:
                            typ = 0 if knm == "ks" else 1
                            for i in range(4):
                                ps = nb()
                                mm_tm(ps, w, 2, 2, i)
                                evac_copy(typ, vst[:, i, typ, :].rearrange("p (g f) -> p g f", g=4)[:, :, 0:64], ps[:, 0:256].rearrange("p (g f) -> p g f", g=4), ps.b, vst.b)
                            nm = "vs" if typ == 0 else "vw"
                            dma(("act", "pool")[typ], s_vX[nm][blk * 4:(blk + 1) * 4].rearrange("i p f -> p i f"), vst[:, :, typ, :], reads=vst.b, writes=[SB[nm]])
                    w = loadw(CH_GL, 1)
                    for i in range(4):
                        ps = nb()
                        mm_tm(ps, w, 0, 1, i, ncol=48)
                        op("act", lambda e, i=i, ps=ps: e.activation(out=gst[:, i, :], in_=ps[:, 0:48], func=AF.Sigmoid), reads=ps.b, writes=gst.b)
                    dma("act", s_glg[blk * 4:(blk + 1) * 4].rearrange("i p f -> p i f"), gst[:], reads=gst.b, writes=[SB["glg"]])
                    m1, m2 = acc, ybuf
                    for dch in range(8):
                        pa, pc = nb(), nb()
                        for kc in range(KC):
                            op("pe", lambda e, kc=kc, dch=dch, pa=pa: e.matmul(pa[:, :], lhsT=wbr0[:, dch, kc * 128:(kc + 1) * 128], rhs=yAT[:, kc, :], start=(kc == 0), stop=(kc == KC - 1)), reads=wbr0.b + yAT.b, writes=pa.b)
                        for kc in range(KC):
                            op("pe", lambda e, kc=kc, dch=dch, pc=pc: e.matmul(pc[:, :], lhsT=wbr2[:, dch, kc * 128:(kc + 1) * 128], rhs=yCT[:, kc, :], start=(kc == 0), stop=(kc == KC - 1)), reads=wbr2.b + yCT.b, writes=pc.b)
                        op("dve", lambda e, dch=dch, pa=pa: e.tensor_tensor(out=m1[:, 0:512], in0=pa[:, :], in1=sgA[:, dch, :], op=ALU.mult), reads=pa.b + sgA.b, writes=m1.b)
                        op("dve", lambda e, dch=dch, pc=pc: e.tensor_tensor(out=m2[:, 0:512], in0=pc[:, :], in1=sgC[:, dch, :], op=ALU.mult), reads=pc.b + sgC.b, writes=m2.b)
                        cs = cst[rr["cs"] % 8]
                        rr["cs"] += 1
                        op("dve", lambda e, cs=cs: e.tensor_tensor(out=cs[:], in0=m1[:, 0:512], in1=m2[:, 0:512], op=ALU.add), reads=m1.b + m2.b, writes=cs.b)
                        dma("pool", s_mp[dch, :, tsl], cs[:], reads=cs.b, writes=[SB["mp"]])
                    if blk + 1 < NQB:
                        a1_trans(blk + 1)
                sc.barrier()


        SLOPES = [2.0 ** (-8.0 * (i + 1) / 16.0) for i in range(16)]
        SKIP_T = 45.0
        XW = 65 + NSEL
        NS2 = NSEL + 2

        def phaseB(l, b):
            with ExitStack() as st:
                kcmpT = sb(st, "kcmpT", [67, 4, NCT * 128], BF16)
                VCX = sb(st, "VCX", [128, NCT, 4, XW + 2], BF16)
                pbias = sb(st, "pbias", [128, 2], F32)
                op("dve", lambda e: e.memset(kcmpT[0:64], 0.0), writes=kcmpT.b)
                op("dve", lambda e: e.memset(kcmpT[64:67], 1.0), writes=kcmpT.b)
                op("dve", lambda e: e.memset(VCX[:], 0.0), writes=VCX.b)
                op("dve", lambda e: e.memset(VCX[:, :, :, 64:65], 1.0), writes=VCX.b)
                for g in range(4):
                    dma("pool", VCX[:, :, g, 65:XW], k_ov[:, :, :], writes=VCX.b)
                with ExitStack() as st0:
                    xcT = [sb(st0, "xcT%d" % kv, [64, 4, S], BF16) for kv in range(2)]
                    w1t = [sb(st0, "w1t%d" % kv, [64, 32 * 128], BF16) for kv in range(2)]
                    w2t = [sb(st0, "w2t%d" % kv, [128, 64], BF16) for kv in range(2)]
                    post = sb(st0, "post", [64, 2, 32], BF16)
                    hid = sb(st0, "hid", [128, 128], BF16)
                    for kv, nm in enumerate(("kc", "vc")):
                        dma("sp", xcT[kv][:], s_kT[nm][:, :, :], reads=[SB[nm]], writes=xcT[kv].b)
                        dma("pool", w1t[kv][:], w1[l, kv], writes=w1t[kv].b)
                        dma("pool", w2t[kv][:], w2[l, kv], writes=w2t[kv].b)
                        dma("pool", post[:, kv, :], posT[l, kv], writes=post.b)
                    op("dve", lambda e: e.memset(hid[:], 0.0), writes=hid.b)
                    for kv in range(2):
                        ps = nb()
                        for lq in range(32):
                            op("pe", lambda e, lq=lq, kv=kv, ps=ps: e.matmul(ps[:, 0:1], lhsT=w1t[kv][0:64, lq * 128:(lq + 1) * 128], rhs=post[0:64, kv, lq:lq + 1], start=(lq == 0), stop=(lq == 31)),
                               reads=w1t[kv].b + post.b, writes=ps.b)
                        op("dve", lambda e, kv=kv, ps=ps: e.tensor_copy(out=pbias[:, kv:kv + 1], in_=ps[:, 0:1]), reads=ps.b, writes=pbias.b)
                    for kv in range(2):
                        for g in range(4):
                            for ct in range(NCT):
                                n_i = min(128, NCMP - ct * 128)
                                ps = nb()
                                for lq in range(32):
                                    s0 = 16 * ct * 128 + lq
                                    op("pe", lambda e, lq=lq, kv=kv, g=g, ps=ps, s0=s0, n_i=n_i: e.matmul(ps[:, 0:n_i], lhsT=w1t[kv][0:64, lq * 128:(lq + 1) * 128], rhs=xcT[kv][0:64, g, s0:s0 + 16 * (n_i - 1) + 1:16], start=(lq == 0), stop=(lq == 31)),
                                       reads=w1t[kv].b + xcT[kv].b, writes=ps.b)
                                op("act", lambda e, kv=kv, ps=ps, n_i=n_i: e.activation(out=hid[:, 0:n_i], in_=ps[:, 0:n_i], func=AF.Silu, bias=pbias[:, kv:kv + 1]), reads=ps.b + pbias.b, writes=hid.b)
                                ps2 = nb()
                                if kv == 0:
                                    op("pe", lambda e, ps2=ps2, n_i=n_i: e.matmul(ps2[0:64, 0:n_i], lhsT=w2t[0][:, 0:64], rhs=hid[:, 0:n_i], start=True, stop=True), reads=w2t[0].b + hid.b, writes=ps2.b)
                                    op("dve", lambda e, ps2=ps2, n_i=n_i, g=g, ct=ct: e.tensor_copy(out=kcmpT[0:64, g, ct * 128:ct * 128 + n_i], in_=ps2[0:64, 0:n_i]), reads=ps2.b, writes=kcmpT.b)
                                else:
                                    op("pe", lambda e, ps2=ps2, n_i=n_i: e.matmul(ps2[:, 0:64], lhsT=hid[:, 0:128], rhs=w2t[1][:, 0:64], start=True, stop=True), reads=w2t[1].b + hid.b, writes=ps2.b)
                                    op("dve", lambda e, ps2=ps2, n_i=n_i, g=g, ct=ct: e.tensor_copy(out=VCX[0:n_i, ct, g, 0:64], in_=ps2[0:n_i, 0:64]), reads=ps2.b, writes=VCX.b)
                    sc.barrier()
                KTs = sb(st, "KTs", [67, 4, S], BF16)
                KTw = [sb(st, "KTw%d" % i, [67, 4, 1024], BF16) for i in range(2)]
                Vs = sb(st, "Vs", [128, NT, 260], BF16)
                Vw = [sb(st, "Vw%d" % i, [128, 8, 260], BF16) for i in range(2)]
                QT = [sb(st, "QT%d" % i, [67, 16, 512], BF16) for i in range(2)]
                glt = [sb(st, "glt%d" % i, [128, 4, 48], F32) for i in range(2)]
                OACC2 = [sb(st, "OACC%d" % i, [128, 4, D], F32) for i in range(2)]
                cur_o = {"o": OACC2[0], "qb": 0}
                PT = [sb(st, "PT%d" % i, [128, 512], BF16) for i in range(6)]
                cmt = sb(st, "cmt", [128, 13, 512], BF16)
                eselt = sb(st, "eselt", [128, NT * 128], BF16)
                faddq = [sb(st, "faddq%d" % i, [128, 4, NSEL], F32) for i in range(2)]
                albt = sb(st, "albt", [128, 16, NT], F32)
                albct = sb(st, "albct", [128, 16, 8], F32)
                impacc = sb(st, "impacc", [128, 4, 4, NSEL], F32)
                sm = sb(st, "sm", [128, 8], F32)
                scr_ = sb(st, "scr_", [128, NSEL], F32)
                scr2 = sb(st, "scr2", [128, NSEL], F32)
                m8 = sb(st, "m8", [128, 16], F32)
                mbs = sb(st, "mbs", [128, 16, NSEL], BF16)
                MBTs = [sb(st, "MBT%d" % i, [128, 512], BF16) for i in range(4)]
                dma("pool", cmt[:], k_cm[:, :, :], writes=cmt.b)
                op("dve", lambda e: e.memset(eselt[:], 0.0), writes=eselt.b)
                for i in range(4):
                    op("dve", lambda e, i=i: e.memset(MBTs[i][:], 0.0), writes=MBTs[i].b)
                dma("pool", eselt[0:NSEL], k_esel[:, :], writes=eselt.b)
                dma("sp", albt[:], k_alb[:, :, :], writes=albt.b)
                dma("sp", albct[:], k_albc[:, :, :], writes=albct.b)
                dma("sp", KTs[0:64], s_kT["ks"][:, :, :], reads=[SB["ks"]], writes=KTs.b)
                op("dve", lambda e: e.memset(KTs[64:67], 1.0), writes=KTs.b)
                dma("sp", Vs[:], s_vX["vs"].rearrange("i p f -> p i f"), reads=[SB["vs"]], writes=Vs.b)
                for i in range(2):
                    op("dve", lambda e, i=i: e.memset(KTw[i][64:67], 1.0), writes=KTw[i].b)
                    dma("pool", QT[i][64:67], k_aq[:, :, :], writes=QT[i].b)
                ctr = {"s": 0, "p": 0, "o": 0, "c": 0}
                SPB = [PS[0], PS[1], PS[4]]
                SPB4 = [PS[0], PS[1], PS[4], PS[5]]
                pipe = []
                LAG = 3

                def push(fn):
                    pipe.append(fn)
                    while len(pipe) > LAG:
                        pipe.pop(0)()

                def flush():
                    while pipe:
                        pipe.pop(0)()

                OSB = [sb(st, "osb%d" % i, [66, 512], F32) for i in range(5)]
                for i in range(5):
                    op("dve", lambda e, i=i: e.memset(OSB[i][:], 0.0), writes=OSB[i].b)
                ISB = [sb(st, "isb%d" % i, [NS2, 512], F32) for i in range(2)]
                for i in range(2):
                    op("dve", lambda e, i=i: e.memset(ISB[i][:], 0.0), writes=ISB[i].b)
                PTR = PS[6]
                PIMS = [PS[7], PS[5]]

                def attend(tiles, Q, h, g, gate_col, G_, first_branch, is_cmp):
                    ob = PS[2 + ctr["o"] % 2]
                    osb = OSB[ctr["o"] % 5]
                    ctr["o"] += 1
                    if is_cmp:
                        PIM = PIMS[ctr["c"] % 2]
                        isb = ISB[ctr["c"] % 2]
                        ctr["c"] += 1
                    nt = len(tiles)

                    def epilogue2():
                        for sub in range(4):
                            op("pe", lambda e, sub=sub: e.transpose(out=PTR[:, sub * 66:(sub + 1) * 66], in_=osb[0:66, sub * 128:(sub + 1) * 128], identity=ident_f[0:66, 0:66]), reads=osb.b + ident_f.b, writes=PTR.b)
                        if is_cmp:
                            for sub in range(4):
                                op("pe", lambda e, sub=sub: e.transpose(out=PIM[:, sub * NS2:(sub + 1) * NS2], in_=isb[0:NS2, sub * 128:(sub + 1) * 128], identity=ident_f[0:NS2, 0:NS2]), reads=isb.b + ident_f.b, writes=PIM.b)
                        op("dve", lambda e: e.tensor_scalar(out=sm[:, 0:4], in0=PTR[:, 0:264].rearrange("p (s f) -> p s f", s=4)[:, :, 64], scalar1=1e-30, scalar2=None, op0=ALU.add), reads=PTR.b, writes=sm.b)
                        op("dve", lambda e: e.reciprocal(out=sm[:, 0:4], in_=sm[:, 0:4]), reads=sm.b, writes=sm.b)
                        op("dve", lambda e: e.tensor_tensor(out=sm[:, 4:8], in0=sm[:, 0:4], in1=G_[:, :, gate_col], op=ALU.mult), reads=sm.b + G_.b, writes=sm.b)
                        for sub in range(4):
                            c0_ = sub * 66
                            OACC = cur_o["o"]
                            osl = OACC[:, sub, h * 64:(h + 1) * 64]
                            if first_branch:
                                op("dve", lambda e, osl=osl, c0_=c0_, sub=sub: e.tensor_scalar(out=osl, in0=PTR[:, c0_:c0_ + 64], scalar1=sm[:, 4 + sub:5 + sub], scalar2=None, op0=ALU.mult), reads=PTR.b + sm.b, writes=OACC.b)
                            else:
                                op("dve", lambda e, osl=osl, c0_=c0_, sub=sub: e.scalar_tensor_tensor(out=osl, in0=PTR[:, c0_:c0_ + 64], scalar=sm[:, 4 + sub:5 + sub], in1=osl, op0=ALU.mult, op1=ALU.add), reads=PTR.b + sm.b + OACC.b, writes=OACC.b)
                            if is_cmp:
                                i0 = sub * NS2
                                if h % 4 == 0:
                                    op("dve", lambda e, sub=sub, i0=i0: e.tensor_scalar(out=impacc[:, g, sub, :], in0=PIM[:, i0:i0 + NSEL], scalar1=sm[:, sub:sub + 1], scalar2=None, op0=ALU.mult), reads=PIM.b + sm.b, writes=impacc.b)
                                else:
                                    op("dve", lambda e, sub=sub, i0=i0: e.scalar_tensor_tensor(out=impacc[:, g, sub, :], in0=PIM[:, i0:i0 + NSEL], scalar=sm[:, sub:sub + 1], in1=impacc[:, g, sub, :], op0=ALU.mult, op1=ALU.add), reads=PIM.b + sm.b + impacc.b, writes=impacc.b)

                    tiles = sorted(tiles, key=lambda t_: 0 if t_[8] == (0, 512) else 1)
                    assert tiles[0][8] == (0, 512)
                    for ti, (k_ap, k_rd, extras, bias_ap, bias_rd, v_ap, ov_ap, v_rd, (lo, hi)) in enumerate(tiles):
                        spb = SPB if is_cmp else SPB4
                        sp = spb[ctr["s"] % len(spb)]
                        ctr["s"] += 1
                        op("pe", lambda e, sp=sp, k_ap=k_ap, lo=lo, hi=hi: e.matmul(sp[:, lo:hi], lhsT=k_ap, rhs=Q[0:67, h, lo:hi], start=True, stop=(len(extras) == 0)), reads=k_rd + Q.b, writes=sp.b)
                        for xi, (xl, xr, xrd) in enumerate(extras):
                            op("pe", lambda e, sp=sp, xl=xl, xr=xr, xi=xi, lo=lo, hi=hi: e.matmul(sp[:, lo:hi], lhsT=xl, rhs=xr[:, lo:hi], start=False, stop=(xi == len(extras) - 1)), reads=xrd, writes=sp.b)
                        pt = PT[ctr["p"] % 6]
                        ctr["p"] += 1
                        op("act", lambda e, sp=sp, pt=pt, bias_ap=bias_ap, lo=lo, hi=hi: e.activation(out=pt[:, lo:hi], in_=sp[:, lo:hi], func=AF.Exp, scale=0.125, bias=bias_ap), reads=sp.b + bias_rd, writes=pt.b)

                        def stage2(ti=ti, pt=pt, v_ap=v_ap, ov_ap=ov_ap, v_rd=v_rd, lo=lo, hi=hi):
                            op("pe", lambda e: e.matmul(ob[0:65, lo:hi], lhsT=v_ap, rhs=pt[:, lo:hi], start=(ti == 0), stop=(ti == nt - 1)), reads=pt.b + v_rd, writes=ob.b)
                            if is_cmp:
                                op("pe", lambda e: e.matmul(PIM[0:NS2, :], lhsT=ov_ap, rhs=pt[:], start=(ti == 0), stop=(ti == nt - 1)), reads=pt.b + v_rd, writes=PIM.b)
                            if ti == nt - 1:
                                if cur_o["qb"] <= 2 and ctr["o"] % 2 == 0:
                                    op("act", lambda e: e.copy(out=osb[0:65, :], in_=ob[0:65, :]), reads=ob.b, writes=osb.b)
                                else:
                                    op("dve", lambda e: e.tensor_copy(out=osb[0:65, :], in_=ob[0:65, :]), reads=ob.b, writes=osb.b)
                                if is_cmp:
                                    op("dve", lambda e: e.tensor_copy(out=isb[0:NSEL, :], in_=PIM[0:NSEL, :]), reads=PIM.b, writes=isb.b)
                                if is_cmp:
                                    epilogue2()
                                else:
                                    push(epilogue2)

                        push(stage2)

                for qb in range(NQB):
                    t0 = qb * 512
                    Q = QT[qb % 2]
                    cur_o["o"] = OACC2[qb % 2]
                    cur_o["qb"] = qb
                    OACC = OACC2[qb % 2]
                    Kw = KTw[qb % 2]
                    Vw_ = Vw[qb % 2]
                    G_ = glt[qb % 2]
                    dma("sp", Q[0:64], s_qT[:, :, t0:t0 + 512], reads=[SB["qT"]], writes=Q.b)
                    k0 = max(0, t0 - 512)
                    dma("sp", Kw[0:64, :, k0 - (t0 - 512):1024], s_kT["kw"][:, :, k0:t0 + 512], reads=[SB["kw"]], writes=Kw.b)
                    c0 = max(0, 4 * qb - 4)
                    dma("sp", Vw_[:, c0 - (4 * qb - 4):8, :], s_vX["vw"][c0:4 * qb + 4].rearrange("i p f -> p i f"), reads=[SB["vw"]], writes=Vw_.b)
                    dma("sp", G_[:], s_glg[4 * qb:4 * qb + 4].rearrange("i p f -> p i f"), reads=[SB["glg"]], writes=G_.b)
                    faddt = faddq[qb % 2]
                    dma("sp", faddt[:], k_fadd[:, 4 * qb:4 * qb + 4, :], writes=faddt.b)
                    for g in range(4):
                        for n in range(4):
                            h = 4 * g + n
                            tiles = []
                            for c in range(NCT):
                                m = qb - 4 * c
                                if m < 0:
                                    continue
                                extras = []
                                if m <= 4:
                                    extras.append((ident_b[:], cmt[:, 8 + m, :], ident_b.b + cmt.b))
                                tiles.append((kcmpT[0:67, g, c * 128:(c + 1) * 128], kcmpT.b, extras, albct[:, h, m:m + 1], albct.b, VCX[:, c, g, 0:65], VCX[:, c, g, 65:XW + 2], VCX.b, (0, 512)))
                            attend(tiles, Q, h, g, 3 * h, G_, True, True)
                    flush()
                    for g in range(4):
                        for n in range(4):
                            h = 4 * g + n
                            sub = n
                            mb_ = mbs[:, g * 4 + sub, :]
                            op("dve", lambda e, sub=sub, g=g: e.tensor_tensor(out=scr_[:], in0=impacc[:, g, sub, :], in1=faddt[:, sub, :], op=ALU.add), reads=impacc.b + faddt.b, writes=scr_.b)
                            op("dve", lambda e: e.max(out=m8[:, 0:8], in_=scr_[:]), reads=scr_.b, writes=m8.b)
                            op("dve", lambda e: e.match_replace(out=scr2[:], in_to_replace=m8[:, 0:8], in_values=scr_[:], imm_value=-1e30), reads=scr_.b + m8.b, writes=scr2.b)
                            op("dve", lambda e: e.max(out=m8[:, 8:16], in_=scr2[:]), reads=scr2.b, writes=m8.b)
                            op("dve", lambda e, mb_=mb_: e.tensor_scalar(out=mb_, in0=scr_[:], scalar1=m8[:, 15:16], scalar2=MNEG, op0=ALU.is_lt, op1=ALU.mult), reads=scr_.b + m8.b, writes=mbs.b)
                            tiles = []
                            for c in range(max(0, 4 * qb - 4), 4 * qb + 4):
                                o = c - 4 * qb
                                dmin = t0 - (128 * c + 127)
                                if SLOPES[h] * dmin > SKIP_T:
                                    continue
                                li = c - (4 * qb - 4)
                                extras = [(ident_b[:], cmt[:, 4 + o, :], ident_b.b + cmt.b)]
                                tiles.append((Kw[0:67, g, li * 128:(li + 1) * 128], Kw.b, extras, albt[:, h, o + NT - 4:o + NT - 3], albt.b, Vw_[:, li, g * 65:(g + 1) * 65], None, Vw_.b, ((128 * o, 512) if o >= 0 else (0, 128 * (o + 5)))))
                            attend(tiles, Q, h, g, 3 * h + 2, G_, False, False)
                    for g in range(4):
                        pm = PIMS[g % 2]
                        pmb = pm[:].bitcast(BF16)
                        for sub in range(4):
                            op("pe", lambda e, sub=sub, g=g, pmb=pmb: e.transpose(out=pmb[0:NSEL, sub * 128:(sub + 1) * 128], in_=mbs[:, g * 4 + sub, :], identity=ident_b[:]), reads=mbs.b + ident_b.b, writes=pm.b)
                        op("act", lambda e, pmb=pmb, g=g: e.copy(out=MBTs[g][0:NSEL, :], in_=pmb[0:NSEL, 0:512]), reads=pm.b, writes=MBTs[g].b)
                    for g in range(4):
                        MBT = MBTs[g]
                        for n in range(4):
                            h = 4 * g + n
                            tiles = []
                            for c in range(4 * qb + 4):
                                o = c - 4 * qb
                                dmin = t0 - (128 * c + 127)
                                if SLOPES[h] * dmin > SKIP_T:
                                    continue
                                extras = [(eselt[:, c * 128:(c + 1) * 128], MBT[:, :], eselt.b + MBT.b)]
                                if o >= 0:
                                    extras.append((ident_b[:], cmt[:, 4 + o, :], ident_b.b + cmt.b))
                                tiles.append((KTs[0:67, g, c * 128:(c + 1) * 128], KTs.b, extras, albt[:, h, o + NT - 4:o + NT - 3], albt.b, Vs[:, c, g * 65:(g + 1) * 65], None, Vs.b, ((128 * o, 512) if o >= 0 else (0, 512))))
                            attend(tiles, Q, h, g, 3 * h + 1, G_, False, False)
                    flush()
                    dma("pool", s_o[4 * qb:4 * qb + 4].rearrange("i p f -> p i f"), OACC[:], reads=OACC.b, writes=[SB["o"]])
                sc.barrier()

        def phaseC(l, b, xcur, xcur_b, xnext, xnext_b):
            with ExitStack() as st:
                wbr1 = sb(st, "wbr1", [128, 8, KC * 128], BF16)
                wout = sb(st, "wout", [128, KC * D], BF16)
                ot2 = [[sb(st, "ot%d_%d" % (k, i), [128, D], F32) for i in range(4)] for k in range(2)]
                szBt2 = [sb(st, "szBt%d" % k, [128, 8, 512], BF16) for k in range(2)]
                gBt2 = [sb(st, "gBt%d" % k, [128, 8, 512], BF16) for k in range(2)]
                mpt2 = [sb(st, "mpt%d" % k, [128, 8, 512], BF16) for k in range(2)]

                def c_loads(blk):
                    tsl = slice(blk * 512, blk * 512 + 512)
                    k = blk % 2
                    for i in range(4):
                        dma("sp", ot2[k][i][:], s_o[4 * blk + i], reads=[SB["o"]], writes=ot2[k][i].b)
                    dma("sp", szBt2[k][:], s_szB[:, :, tsl].rearrange("j p t -> p j t"), reads=[SB["szB"]], writes=szBt2[k].b)
                    dma("sp", gBt2[k][:], s_gB[:, :, tsl].rearrange("j p t -> p j t"), reads=[SB["gB"]], writes=gBt2[k].b)
                    dma("sp", mpt2[k][:], s_mp[:, :, tsl].rearrange("j p t -> p j t"), reads=[SB["mp"]], writes=mpt2[k].b)
                yBT = sb(st, "yBT", [128, 8, 512], BF16)
                mT = sb(st, "mT", [128, 8, 512], BF16)
                m1 = sb(st, "m1c", [128, 512], F32)
                xt = [sb(st, "xtc%d" % i, [128, D], F32) for i in range(3)]
                res = [sb(st, "res%d" % i, [128, D], F32) for i in range(3)]
                junk = sb(st, "junkc", [128, 512], BF16)
                stc = sb(st, "stc", [128, 4], F32)
                dma("sp", wbr1[:], wbrB[l, 8:16].rearrange("c p f -> p c f"), reads=[SB["wbrB"]], writes=wbr1.b)
                dma("sp", wout[:], woutB[l], reads=[SB["woutB"]], writes=wout.b)
                for blk in range(NQB):
                    t0 = blk * 512
                    tsl = slice(t0, t0 + 512)
                    if blk == 0:
                        c_loads(0)
                    if blk + 1 < NQB:
                        c_loads(blk + 1)
                    ot, szBt, gBt, mpt = ot2[blk % 2], szBt2[blk % 2], gBt2[blk % 2], mpt2[blk % 2]
                    for j in range(8):
                        ps = nb()
                        for i in range(4):
                            op("pe", lambda e, i=i, j=j, ps=ps: e.transpose(out=ps[:, i * 128:(i + 1) * 128], in_=ot[i][:, j * 128:(j + 1) * 128], identity=ident_f[:]), reads=ot[i].b + ident_f.b, writes=ps.b)
                        op("dve", lambda e, j=j, ps=ps: e.tensor_tensor(out=yBT[:, j, :], in0=ps[:, :], in1=szBt[:, j, :], op=ALU.mult), reads=ps.b + szBt.b, writes=yBT.b)
                    for dch in range(8):
                        ps = nb()
                        for kc in range(KC):
                            op("pe", lambda e, kc=kc, dch=dch, ps=ps: e.matmul(ps[:, :], lhsT=wbr1[:, dch, kc * 128:(kc + 1) * 128], rhs=yBT[:, kc, :], start=(kc == 0), stop=(kc == KC - 1)), reads=wbr1.b + yBT.b, writes=ps.b)
                        op("dve", lambda e, dch=dch, ps=ps: e.tensor_tensor(out=m1[:], in0=ps[:, :], in1=gBt[:, dch, :], op=ALU.mult), reads=ps.b + gBt.b, writes=m1.b)
                        op("pool", lambda e, dch=dch: e.tensor_tensor(out=mT[:, dch, :], in0=m1[:], in1=mpt[:, dch, :], op=ALU.add), reads=m1.b + mpt.b, writes=mT.b)
                    for i in range(4):
                        x_ = xt[(4 * blk + i) % 3]
                        r_ = res[(4 * blk + i) % 3]
                        rows = slice(t0 + i * 128, t0 + (i + 1) * 128)
                        dma("sp", x_[:], xcur[b, rows, :], reads=xcur_b, writes=x_.b)
                        pp = [nb(), nb()]
                        for n_ in range(2):
                            for kc in range(KC):
                                op("pe", lambda e, kc=kc, n_=n_, i=i, p_=pp[n_]: e.matmul(p_[:, :], lhsT=mT[:, kc, i * 128:(i + 1) * 128], rhs=wout[:, kc * D + n_ * 512:kc * D + (n_ + 1) * 512], start=(kc == 0), stop=(kc == KC - 1)),
                                   reads=mT.b + wout.b, writes=pp[n_].b)
                            op("act", lambda e, n_=n_, p_=pp[n_]: e.activation(out=junk[:], in_=p_[:, :], func=AF.Square, accum_out=stc[:, n_:n_ + 1]), reads=pp[n_].b, writes=junk.b + stc.b)
                        op("dve", lambda e: e.tensor_tensor(out=stc[:, 2:3], in0=stc[:, 0:1], in1=stc[:, 1:2], op=ALU.add), reads=stc.b, writes=stc.b)
                        op("dve", lambda e: e.tensor_scalar(out=stc[:, 2:3], in0=stc[:, 2:3], scalar1=1.0 / D, scalar2=EPS, op0=ALU.mult, op1=ALU.add), reads=stc.b, writes=stc.b)
                        op("act", lambda e: e.activation(out=stc[:, 2:3], in_=stc[:, 2:3], func=AF.Sqrt), reads=stc.b, writes=stc.b)
                        op("dve", lambda e: e.reciprocal(out=stc[:, 3:4], in_=stc[:, 2:3]), reads=stc.b, writes=stc.b)
                        for n_ in range(2):
                            sl = slice(n_ * 512, (n_ + 1) * 512)
                            op("dve", lambda e, n_=n_, sl=sl, p_=pp[n_], r_=r_: e.scalar_tensor_tensor(out=r_[:, sl], in0=p_[:, :], scalar=stc[:, 3:4], in1=GP[:, b, sl], op0=ALU.mult, op1=ALU.mult), reads=pp[n_].b + stc.b + GP.b, writes=r_.b)
                        op("pool", lambda e, r_=r_, x_=x_: e.tensor_tensor(out=r_[:], in0=r_[:], in1=x_[:], op=ALU.add), reads=r_.b + x_.b, writes=r_.b)
                        dma("pool", xnext[b, rows, :], r_[:], reads=r_.b, writes=xnext_b)
                sc.barrier()

        for l in range(DEPTH):
            xcur = x_in if l == 0 else xmid
            xnext = y_out if l == DEPTH - 1 else xmid
            xcur_b = [] if l == 0 else [SB["xmid"]]
            xnext_b = [SB["y"]] if l == DEPTH - 1 else [SB["xmid"]]
            with ExitStack() as st:
                sct = sb(st, "sct", [128, KC, NB], F32)
                scb = sb(st, "scb", [128, NB, KC, 128], F32)
                ones = sb(st, "ones", [128, 128], F32)
                gpre_t = sb(st, "gpre_t", [128, KC], F32)
                badaf_t = sb(st, "badaf_t", [128, 16], F32)
                ss_t = sb(st, "ss_t", [128, 16, NB], F32)
                gpost_bc = sb(st, "gpost_bc", [128, D], F32)
                badag_bc = sb(st, "badag_bc", [128, D], F32)
                wa = [sb(st, "wa%d" % i, [128, 4, KC * 128], F32) for i in range(2)]
                dma("sp", sct[:], cT[:, :, :], writes=sct.b)
                dma("sp", gpre_t[:], gpre[l], writes=gpre_t.b)
                dma("sp", badaf_t[:], badaf[l], writes=badaf_t.b)
                dma("sp", gpost_bc[:], gpost[l].broadcast_to([128, D]), writes=gpost_bc.b)
                dma("sp", badag_bc[:], badag[l].broadcast_to([128, D]), writes=badag_bc.b)
                op("act", lambda e: e.activation(out=sct[:], in_=sct[:], func=AF.Silu), reads=sct.b, writes=sct.b)
                op("dve", lambda e: e.memset(ones[:], 1.0), writes=ones.b)
                for b in range(NB):
                    for kc in range(KC):
                        op("dve", lambda e, b=b, kc=kc: e.tensor_scalar(out=scb[:, b, kc, :], in0=ones[:], scalar1=sct[:, kc, b:b + 1], scalar2=None, op0=ALU.mult),
                           reads=ones.b + sct.b, writes=scb.b)
                for grp in range(6):
                    w = wa[grp % 2]
                    dma("sp", w[:], wadaT[l, grp * 4:(grp + 1) * 4].rearrange("c p f -> p c f"), writes=w.b)
                    if grp < 4:
                        ps = PS[grp % 2]
                        for ci in range(4):
                            for kc in range(KC):
                                op("pe", lambda e, w=w, ci=ci, kc=kc, ps=ps: e.matmul(ps[:, ci * NB:(ci + 1) * NB], lhsT=w[:, ci, kc * 128:(kc + 1) * 128], rhs=sct[:, kc, :], start=(kc == 0), stop=(kc == KC - 1)),
                                   reads=w.b + sct.b, writes=ps.b)
                        for ci in range(4):
                            f = grp * 4 + ci
                            op("dve", lambda e, ps=ps, ci=ci, f=f: e.tensor_scalar(out=ss_t[:, f, :], in0=ps[:, ci * NB:(ci + 1) * NB], scalar1=badaf_t[:, f:f + 1], scalar2=None, op0=ALU.add),
                               reads=ps.b + badaf_t.b, writes=ss_t.b)
                    else:
                        half = grp - 4
                        for b in range(NB):
                            ps = PS[2 + b]
                            for kc in range(KC):
                                op("pe", lambda e, w=w, kc=kc, ps=ps, b=b: e.matmul(ps[:, :], lhsT=scb[:, b, kc, :], rhs=w[:, :, kc * 128:(kc + 1) * 128], start=(kc == 0), stop=(kc == KC - 1)),
                                   reads=w.b + scb.b, writes=ps.b)
                            sl = slice(half * 512, (half + 1) * 512)
                            op("dve", lambda e, ps=ps, b=b, sl=sl: e.tensor_tensor(out=GP[:, b, sl], in0=ps[:, :], in1=badag_bc[:, sl], op=ALU.add), reads=ps.b + badag_bc.b, writes=GP.b)
                            op("dve", lambda e, b=b, sl=sl: e.tensor_tensor(out=GP[:, b, sl], in0=GP[:, b, sl], in1=gpost_bc[:, sl], op=ALU.mult), reads=GP.b + gpost_bc.b, writes=GP.b)
                for b in range(NB):
                    op("dve", lambda e, b=b: e.tensor_copy(out=A_shift[:, :, b], in_=ss_t[:, 0:8, b]), reads=ss_t.b, writes=A_shift.b)
                    op("dve", lambda e, b=b: e.scalar_tensor_tensor(out=A_scale[:, :, b], in0=ss_t[:, 8:16, b], scalar=1.0, in1=gpre_t[:, :], op0=ALU.add, op1=ALU.mult),
                       reads=ss_t.b + gpre_t.b, writes=A_scale.b)
                sc.barrier()

            for b in range(NB):
                phaseA(l, b, xcur, xcur_b)
                if "stopA" in dbg:
                    continue
                phaseB(l, b)
                if "stopB" in dbg:
                    continue
                phaseC(l, b, xcur, xcur_b, xnext, xnext_b)
        sc.finish()
    return sc


def _chunked(W):
    N = W.shape[1]
    return np.ascontiguousarray(W.reshape(8, 128, N // 128, 128).transpose(2, 1, 0, 3)).reshape(N // 128, 128, 1024)


def prep_shared(inp, S, DEPTH):
    f = np.float32
    g = {k: np.asarray(v, dtype=f) for k, v in inp.items() if k not in ("x", "c")}
    out = {}
    win = []
    for l in range(DEPTH):
        W = g["w_in"][l]
        cols = []
        for j in range(8):
            for kind in range(4):
                cols.append(W[:, kind * 1024 + j * 128: kind * 1024 + (j + 1) * 128])
        cols.append(W[:, 4096:7680])
        pad = np.zeros((1024, 128), f)
        pad[:, :48] = W[:, 7680:7728]
        cols.append(pad)
        cols.append(W[:, 7728:])
        Wp = np.concatenate(cols, axis=1)
        assert Wp.shape[1] == NCH * 128
        win.append(_chunked(Wp))
    out["winT"] = np.stack(win)
    out["wadaT"] = np.stack([_chunked(g["w_ada"][l]) for l in range(DEPTH)])
    out["wbrT"] = np.stack([np.concatenate([_chunked(g["w_br"][l, i]) for i in range(3)], 0) for l in range(DEPTH)])
    out["woutT"] = np.stack([np.ascontiguousarray(g["w_out"][l].reshape(8, 128, 1024).transpose(1, 0, 2)).reshape(128, 8192) for l in range(DEPTH)])
    out["gpre"] = np.ascontiguousarray(g["g_pre"][:DEPTH].reshape(DEPTH, 8, 128).transpose(0, 2, 1))
    out["gpost"] = g["g_post"][:DEPTH].reshape(DEPTH, 1, D)
    out["badag"] = np.ascontiguousarray(g["b_ada"][:DEPTH, 2048:]).reshape(DEPTH, 1, D)
    out["badaf"] = np.ascontiguousarray(g["b_ada"][:DEPTH, :2048].reshape(DEPTH, 16, 128).transpose(0, 2, 1))
    out["cw"] = np.ascontiguousarray(g["conv_w"][:DEPTH].reshape(DEPTH, 3, 8, 128).transpose(0, 3, 2, 1))
    out["cb"] = np.ascontiguousarray(g["conv_b"][:DEPTH].reshape(DEPTH, 8, 128).transpose(0, 2, 1))
    out["posT"] = np.ascontiguousarray(np.stack([g["pos_ck"][:DEPTH], g["pos_cv"][:DEPTH]], 1).transpose(0, 1, 3, 2))
    w1 = np.stack([g["w_ck1"][:DEPTH], g["w_cv1"][:DEPTH]], 1)
    out["w1"] = np.ascontiguousarray(w1.reshape(DEPTH, 2, 32, 64, 128).transpose(0, 1, 3, 2, 4)).reshape(DEPTH, 2, 64, 4096)
    out["w2"] = np.ascontiguousarray(np.stack([g["w_ck2"][:DEPTH], g["w_cv2"][:DEPTH]], 1))
    out["lng"] = np.ascontiguousarray(g["ln_g"][:DEPTH].reshape(DEPTH, 8, 128).transpose(0, 2, 1))
    out["lnb"] = g["ln_b"][:DEPTH].reshape(DEPTH, 1, D)
    out["wsT"] = np.ascontiguousarray(g["w_s"][:DEPTH].transpose(0, 3, 1, 2))
    out["bs"] = g["b_s"][:DEPTH].reshape(DEPTH, 1, D)
    for k, v in host_consts(S).items():
        out["k_" + k] = v
    return out


def prep_core(x, c, b0, NB):
    xs = np.ascontiguousarray(np.asarray(x[b0:b0 + NB], dtype=np.float32))
    cs = np.asarray(c[b0:b0 + NB], dtype=np.float32)
    cT = np.ascontiguousarray(cs.reshape(NB, 8, 128).transpose(2, 1, 0))
    return {"x": xs, "cT": cT}


_CACHE = {}


def run(inputs, S, NB, DEPTH, ncores, dbg=None):
    key = (S, NB, DEPTH, tuple(sorted(dbg)) if dbg else None)
    nc = bass.Bass("TRN2", target_bir_lowering=False)
    build(nc, S, NB, DEPTH, dbg)
    shared = prep_shared(inputs, S, DEPTH)
    in_maps = []
    for core in range(ncores):
        m = dict(shared)
        m.update(prep_core(inputs["x"], inputs["c"], core * NB, NB))
        in_maps.append(m)
    res = run_bass_kernel_spmd(nc, in_maps, core_ids=list(range(ncores)))
    return res


def kernel(**inputs):
    S, NB, DEPTH, NCORES = 4096, 2, 2, 8
    res = run(inputs, S, NB, DEPTH, NCORES)
    return np.concatenate([np.asarray(r["y"]) for r in res.results], axis=0).astype(np.float32)
```
